# Optimizing a Trainium2 kernel written in Bass

```python
import jax
import jax.numpy as jnp
from jax import lax
import numpy as np

D_MODEL = 2048
BATCH = 4
SEQ = 2048
DEPTH = 4

N_MIXERS = 3
N_LAYERS_SG = (DEPTH + 2) // 3
N_LAYERS_RET = (DEPTH + 1) // 3
N_LAYERS_RWKV = DEPTH // 3
N_MOD = 6
NORM_EPS = 1e-6
LN_EPS = 1e-5

SG_CHUNK = 128
SG_WIDTH = D_MODEL
SG_GROUPS = 16
SG_GROUP_DIM = SG_WIDTH // SG_GROUPS

RET_HEADS = 8
RET_QK_DIM = D_MODEL // RET_HEADS
RET_V_DIM = 2 * D_MODEL // RET_HEADS
RET_CHUNK = 128
ROPE_BASE = 10000.0
POS_OFFSET_MAX = 1024

RWKV_HEAD_DIM = 64
RWKV_HEADS = D_MODEL // RWKV_HEAD_DIM
RWKV_LORA_DECAY = max(32, int(round(1.8 * D_MODEL ** 0.5 / 32)) * 32)
RWKV_LORA_AAA = max(32, int(round(1.8 * D_MODEL ** 0.5 / 32)) * 32)
RWKV_LORA_GATE = max(32, int(round(0.6 * D_MODEL ** 0.8 / 32)) * 32)
RWKV_GN_EPS = RWKV_HEAD_DIM * 1e-5
RWKV_DECAY_OFFSET = 0.5

D_FF = int(round(8 * D_MODEL / 3 / 128)) * 128
CONV_WIDTH = 3

kernel_name = 'hybrid_sgmlp_retnet_rwkv7_trunk'


def rms_norm(x):
    xf = x.astype(jnp.float32)
    return (xf * lax.rsqrt(jnp.mean(xf * xf, -1, keepdims=True) + NORM_EPS)).astype(x.dtype)


def group_norm(x, gain, bias, n_groups, eps):
    shp = x.shape
    xf = x.astype(jnp.float32).reshape(shp[:-1] + (n_groups, shp[-1] // n_groups))
    mu = jnp.mean(xf, -1, keepdims=True)
    var = jnp.mean(jnp.square(xf - mu), -1, keepdims=True)
    y = ((xf - mu) * lax.rsqrt(var + eps)).reshape(shp).astype(x.dtype)
    return y * gain + bias


def token_shift(x):
    return jnp.pad(x, ((0, 0), (1, 0), (0, 0)))[:, :-1]


def rotary(x, positions):
    half = x.shape[-1] // 2
    inv_freq = ROPE_BASE ** (-jnp.arange(half, dtype=jnp.float32) / half)
    ang = positions.astype(jnp.float32)[..., None] * inv_freq
    cos = jnp.cos(ang)[:, :, None, :]
    sin = jnp.sin(ang)[:, :, None, :]
    xf = x.astype(jnp.float32)
    x1, x2 = xf[..., :half], xf[..., half:]
    return jnp.concatenate([x1 * cos - x2 * sin, x2 * cos + x1 * sin], -1).astype(x.dtype)


def chunked_spatial_gating(x, w_in, ln_g, ln_b, w_s, b_s, w_out):
    b, s, _ = x.shape
    z = jax.nn.gelu(x @ w_in)
    u, v = jnp.split(z, 2, axis=-1)
    v = group_norm(v, ln_g, ln_b, 1, LN_EPS)
    v = v.reshape(b, s // SG_CHUNK, SG_CHUNK, SG_GROUPS, SG_GROUP_DIM)
    causal = jnp.tril(jnp.ones((SG_CHUNK, SG_CHUNK), dtype=bool))
    w_causal = jnp.where(causal, w_s, 0).astype(v.dtype)
    sv = jnp.einsum('gts,bnsgc->bntgc', w_causal, v) + b_s.T[None, None, :, :, None]
    return (u * sv.reshape(b, s, SG_WIDTH)) @ w_out


def retention(x, positions, w_in, gn_g, gn_b, w_out):
    b, s, d = x.shape
    nc = s // RET_CHUNK
    q, k, v, g = jnp.split(x @ w_in, [d, 2 * d, 4 * d], axis=-1)
    q = rotary(q.reshape(b, s, RET_HEADS, RET_QK_DIM), positions)
    k = rotary(k.reshape(b, s, RET_HEADS, RET_QK_DIM), positions) * (RET_QK_DIM ** -0.5)
    v = v.reshape(b, s, RET_HEADS, RET_V_DIM)

    log_gamma = jnp.log1p(-jnp.exp2(-5.0 - jnp.arange(RET_HEADS, dtype=jnp.float32)))
    idx = jnp.arange(RET_CHUNK, dtype=jnp.float32)
    rel = idx[:, None] - idx[None, :]
    decay_inner = jnp.where(rel >= 0, jnp.exp(log_gamma[:, None, None] * jnp.maximum(rel, 0.0)), 0.0)
    q_decay = jnp.exp(log_gamma[:, None] * (idx + 1.0))[..., None]
    k_decay = jnp.exp(log_gamma[:, None] * (RET_CHUNK - 1.0 - idx))[..., None]
    chunk_decay = jnp.exp(log_gamma * RET_CHUNK)[:, None, None]

    def chunks(t):
        return t.astype(jnp.float32).reshape(b, nc, RET_CHUNK, RET_HEADS, -1).transpose(1, 0, 3, 2, 4)

    def step(state, inp):
        qn, kn, vn = inp
        inner = jnp.einsum('bhtd,bhsd->bhts', qn, kn) * decay_inner
        out = (jnp.einsum('bhts,bhse->bhte', inner, vn)
               + jnp.einsum('bhtd,bhde->bhte', qn * q_decay, state))
        state = state * chunk_decay + jnp.einsum('bhsd,bhse->bhde', kn * k_decay, vn)
        return state, out

    state0 = jnp.zeros((b, RET_HEADS, RET_QK_DIM, RET_V_DIM), jnp.float32)
    _, o = lax.scan(step, state0, (chunks(q), chunks(k), chunks(v)))
    o = o.transpose(1, 0, 3, 2, 4).reshape(b, s, RET_HEADS * RET_V_DIM).astype(x.dtype)
    o = group_norm(o, gn_g, gn_b, RET_HEADS, NORM_EPS)
    return (jax.nn.silu(g) * o) @ w_out


def rwkv7_time_mix(x, mu, w_rkv, w0, w1, w2, a0, a1, a2, g1, g2, k_k, k_a, r_k, ln_g, ln_b, w_out):
    b, s, d = x.shape
    hshape = (b, s, RWKV_HEADS, RWKV_HEAD_DIM)
    xs = x[None] + (token_shift(x) - x)[None] * mu[:, None, None, :]
    rkv = jnp.einsum('pbsd,pde->pbse', xs[:3], w_rkv)
    r, k, v = rkv[0], rkv[1], rkv[2]
    w_log = -jax.nn.softplus(-(w0 + jnp.tanh(xs[3] @ w1) @ w2)) - RWKV_DECAY_OFFSET
    decay = jnp.exp(-jnp.exp(w_log.astype(jnp.float32)))
    a = jax.nn.sigmoid(a0 + (xs[4] @ a1) @ a2)
    g = jax.nn.sigmoid(xs[5] @ g1) @ g2
    kk = (k * k_k).astype(jnp.float32).reshape(hshape)
    kk = kk / jnp.maximum(jnp.sqrt(jnp.sum(kk * kk, -1, keepdims=True)), 1e-12)
    k = k * (1 + (a - 1) * k_a)

    def time_major(t):
        return t.astype(jnp.float32).reshape(hshape).transpose(1, 0, 2, 3)

    def step(state, inp):
        r_t, w_t, k_t, v_t, kk_t, a_t = inp
        sa = jnp.einsum('bhvk,bhk->bhv', state, -kk_t)
        state = (state * w_t[:, :, None, :]
                 + sa[..., None] * (kk_t * a_t)[:, :, None, :]
                 + v_t[..., None] * k_t[:, :, None, :])
        return state, jnp.einsum('bhvk,bhk->bhv', state, r_t)

    state0 = jnp.zeros((b, RWKV_HEADS, RWKV_HEAD_DIM, RWKV_HEAD_DIM), jnp.float32)
    seq_in = (time_major(r), time_major(decay), time_major(k), time_major(v),
              kk.transpose(1, 0, 2, 3), time_major(a))
    _, o = lax.scan(step, state0, seq_in)
    o = o.transpose(1, 0, 2, 3).reshape(b, s, d).astype(x.dtype)
    o = group_norm(o, ln_g, ln_b, RWKV_HEADS, RWKV_GN_EPS)
    bonus = jnp.sum(r.reshape(hshape) * k.reshape(hshape) * r_k, -1, keepdims=True) * v.reshape(hshape)
    return ((o + bonus.reshape(b, s, d)) * g) @ w_out


def conv_gated_ffn(x, w_up, conv_w, conv_b, w_down):
    s = x.shape[1]
    h = x @ w_up
    hp = jnp.pad(h, ((0, 0), (CONV_WIDTH - 1, 0), (0, 0)))
    h = sum(hp[:, j:j + s] * conv_w[j] for j in range(CONV_WIDTH)) + conv_b
    val, gate = jnp.split(h, 2, axis=-1)
    return (jax.nn.silu(gate) * val) @ w_down


def setup_inputs(seed: int = 0) -> dict:
    key = jax.random.key(seed)
    keys = iter(jax.random.split(key, 48))

    def normal(shape, std):
        return std * jax.random.normal(next(keys), shape, jnp.float32)

    def uniform(shape, lo, hi):
        return jax.random.uniform(next(keys), shape, jnp.float32, lo, hi)

    D, F2 = D_MODEL, 2 * D_FF
    nsg, nret, nrw = N_LAYERS_SG, N_LAYERS_RET, N_LAYERS_RWKV
    start = jax.random.randint(next(keys), (BATCH, 1), 0, POS_OFFSET_MAX, dtype=jnp.int32)
    return {
        'x': normal((BATCH, SEQ, D), 1.0),
        'c': normal((BATCH, D), 1.0),
        'positions': start + jnp.arange(SEQ, dtype=jnp.int32)[None, :],
        'ada_w': normal((DEPTH, D, N_MOD * D), 0.5 * D ** -0.5),
        'ada_b': normal((DEPTH, N_MOD * D), 0.02),
        'ffn_w_up': normal((DEPTH, D, F2), D ** -0.5),
        'ffn_conv_w': normal((DEPTH, CONV_WIDTH, F2), CONV_WIDTH ** -0.5),
        'ffn_conv_b': normal((DEPTH, F2), 0.02),
        'ffn_w_down': normal((DEPTH, D_FF, D), D_FF ** -0.5),
        'sg_w_in': normal((nsg, D, 2 * SG_WIDTH), D ** -0.5),
        'sg_ln_g': 1.0 + normal((nsg, SG_WIDTH), 0.02),
        'sg_ln_b': normal((nsg, SG_WIDTH), 0.02),
        'sg_w_s': normal((nsg, SG_GROUPS, SG_CHUNK, SG_CHUNK), SG_CHUNK ** -0.5),
        'sg_b_s': 1.0 + normal((nsg, SG_GROUPS, SG_CHUNK), 0.02),
        'sg_w_out': normal((nsg, SG_WIDTH, D), SG_WIDTH ** -0.5),
        'ret_w_in': normal((nret, D, 6 * D), D ** -0.5),
        'ret_gn_g': 1.0 + normal((nret, 2 * D), 0.02),
        'ret_gn_b': normal((nret, 2 * D), 0.02),
        'ret_w_out': normal((nret, 2 * D, D), (2 * D) ** -0.5),
        'rwkv_mu': uniform((nrw, 6, D), 0.0, 1.0),
        'rwkv_w_rkv': normal((nrw, 3, D, D), D ** -0.5),
        'rwkv_w0': uniform((nrw, D), -4.0, 1.0),
        'rwkv_w1': normal((nrw, D, RWKV_LORA_DECAY), D ** -0.5),
        'rwkv_w2': normal((nrw, RWKV_LORA_DECAY, D), 0.5 * RWKV_LORA_DECAY ** -0.5),
        'rwkv_a0': normal((nrw, D), 0.1),
        'rwkv_a1': normal((nrw, D, RWKV_LORA_AAA), D ** -0.5),
        'rwkv_a2': normal((nrw, RWKV_LORA_AAA, D), 0.5 * RWKV_LORA_AAA ** -0.5),
        'rwkv_g1': normal((nrw, D, RWKV_LORA_GATE), D ** -0.5),
        'rwkv_g2': normal((nrw, RWKV_LORA_GATE, D), RWKV_LORA_GATE ** -0.5),
        'rwkv_k_k': 0.85 + normal((nrw, D), 0.02),
        'rwkv_k_a': 1.0 + normal((nrw, D), 0.02),
        'rwkv_r_k': normal((nrw, RWKV_HEADS, RWKV_HEAD_DIM), 0.1),
        'rwkv_ln_g': 1.0 + normal((nrw, D), 0.02),
        'rwkv_ln_b': normal((nrw, D), 0.02),
        'rwkv_w_out': normal((nrw, D, D), D ** -0.5),
        'final_norm_g': 1.0 + normal((D,), 0.02),
    }


def reference(x, c, positions, ada_w, ada_b, ffn_w_up, ffn_conv_w, ffn_conv_b, ffn_w_down,
              sg_w_in, sg_ln_g, sg_ln_b, sg_w_s, sg_b_s, sg_w_out,
              ret_w_in, ret_gn_g, ret_gn_b, ret_w_out,
              rwkv_mu, rwkv_w_rkv, rwkv_w0, rwkv_w1, rwkv_w2, rwkv_a0, rwkv_a1, rwkv_a2,
              rwkv_g1, rwkv_g2, rwkv_k_k, rwkv_k_a, rwkv_r_k, rwkv_ln_g, rwkv_ln_b, rwkv_w_out,
              final_norm_g):
    cond = jax.nn.silu(c)
    h = x
    for layer in range(DEPTH):
        mod = cond @ ada_w[layer] + ada_b[layer]
        sh_t, sc_t, g_t, sh_c, sc_c, g_c = [m[:, None, :] for m in jnp.split(mod, N_MOD, axis=-1)]
        xm = rms_norm(h) * (1 + sc_t) + sh_t
        kind, j = layer % N_MIXERS, layer // N_MIXERS
        if kind == 0:
            y = chunked_spatial_gating(xm, sg_w_in[j], sg_ln_g[j], sg_ln_b[j], sg_w_s[j], sg_b_s[j], sg_w_out[j])
        elif kind == 1:
            y = retention(xm, positions, ret_w_in[j], ret_gn_g[j], ret_gn_b[j], ret_w_out[j])
        else:
            y = rwkv7_time_mix(xm, rwkv_mu[j], rwkv_w_rkv[j], rwkv_w0[j], rwkv_w1[j], rwkv_w2[j],
                               rwkv_a0[j], rwkv_a1[j], rwkv_a2[j], rwkv_g1[j], rwkv_g2[j],
                               rwkv_k_k[j], rwkv_k_a[j], rwkv_r_k[j], rwkv_ln_g[j], rwkv_ln_b[j], rwkv_w_out[j])
        h = h + g_t * y
        xm = rms_norm(h) * (1 + sc_c) + sh_c
        h = h + g_c * conv_gated_ffn(xm, ffn_w_up[layer], ffn_conv_w[layer], ffn_conv_b[layer], ffn_w_down[layer])
    return rms_norm(h) * final_norm_g
```

```python
import math
import numpy as np
import concourse.bass as bass
import concourse.mybir as mybir
from concourse.bass_utils import run_bass_kernel_spmd
from contextlib import ExitStack

F32 = mybir.dt.float32
BF16 = mybir.dt.bfloat16
I32 = mybir.dt.int32
AF = mybir.ActivationFunctionType
ALU = mybir.AluOpType
AX = mybir.AxisListType

COMPUTE = ("pe", "act", "dve", "pool")
SEM_EPOCH = 20000


class Op:
    __slots__ = ("eng", "fn", "is_dma", "signal", "waits", "sem", "val", "idx", "deps", "is_cc")

    def __init__(self, eng, fn, is_dma):
        self.eng = eng
        self.fn = fn
        self.is_dma = is_dma
        self.signal = False
        self.waits = []
        self.sem = None
        self.val = None
        self.deps = []
        self.is_cc = False


class Prog:
    def __init__(self, nc, n_dma_sems=8):
        self.nc = nc
        self.ops = []
        self.last_writer = {}
        self.readers = {}
        self.n_dma_sems = n_dma_sems
        self.dma_count = {"sp": 0, "act": 0, "pool": 0}
        self.dma_hist = {"sp": [], "act": [], "pool": []}
        self.out_dmas = []
        self.cc_hist = []

    def op(self, eng, fn, reads=(), writes=(), dma=False, is_output=False, cc=False):
        o = Op(eng, fn, dma)
        o.is_cc = cc
        deps = []
        for k in reads:
            lw = self.last_writer.get(k)
            if lw is not None:
                deps.append((lw, "raw"))
        for k in writes:
            lw = self.last_writer.get(k)
            if lw is not None:
                deps.append((lw, "waw"))
            for r in self.readers.get(k, ()):
                deps.append((r, "war"))
        for k in reads:
            self.readers.setdefault(k, []).append(o)
        for k in writes:
            self.last_writer[k] = o
            self.readers[k] = []
        seen = set()
        for d, kind in deps:
            if d is o or id(d) in seen:
                continue
            if (not d.is_dma) and (not o.is_dma) and d.eng == o.eng:
                if d.eng == "pe":
                    continue
                if kind != "raw":
                    if not any((dd is d and kk == "raw") for dd, kk in deps):
                        continue
            seen.add(id(d))
            o.deps.append(d)
            d.signal = True
        if cc:
            o.signal = True
            self.cc_hist.append(o)
        elif dma:
            h = self.dma_hist[eng]
            j = len(h)
            o.idx = j
            if j >= self.n_dma_sems:
                prev = h[j - self.n_dma_sems]
                if id(prev) not in seen:
                    o.deps.append(prev)
                    prev.signal = True
            h.append(o)
            o.signal = True
            if is_output:
                self.out_dmas.append(o)
        self.ops.append(o)
        return o

    def cc(self, fn, reads=(), writes=()):
        return self.op("pool", fn, reads=reads, writes=writes, dma=True, cc=True)

    def barrier(self):
        lasts = {}
        for o in self.ops:
            if not o.is_dma and o.fn is not None:
                lasts[o.eng] = o
        dm = []
        for q in ("sp", "act", "pool"):
            dm += self.dma_hist[q][-self.n_dma_sems:]
        dm += self.cc_hist[-4:]
        for e in ("pe", "act", "dve", "pool", "sp"):
            b = Op(e, None, False)
            for x, lo in lasts.items():
                if x != e:
                    b.deps.append(lo)
                    lo.signal = True
            for d in dm:
                b.deps.append(d)
            self.ops.append(b)
        self.last_writer = {}
        self.readers = {}
        self.out_dmas = []

    def emit(self):
        nc = self.nc
        with ExitStack() as es:
            n_sig = {e: sum(1 for o in self.ops if o.eng == e and not o.is_dma and o.signal) for e in COMPUTE}
            csems = {}
            for e in COMPUTE:
                ne = max(1, (n_sig[e] + SEM_EPOCH - 1) // SEM_EPOCH)
                csems[e] = [es.enter_context(nc.semaphore(f"c_{e}_{i}")) for i in range(ne)]
            dsems = {}
            for q in ("sp", "act", "pool"):
                if self.dma_hist[q]:
                    dsems[q] = [es.enter_context(nc.semaphore(f"d_{q}_{i}")) for i in range(self.n_dma_sems)]
            cnt = {e: 0 for e in COMPUTE}
            streams_cc = []
            for o in self.ops:
                if o.is_cc:
                    o.sem = es.enter_context(nc.semaphore(f"cc_{len(streams_cc)}"))
                    streams_cc.append(o)
                    o.val = 1
                elif o.is_dma:
                    o.sem = dsems[o.eng][o.idx % self.n_dma_sems]
                    o.val = 16 * (o.idx // self.n_dma_sems + 1)
                elif o.signal:
                    c = cnt[o.eng]
                    o.sem = csems[o.eng][c // SEM_EPOCH]
                    o.val = c % SEM_EPOCH + 1
                    cnt[o.eng] = c + 1
            streams = {e: [] for e in ("pe", "act", "dve", "pool", "sp")}
            for o in self.ops:
                streams[o.eng].append(o)
            block = es.enter_context(nc.Block())

            def run(eng_name, eng):
                waited = {}
                for o in streams[eng_name]:
                    for d in o.deps:
                        key = id(d.sem)
                        if waited.get(key, 0) >= d.val:
                            continue
                        eng.wait_ge(d.sem, d.val)
                        waited[key] = d.val
                    if o.fn is None:
                        continue
                    ins = o.fn(eng)
                    if o.signal:
                        ins.then_inc(o.sem, 1 if o.is_cc else (16 if o.is_dma else 1))
                for d in self.out_dmas:
                    if d.eng == eng_name:
                        eng.wait_ge(d.sem, d.val)

            @block.tensor
            def _(e):
                run("pe", e)

            @block.scalar
            def _(e):
                run("act", e)

            @block.vector
            def _(e):
                run("dve", e)

            @block.gpsimd
            def _(e):
                run("pool", e)

            @block.sync
            def _(e):
                run("sp", e)


CX = [None]


class Ctx:
    def __init__(self, nc, P, arena, nwords, pss_all):
        self.nc, self.P, self.arena, self.nwords, self.pss_all = nc, P, arena, nwords, pss_all
        self.base = 0
        self.off = 0
        self.io = {}
        self.uid = 0

    def reset(self):
        self.off = self.base

    def sb(self, name, shape, dt=F32):
        n = 1
        for d in shape[1:]:
            n *= d
        esz = mybir.dt.size(dt)
        words = (n * esz + 3) // 4
        words = (words + 15) // 16 * 16
        assert self.off + words <= self.nwords, f"SBUF arena overflow at {name}: {self.off}+{words} > {self.nwords}"
        v = self.arena[0:shape[0], self.off:self.off + words]
        self.off += words
        if dt != F32:
            v = v.bitcast(dt)
        v = v[:, 0:n]
        if len(shape) == 3:
            v = v.rearrange("p (a b) -> p a b", a=shape[1])
        elif len(shape) == 4:
            v = v.rearrange("p (a b c) -> p a b c", a=shape[1], b=shape[2])
        return v


def mk_nc():
    return CX[0].nc if CX[0] else bass.Bass("TRN2", target_bir_lowering=False)


def dram_in(nc, name, shape, dt):
    if CX[0]:
        return CX[0].io[name]
    return nc.dram_tensor(name, shape, dt, kind="ExternalInput").ap()


def dram_out(nc, name, shape, dt):
    if CX[0]:
        return CX[0].io[name]
    return nc.dram_tensor(name, shape, dt, kind="ExternalOutput").ap()


def dram_tmp(nc, name, shape, dt):
    if CX[0]:
        CX[0].uid += 1
        return nc.dram_tensor(f"{name}_{CX[0].uid}", shape, dt).ap()
    return nc.dram_tensor(name, shape, dt).ap()


def mk_sb(nc, es):
    if CX[0]:
        CX[0].reset()
        return CX[0].sb
    return lambda name, shape, dt=F32: es.enter_context(nc.sbuf_tensor(name, shape, dt))


def mk_pss(nc, es, n):
    if CX[0]:
        return CX[0].pss_all[0:n]
    return [es.enter_context(nc.psum_tensor(f"ps{i}", [128, 512], F32)) for i in range(n)]


def mk_pst(nc, es):
    if CX[0]:
        return [CX[0].pss_all[i][:, :].bitcast(BF16).rearrange("p (a b) -> p a b", a=8) for i in (6, 7)]
    return [es.enter_context(nc.psum_tensor(f"pst{i}", [128, 8, 128], BF16)) for i in range(2)]


def mk_prog(nc):
    return CX[0].P if CX[0] else Prog(nc)


def finish(P):
    if CX[0]:
        P.barrier()
    else:
        P.emit()

D = 2048
DFF = 5504
NP = 43
T = 1024
HALO = 2
EPS = 1e-6


def L(f, *a):
    return lambda e: f(e, *a)


def emit_norm_mod(P, nc, es, hT, XM, modsb, i_sh, i_sc1, ncols, col0_dram, stage, ones, pss, tmp_keys, flag=None, halo=0):
    rstd_t = tmp_keys["rstd"]
    sq = tmp_keys["sq"]
    xn = tmp_keys["xn"]
    nlast = len(pss) - 1
    b0 = 0
    blk = 0
    while b0 < ncols:
        nb = min(128, ncols - b0)
        if ncols - b0 - nb == 1:
            nb -= 1
        par = blk % 2
        st = stage[:, :, par * 128:par * 128 + nb]
        pi = 0 if par == 0 else nlast
        ps = pss[pi]
        rstd = rstd_t[:, par * 256:par * 256 + nb]
        skey, rkey, pkey = ("stage", par), ("rstd", par), ("ps", pi)
        P.op("sp", L(lambda e, b0, nb, st: e.dma_start(out=st, in_=hT[:, col0_dram + b0: col0_dram + b0 + nb].rearrange("(c p) t -> p c t", p=128)), b0, nb, st),
             writes=[skey], dma=True)
        for c in range(16):
            P.op("act", L(lambda e, c, nb, st: e.activation(out=sq[:, c % 2, 0:nb], in_=st[:, c, :], func=AF.Square), c, nb, st),
                 reads=[skey], writes=[("sq", c % 2)])
            P.op("pe", L(lambda e, c, nb, ps: e.matmul(ps[:, 0:nb], ones[:], sq[:, c % 2, 0:nb], start=(c == 0), stop=(c == 15)), c, nb, ps),
                 reads=[("sq", c % 2), "ones"], writes=[pkey])
        P.op("act", L(lambda e, nb, ps, rstd: e.activation(out=rstd, in_=ps[:, 0:nb], func=AF.Sqrt, scale=1.0 / D, bias=tmp_keys["eps"][:, 0:1]), nb, ps, rstd),
             reads=[pkey, "eps"], writes=[rkey])
        P.op("dve", L(lambda e, rstd: e.reciprocal(out=rstd, in_=rstd), rstd), reads=[rkey], writes=[rkey])
        for c in range(16):
            P.op("dve", L(lambda e, c, nb, st, rstd: e.tensor_tensor(out=xn[:, c % 2, 0:nb], in0=st[:, c, :], in1=rstd, op=ALU.mult), c, nb, st, rstd),
                 reads=[skey, rkey], writes=[("xn", c % 2)])
            P.op("act", L(lambda e, c, nb, b0: e.activation(out=XM[:, c, b0:b0 + nb], in_=xn[:, c % 2, 0:nb], func=AF.Identity,
                                                          scale=modsb[:, i_sc1, c:c + 1], bias=modsb[:, i_sh, c:c + 1]), c, nb, b0),
                 reads=[("xn", c % 2), "mod"], writes=[("xm", c)])
        b0 += nb
        blk += 1
    if flag is not None and halo > 0:
        for c in range(16):
            P.op("dve", L(lambda e, c: e.tensor_scalar(out=XM[:, c, 0:halo], in0=XM[:, c, 0:halo], scalar1=flag[:, 0:1], scalar2=None, op0=ALU.mult), c),
                 reads=[("xm", c), "flag"], writes=[("xm", c)])


def build_ffn():
    nc = mk_nc()
    hT = dram_in(nc, "hT", [D, T + HALO], F32)
    mod = dram_in(nc, "mod", [128, 4, 16], F32)
    flag = dram_in(nc, "flag", [128, 1], F32)
    w_up = dram_in(nc, "w_up", [D, 2 * DFF], F32)
    cw = dram_in(nc, "cw", [128, 86, 4], F32)
    w_dn = dram_in(nc, "w_dn", [DFF, D], F32)
    out = dram_out(nc, "out", [D, T], F32)
    with ExitStack() as es:
        sb = mk_sb(nc, es)
        XM = sb("XM", [128, 16, T + HALO], BF16)
        A = sb("A", [128, NP, T], BF16)
        W = [sb(f"W{i}", [128, 16 * 512], BF16) for i in range(2)]
        stage = sb("stage", [128, 16, 264], F32)
        modsb = sb("modsb", [128, 4, 16], F32)
        flagsb = sb("flagsb", [128, 1], F32)
        cwsb = sb("cwsb", [128, 86, 4], F32)
        ones = sb("ones", [128, 128], BF16)
        epsb = sb("epsb", [128, 1], F32)
        rstd = sb("rstd", [128, 512], F32)
        sq = sb("sq", [128, 2, 512], BF16)
        xn = sb("xn", [128, 2, 512], F32)
        hres = sb("hres", [128, 2, 512], F32)
        hout = sb("hout", [128, 2, 512], F32)
        pss = mk_pss(nc, es, 8)
        P = mk_prog(nc)
        P.op("sp", lambda e: e.dma_start(out=modsb[:], in_=mod), writes=["mod"], dma=True)
        P.op("sp", lambda e: e.dma_start(out=flagsb[:], in_=flag), writes=["flag"], dma=True)
        P.op("sp", lambda e: e.dma_start(out=cwsb[:], in_=cw), writes=["cw"], dma=True)
        P.op("dve", lambda e: e.memset(ones[:], 1.0), writes=["ones"])
        P.op("dve", lambda e: e.memset(epsb[:], EPS), writes=["eps"])
        P.op("dve", lambda e: e.tensor_scalar(out=modsb[:, 1, :], in0=modsb[:, 1, :], scalar1=1.0, scalar2=None, op0=ALU.add), reads=["mod"], writes=["mod"])
        tk = {"rstd": rstd, "sq": sq, "xn": xn, "eps": epsb}
        emit_norm_mod(P, nc, es, hT, XM, modsb, 0, 1, T + HALO, 0, stage, ones, pss, tk, flag=flagsb, halo=HALO)

        stg = stage[:].rearrange("p c t -> p (c t)")
        HUP = lambda vg, b: stg[:, vg * 1032:vg * 1032 + T + HALO]
        ACC = lambda vg, b: stg[:, 2064 + vg * 1024: 2064 + (vg + 1) * 1024]
        tiles = [(i, min(2, NP - i)) for i in range(0, NP, 2)]
        for ti, (p0, npair) in enumerate(tiles):
            wb = ti % 2
            Wt = W[wb][:, 0:16 * 512].rearrange("p (k n) -> p k n", k=16)
            ncol = npair * 128
            P.op("pool", L(lambda e, Wt, p0, ncol: e.dma_start(out=Wt[:, :, 0:ncol], in_=w_up[:, p0 * 128:p0 * 128 + ncol].rearrange("(k p) n -> p k n", p=128)), Wt, p0, ncol),
                 reads=[], writes=[("W", wb)], dma=True)
            P.op("pool", L(lambda e, Wt, p0, ncol: e.dma_start(out=Wt[:, :, 256:256 + ncol], in_=w_up[:, DFF + p0 * 128:DFF + p0 * 128 + ncol].rearrange("(k p) n -> p k n", p=128)), Wt, p0, ncol),
                 reads=[], writes=[("W", wb)], dma=True)
            for j in range(npair):
                pi = p0 + j
                hb = 0
                for vg in range(2):
                    col = vg * 256 + j * 128
                    chunk = pi + vg * NP
                    bank_h = pss[1 + vg]
                    banks = [pss[3 + vg * 2], pss[4 + vg * 2]]
                    for k in range(16):
                        P.op("pe", L(lambda e, Wt, k, col, bank_h: e.matmul(bank_h[:, 0:HALO], Wt[:, k, col:col + 128], XM[:, k, 0:HALO], start=(k == 0), stop=(k == 15)), Wt, k, col, bank_h),
                             reads=[("W", wb), ("xm", k)], writes=[("ps", 1 + vg)])
                    for hb2 in range(2):
                        for k in range(16):
                            P.op("pe", L(lambda e, Wt, k, col, bk, hb2: e.matmul(bk[:, :], Wt[:, k, col:col + 128], XM[:, k, HALO + hb2 * 512: HALO + (hb2 + 1) * 512], start=(k == 0), stop=(k == 15)), Wt, k, col, banks[hb2], hb2),
                                 reads=[("W", wb), ("xm", k)], writes=[("ps", 3 + vg * 2 + hb2)])
                    hup = HUP(vg, hb)
                    acc = ACC(vg, hb)
                    P.op("act", L(lambda e, hup, bank_h: e.copy(out=hup[:, 0:HALO], in_=bank_h[:, 0:HALO]), hup, bank_h),
                         reads=[("ps", 1 + vg)], writes=[("hup", vg, hb)])
                    for hb2 in range(2):
                        P.op("act", L(lambda e, hup, bk, hb2: e.copy(out=hup[:, HALO + hb2 * 512:HALO + (hb2 + 1) * 512], in_=bk[:, :]), hup, banks[hb2], hb2),
                             reads=[("ps", 3 + vg * 2 + hb2)], writes=[("hup", vg, hb)])
                    eng = "dve"
                    P.op(eng, L(lambda e, acc, hup, chunk: e.tensor_scalar(out=acc, in0=hup[:, 2:2 + T], scalar1=cwsb[:, chunk, 2:3], scalar2=cwsb[:, chunk, 3:4], op0=ALU.mult, op1=ALU.add), acc, hup, chunk),
                         reads=[("hup", vg, hb), "cw"], writes=[("acc", vg, hb)])
                    P.op(eng, L(lambda e, acc, hup, chunk: e.scalar_tensor_tensor(out=acc, in0=hup[:, 1:1 + T], scalar=cwsb[:, chunk, 1:2], in1=acc, op0=ALU.mult, op1=ALU.add), acc, hup, chunk),
                         reads=[("hup", vg, hb), "cw", ("acc", vg, hb)], writes=[("acc", vg, hb)])
                    P.op(eng, L(lambda e, acc, hup, chunk: e.scalar_tensor_tensor(out=acc, in0=hup[:, 0:T], scalar=cwsb[:, chunk, 0:1], in1=acc, op0=ALU.mult, op1=ALU.add), acc, hup, chunk),
                         reads=[("hup", vg, hb), "cw", ("acc", vg, hb)], writes=[("acc", vg, hb)])
                accv, accg = ACC(0, hb), ACC(1, hb)
                hupg = HUP(1, hb)
                P.op("act", L(lambda e, hupg, accg: e.activation(out=hupg[:, 0:T], in_=accg, func=AF.Silu), hupg, accg),
                     reads=[("acc", 1, hb)], writes=[("hup", 1, hb)])
                P.op("dve", L(lambda e, pi, hupg, accv: e.tensor_tensor(out=A[:, pi, :], in0=hupg[:, 0:T], in1=accv, op=ALU.mult), pi, hupg, accv),
                     reads=[("hup", 1, hb), ("acc", 0, hb)], writes=[("A", pi)])

        for di in range(16):
            wb = (len(tiles) + di) % 2
            Wt = W[wb][:, 0:NP * 128].rearrange("p (k n) -> p k n", k=NP)
            P.op("pool", L(lambda e, Wt, di: e.dma_start(out=Wt[:, :, :], in_=w_dn[:, di * 128:(di + 1) * 128].rearrange("(k p) n -> p k n", p=128)), Wt, di),
                 writes=[("W", wb)], dma=True)
            m = di
            for hb2 in range(2):
                q = (m * 2 + hb2) % 2
                bi = 1 + (m * 2 + hb2) % 6
                bank = pss[bi]
                P.op("sp", L(lambda e, m, hb2, q: e.dma_start(out=hres[:, q, :], in_=hT[m * 128:(m + 1) * 128, HALO + hb2 * 512:HALO + (hb2 + 1) * 512]), m, hb2, q),
                     writes=[("hres", q)], dma=True)
                for k in range(NP):
                    P.op("pe", L(lambda e, Wt, k, hb2, bank: e.matmul(bank[:, :], Wt[:, k, :], A[:, k, hb2 * 512:(hb2 + 1) * 512], start=(k == 0), stop=(k == NP - 1)), Wt, k, hb2, bank),
                         reads=[("W", wb), ("A", k)], writes=[("ps", bi)])
                P.op("dve", L(lambda e, m, q, bank: e.scalar_tensor_tensor(out=hout[:, q, :], in0=bank[:, :], scalar=modsb[:, 2, m:m + 1], in1=hres[:, q, :], op0=ALU.mult, op1=ALU.add), m, q, bank),
                     reads=[("ps", bi), ("hres", q), "mod"], writes=[("hout", q)])
                P.op("sp", L(lambda e, m, hb2, q: e.dma_start(out=out[m * 128:(m + 1) * 128, hb2 * 512:(hb2 + 1) * 512], in_=hout[:, q, :]), m, hb2, q),
                     reads=[("hout", q)], dma=True, is_output=True)
        finish(P)
    return nc

T = 1024
LN_EPS = 1e-5
GELU = AF.Gelu_apprx_tanh


def build_sg():
    nc = mk_nc()
    hT = dram_in(nc, "hT", [D, T], F32)
    mod = dram_in(nc, "mod", [128, 4, 16], F32)
    w_in = dram_in(nc, "w_in", [D, 2 * D], F32)
    lnp = dram_in(nc, "lnp", [128, 2, D], F32)
    wsT = dram_in(nc, "wsT", [128, 16, 128], F32)
    tri = dram_in(nc, "tri", [128, 128], F32)
    bs = dram_in(nc, "bs", [1, D], F32)
    w_out = dram_in(nc, "w_out", [D, D], F32)
    out = dram_out(nc, "out", [D, T], F32)
    with ExitStack() as es:
        sb = mk_sb(nc, es)
        XM = sb("XM", [128, 16, T], BF16)
        U = sb("U", [128, 16, T], BF16)
        V = sb("V", [128, 8, D], BF16)
        W = [sb(f"W{i}", [128, 16, 512], BF16) for i in range(2)]
        stage = sb("stage", [128, 16, 256], F32)
        modsb = sb("modsb", [128, 4, 16], F32)
        lnsb = sb("lnsb", [128, 2, D], F32)
        wsf = sb("wsf", [128, 16, 128], F32)
        wsb = sb("wsb", [128, 16, 128], BF16)
        trisb = sb("trisb", [128, 128], F32)
        bssb = sb("bssb", [1, D], F32)
        ones = sb("ones", [128, 128], F32)
        epsb = sb("epsb", [128, 2], F32)
        rstd = sb("rstd", [128, 512], F32)
        sq = sb("sq", [128, 2, 512], BF16)
        onesb = sb("onesb", [128, 128], BF16)
        xn = sb("xn", [128, 2, 512], F32)
        hres = sb("hres", [128, 2, 512], F32)
        hout = sb("hout", [128, 2, 512], F32)
        stats = sb("stats", [128, 8, 4, 6], F32)
        mv = sb("mv", [128, 8, 4], F32)
        pss = mk_pss(nc, es, 8)
        P = mk_prog(nc)
        P.op("sp", lambda e: e.dma_start(out=modsb[:], in_=mod), writes=["mod"], dma=True)
        P.op("sp", lambda e: e.dma_start(out=lnsb[:], in_=lnp), writes=["ln"], dma=True)
        P.op("sp", lambda e: e.dma_start(out=wsf[:], in_=wsT), writes=["wsf"], dma=True)
        P.op("sp", lambda e: e.dma_start(out=trisb[:], in_=tri), writes=["tri"], dma=True)
        P.op("sp", lambda e: e.dma_start(out=bssb[:], in_=bs), writes=["bs"], dma=True)
        P.op("dve", lambda e: e.memset(ones[:], 1.0), writes=["ones"])
        P.op("dve", lambda e: e.memset(onesb[:], 1.0), writes=["ones"])
        P.op("dve", lambda e: e.memset(epsb[:, 0:1], EPS), writes=["eps"])
        P.op("dve", lambda e: e.memset(epsb[:, 1:2], LN_EPS), writes=["eps"])
        P.op("dve", lambda e: e.tensor_scalar(out=modsb[:, 1, :], in0=modsb[:, 1, :], scalar1=1.0, scalar2=None, op0=ALU.add), reads=["mod"], writes=["mod"])
        for g in range(16):
            P.op("dve", L(lambda e, g: e.tensor_tensor(out=wsb[:, g, :], in0=wsf[:, g, :], in1=trisb[:], op=ALU.mult), g),
                 reads=["wsf", "tri"], writes=["wsb"])
        tk = {"rstd": rstd, "sq": sq, "xn": xn, "eps": epsb}
        emit_norm_mod(P, nc, es, hT, XM, modsb, 0, 1, T, 0, stage, onesb, pss, tk)

        wcount = [0]

        def load_w(src_ap):
            wb = wcount[0] % 2
            wcount[0] += 1
            P.op("pool", L(lambda e, wb, src_ap: e.dma_start(out=W[wb][:], in_=src_ap.rearrange("(k p) n -> p k n", p=128)), wb, src_ap),
                 writes=[("W", wb)], dma=True)
            return wb

        bankc = [0]

        def next_bank():
            b = 1 + bankc[0] % 7
            bankc[0] += 1
            return b

        for ti in range(4):
            wb = load_w(w_in[:, ti * 512:(ti + 1) * 512])
            for mm in range(4):
                m = ti * 4 + mm
                for hb in range(2):
                    bi = next_bank()
                    for k in range(16):
                        P.op("pe", L(lambda e, wb, k, mm, hb, bi: e.matmul(pss[bi][:, :], W[wb][:, k, mm * 128:(mm + 1) * 128], XM[:, k, hb * 512:(hb + 1) * 512], start=(k == 0), stop=(k == 15)), wb, k, mm, hb, bi),
                             reads=[("W", wb), ("xm", k)], writes=[("ps", bi)])
                    P.op("act", L(lambda e, m, hb, bi: e.activation(out=U[:, m, hb * 512:(hb + 1) * 512], in_=pss[bi][:, :], func=GELU), m, hb, bi),
                         reads=[("ps", bi)], writes=[("u", m)])
        for fb in range(4):
            wb = load_w(w_in[:, D + fb * 512:D + (fb + 1) * 512])
            for n in range(8):
                bi = next_bank()
                for k in range(16):
                    P.op("pe", L(lambda e, wb, k, n, bi: e.matmul(pss[bi][:, :], XM[:, k, n * 128:(n + 1) * 128], W[wb][:, k, :], start=(k == 0), stop=(k == 15)), wb, k, n, bi),
                         reads=[("W", wb), ("xm", k)], writes=[("ps", bi)])
                P.op("act", L(lambda e, n, fb, bi: e.activation(out=V[:, n, fb * 512:(fb + 1) * 512], in_=pss[bi][:, :], func=GELU), n, fb, bi),
                     reads=[("ps", bi)], writes=[("v", n)])
                P.op("dve", L(lambda e, n, fb: e.bn_stats(out=stats[:, n, fb, :], in_=V[:, n, fb * 512:(fb + 1) * 512]), n, fb),
                     reads=[("v", n)], writes=[("stats", n)])
        stg = stage[:].rearrange("p c t -> p (c t)")
        for n in range(8):
            tmp = stg[:, (n % 2) * D:(n % 2 + 1) * D]
            P.op("dve", L(lambda e, n: e.bn_aggr(out=mv[:, n, 0:2], in_=stats[:, n, :, :].rearrange("p a b -> p (a b)")), n),
                 reads=[("stats", n)], writes=[("mv", n)])
            P.op("act", L(lambda e, n: e.activation(out=mv[:, n, 2:3], in_=mv[:, n, 1:2], func=AF.Sqrt, bias=epsb[:, 1:2]), n),
                 reads=[("mv", n), "eps"], writes=[("mv2", n)])
            P.op("dve", L(lambda e, n: e.reciprocal(out=mv[:, n, 3:4], in_=mv[:, n, 2:3]), n), reads=[("mv2", n)], writes=[("mv3", n)])
            P.op("dve", L(lambda e, n, tmp: e.tensor_scalar(out=tmp, in0=V[:, n, :], scalar1=mv[:, n, 0:1], scalar2=mv[:, n, 3:4], op0=ALU.subtract, op1=ALU.mult), n, tmp),
                 reads=[("v", n), ("mv", n), ("mv3", n), "stage"], writes=[("vtmp", n % 2)])
            P.op("dve", L(lambda e, n, tmp: e.tensor_tensor(out=tmp, in0=tmp, in1=lnsb[:, 0, :], op=ALU.mult), n, tmp),
                 reads=[("vtmp", n % 2), "ln"], writes=[("vtmp", n % 2)])
            P.op("dve", L(lambda e, n, tmp: e.tensor_tensor(out=V[:, n, :], in0=tmp, in1=lnsb[:, 1, :], op=ALU.add), n, tmp),
                 reads=[("vtmp", n % 2), "ln"], writes=[("v", n)])
        for n in range(8):
            for gq in range(4):
                bi = next_bank()
                for j in range(4):
                    g = gq * 4 + j
                    P.op("pe", L(lambda e, n, g, j, bi: e.matmul(pss[bi][:, j * 128:(j + 1) * 128], V[:, n, g * 128:(g + 1) * 128], wsb[:, g, :], start=True, stop=False), n, g, j, bi),
                         reads=[("v", n), "wsb"], writes=[("ps", bi)])
                    P.op("pe", L(lambda e, g, j, bi: e.matmul(pss[bi][:, j * 128:(j + 1) * 128], ones[0:1, :], bssb[0:1, g * 128:(g + 1) * 128], start=False, stop=True), g, j, bi),
                         reads=["ones", "bs"], writes=[("ps", bi)])
                P.op("dve", L(lambda e, n, gq, bi: e.tensor_tensor(out=U[:, gq * 4:(gq + 1) * 4, n * 128:(n + 1) * 128], in0=pss[bi][:, :].rearrange("p (j t) -> p j t", j=4),
                                                                  in1=U[:, gq * 4:(gq + 1) * 4, n * 128:(n + 1) * 128], op=ALU.mult), n, gq, bi),
                     reads=[("ps", bi)] + [("u", gq * 4 + j) for j in range(4)], writes=[("u", gq * 4 + j) for j in range(4)])
        for ti in range(4):
            wb = load_w(w_out[:, ti * 512:(ti + 1) * 512])
            for mm in range(4):
                m = ti * 4 + mm
                for hb in range(2):
                    q = (m * 2 + hb) % 2
                    bi = next_bank()
                    P.op("sp", L(lambda e, m, hb, q: e.dma_start(out=hres[:, q, :], in_=hT[m * 128:(m + 1) * 128, hb * 512:(hb + 1) * 512]), m, hb, q),
                         writes=[("hres", q)], dma=True)
                    for k in range(16):
                        P.op("pe", L(lambda e, wb, k, mm, hb, bi: e.matmul(pss[bi][:, :], W[wb][:, k, mm * 128:(mm + 1) * 128], U[:, k, hb * 512:(hb + 1) * 512], start=(k == 0), stop=(k == 15)), wb, k, mm, hb, bi),
                             reads=[("W", wb), ("u", k)], writes=[("ps", bi)])
                    P.op("dve", L(lambda e, m, q, bi: e.scalar_tensor_tensor(out=hout[:, q, :], in0=pss[bi][:, :], scalar=modsb[:, 2, m:m + 1], in1=hres[:, q, :], op0=ALU.mult, op1=ALU.add), m, q, bi),
                         reads=[("ps", bi), ("hres", q), "mod"], writes=[("hout", q)])
                    P.op("sp", L(lambda e, m, hb, q: e.dma_start(out=out[m * 128:(m + 1) * 128, hb * 512:(hb + 1) * 512], in_=hout[:, q, :]), m, hb, q),
                         reads=[("hout", q)], dma=True, is_output=True)
        finish(P)
    return nc

TWO_PI = 2 * math.pi
C1 = 6.28125
C2 = TWO_PI - C1
PI_SAFE = 3.1415925


def emit_rope_tables(P, posi, invf, sin_t, cos_t, tmps, n, key="rope"):
    pf, ang, r, t = tmps[0], tmps[1], tmps[2], tmps[3]
    ki = tmps[4]
    K = lambda s: (key, s)
    P.op("dve", lambda e: e.tensor_copy(out=pf[:, 0:n], in_=posi[:, 0:n]), reads=[K("posi")], writes=[K("pf")])
    P.op("dve", lambda e: e.tensor_scalar(out=ang[:, 0:n], in0=pf[:, 0:n], scalar1=invf[:, 0:1], scalar2=None, op0=ALU.mult), reads=[K("pf"), K("invf")], writes=[K("ang")])
    P.op("dve", lambda e: e.tensor_scalar(out=t[:, 0:n], in0=ang[:, 0:n], scalar1=1.0 / TWO_PI, scalar2=None, op0=ALU.mult), reads=[K("ang")], writes=[K("t")])
    P.op("dve", lambda e: e.tensor_copy(out=ki[:, 0:n], in_=t[:, 0:n]), reads=[K("t")], writes=[K("ki")])
    P.op("dve", lambda e: e.tensor_copy(out=pf[:, 0:n], in_=ki[:, 0:n]), reads=[K("ki"), K("ang")], writes=[K("pf")])
    P.op("dve", lambda e: e.scalar_tensor_tensor(out=r[:, 0:n], in0=pf[:, 0:n], scalar=-C1, in1=ang[:, 0:n], op0=ALU.mult, op1=ALU.add), reads=[K("pf"), K("ang")], writes=[K("r")])
    P.op("dve", lambda e: e.scalar_tensor_tensor(out=r[:, 0:n], in0=pf[:, 0:n], scalar=-C2, in1=r[:, 0:n], op0=ALU.mult, op1=ALU.add), reads=[K("pf"), K("r")], writes=[K("r")])

    def wrap(x, kx):
        P.op("dve", lambda e: e.tensor_scalar(out=t[:, 0:n], in0=x[:, 0:n], scalar1=math.pi, scalar2=-TWO_PI, op0=ALU.is_gt, op1=ALU.mult), reads=[kx], writes=[K("t")])
        P.op("dve", lambda e: e.tensor_tensor(out=x[:, 0:n], in0=x[:, 0:n], in1=t[:, 0:n], op=ALU.add), reads=[kx, K("t")], writes=[kx])
        P.op("dve", lambda e: e.tensor_scalar(out=t[:, 0:n], in0=x[:, 0:n], scalar1=-math.pi, scalar2=TWO_PI, op0=ALU.is_lt, op1=ALU.mult), reads=[kx], writes=[K("t")])
        P.op("dve", lambda e: e.tensor_tensor(out=x[:, 0:n], in0=x[:, 0:n], in1=t[:, 0:n], op=ALU.add), reads=[kx, K("t")], writes=[kx])
        P.op("dve", lambda e: e.tensor_scalar(out=x[:, 0:n], in0=x[:, 0:n], scalar1=PI_SAFE, scalar2=-PI_SAFE, op0=ALU.min, op1=ALU.max), reads=[kx], writes=[kx])

    wrap(r, K("r"))
    P.op("act", lambda e: e.activation(out=sin_t[:, 0:n], in_=r[:, 0:n], func=AF.Sin), reads=[K("r")], writes=[K("sin")])
    P.op("dve", lambda e: e.tensor_scalar(out=ang[:, 0:n], in0=r[:, 0:n], scalar1=math.pi / 2, scalar2=None, op0=ALU.add), reads=[K("r"), K("ang")], writes=[K("ang")])
    wrap(ang, K("ang"))
    P.op("act", lambda e: e.activation(out=cos_t[:, 0:n], in_=ang[:, 0:n], func=AF.Sin), reads=[K("ang")], writes=[K("cos")])

T = 1024
NH = 8
GN_EPS = 1e-6


def ret_consts():
    lg = np.log1p(-np.exp2(-5.0 - np.arange(NH, dtype=np.float64)))
    idx = np.arange(128, dtype=np.float64)
    rel = idx[None, :] - idx[:, None]
    decT = np.where(rel >= 0, np.exp(lg[:, None, None] * np.maximum(rel, 0)), 0.0)
    decT = np.ascontiguousarray(decT.transpose(1, 0, 2)).astype(np.float32)
    qdec = np.exp(lg[:, None] * (idx + 1.0))
    qdec = np.ascontiguousarray(np.broadcast_to(qdec[None], (128, NH, 128))).astype(np.float32)
    kdec = np.exp(lg[None, :] * (127.0 - idx[:, None])).astype(np.float32)
    n = np.arange(8, dtype=np.float64)
    kdecA = np.exp(lg[None, :, None] * (1023.0 - (n[None, None, :] * 128 + idx[:, None, None]))).astype(np.float32)
    cd = [float(np.exp(lg[h] * 128.0)) for h in range(NH)]
    ident = np.eye(128, dtype=np.float32)
    invf = (10000.0 ** (-np.arange(128, dtype=np.float64) / 128)).astype(np.float32)[:, None]
    return dict(decT=decT, qdec=qdec, kdec=kdec, kdecA=kdecA, ident=ident, invf=invf), cd


class _Stop(Exception):
    pass


STOP = [0]


def chk(n):
    if STOP[0] == n:
        raise _Stop()


def build_ret(mode):
    _, cd = ret_consts()
    nc = mk_nc()
    dt_in = lambda name, shape, dt=F32: dram_in(nc, name, shape, dt)
    hT = dt_in("hT", [D, T])
    mod = dt_in("mod", [128, 4, 16])
    posb = dt_in("posb", [128, T], I32)
    invf = dt_in("invf", [128, 1])
    w_in = dt_in("w_in", [D, 6 * D])
    kdecA = dt_in("kdecA", [128, NH, 8])
    kdec = dt_in("kdec", [128, NH])
    ident = dt_in("ident", [128, 128])
    if mode == "main":
        decT = dt_in("decT", [128, NH, 128])
        qdec = dt_in("qdec", [128, NH, 128])
        gnp = dt_in("gnp", [128, 32, 2])
        s_in = dt_in("s_in", [NH, 256, 512])
        w_out = dt_in("w_out", [2 * D, D])
        out = dram_out(nc, "out", [D, T], F32)
        zs = dram_tmp(nc, "zs", [32 * 128, T], BF16)
    else:
        s_out = dram_out(nc, "s_out", [NH, 256, 512], F32)
    with ExitStack() as es:
        sb = mk_sb(nc, es)
        XX = sb("XX", [128, 32, T], BF16)
        XM = XX[:, 0:16, :]
        W = [sb(f"W{i}", [128, 16, 512], BF16) for i in range(2)]
        scratch = sb("scratch", [128, 16, 320], F32)
        scr = scratch[:].rearrange("p c t -> p (c t)")
        modsb = sb("modsb", [128, 4, 16], F32)
        ones = sb("ones", [128, 128], BF16)
        epsb = sb("epsb", [128, 2], F32)
        rstd = sb("rstd", [128, 512], F32)
        sq = sb("sq", [128, 2, 512], BF16)
        xn = sb("xn", [128, 2, 512], F32)
        posi = sb("posi", [128, T], I32)
        invsb = sb("invsb", [128, 1], F32)
        sin_t = sb("sin_t", [128, T], F32)
        cos_t = sb("cos_t", [128, T], F32)
        identf = sb("identf", [128, 128], F32)
        identb = sb("identb", [128, 128], BF16)
        kdecsb = sb("kdecsb", [128, NH], F32)
        kdecAsb = sb("kdecAsb", [128, NH, 8], F32)
        KT = sb("KT", [128, 2, T], BF16)
        KTOK = sb("KTOK", [128, 8, 256], BF16)
        VTOK = sb("VTOK", [128, 8, 512], BF16)
        pss = mk_pss(nc, es, 6)
        pstb = mk_pst(nc, es)
        if mode == "main":
            QT = sb("QT", [128, 2, T], BF16)
            QD = sb("QD", [128, 2, T], BF16)
            GT = sb("GT", [128, 4, T], BF16)
            ST = sb("ST", [128, 2, 512], F32)
            STB = sb("STB", [128, 2, 512], BF16)
            ON = sb("ON", [128, 2, 512], BF16)
            IT = sb("IT", [128, 2, 128], BF16)
            tmpz = sb("tmpz", [128, 4, 128], F32)
            decTsb = sb("decTsb", [128, NH, 128], F32)
            qdecsb = sb("qdecsb", [128, NH, 128], F32)
            gnsb = sb("gnsb", [128, 32, 2], F32)
            hres = sb("hres", [128, 2, 512], F32)
            hout = sb("hout", [128, 2, 512], F32)
            bst = sb("bst", [128, 2, 6], F32)
            gmv = sb("gmv", [128, 2, 4], F32)
        else:
            SO = sb("SO", [128, 2, 512], F32)
        P = mk_prog(nc)
        try:
            ld = lambda dst, src, key: P.op("sp", lambda e: e.dma_start(out=dst, in_=src), writes=[key], dma=True)
            ld(modsb[:], mod, "mod")
            ld(posi[:], posb, ("rope", "posi"))
            ld(invsb[:], invf, ("rope", "invf"))
            ld(identf[:], ident, "identf")
            ld(kdecsb[:], kdec, "kdec")
            ld(kdecAsb[:], kdecA, "kdecA")
            if mode == "main":
                ld(decTsb[:], decT, "decT")
                ld(qdecsb[:], qdec, "qdec")
                ld(gnsb[:], gnp, "gn")
            P.op("dve", lambda e: e.memset(ones[:], 1.0), writes=["ones"])
            P.op("dve", lambda e: e.memset(epsb[:, 0:1], EPS), writes=["eps"])
            P.op("dve", lambda e: e.memset(epsb[:, 1:2], GN_EPS), writes=["eps"])
            P.op("dve", lambda e: e.tensor_copy(out=identb[:], in_=identf[:]), reads=["identf"], writes=["ident"])
            P.op("dve", lambda e: e.tensor_scalar(out=modsb[:, 1, :], in0=modsb[:, 1, :], scalar1=1.0, scalar2=None, op0=ALU.add), reads=["mod"], writes=["mod"])
            tm = [scr[:, i * 1024:(i + 1) * 1024] for i in range(4)]
            ki = scr[:, 4096:5120].bitcast(I32)
            emit_rope_tables(P, posi, invsb, sin_t, cos_t, tm + [ki], T)
            stage = scratch[:, :, 0:256]
            tk = {"rstd": rstd, "sq": sq, "xn": xn, "eps": epsb}
            P.barrier()
            emit_norm_mod(P, nc, es, hT, XM, modsb, 0, 1, T, 0, stage, ones, pss, tk)
            P.barrier()
            chk(1)
            RT = lambda i: scr[:, i * 512:(i + 1) * 512]
            Zh = scr[:, 3072:3072 + 2048].bitcast(BF16).rearrange("p (j t) -> p j t", j=4)

            wcount = [0]

            def load_w(parts):
                wb = wcount[0] % 2
                wcount[0] += 1
                for off, src in parts:
                    ncol = src.shape[1]
                    P.op("pool", L(lambda e, wb, off, src, ncol: e.dma_start(out=W[wb][:, :, off:off + ncol], in_=src.rearrange("(k p) n -> p k n", p=128)), wb, off, src, ncol),
                         writes=[("W", wb)], dma=True)
                return wb

            bankc = [0]

            def nb():
                b = 1 + bankc[0] % (3 if mode == "main" else 5)
                bankc[0] += 1
                return b

            def proj_rot(wb, col0, dst, dkey, sc):
                for hb in range(2):
                    banks = [nb(), nb()]
                    for c in range(2):
                        for k in range(16):
                            P.op("pe", L(lambda e, wb, k, c, hb, bi: e.matmul(pss[bi][:, :], W[wb][:, k, col0 + c * 128:col0 + (c + 1) * 128], XM[:, k, hb * 512:(hb + 1) * 512], start=(k == 0), stop=(k == 15)), wb, k, c, hb, banks[c]),
                                 reads=[("W", wb), ("xm", k)], writes=[("ps", banks[c])])
                    x1, x2, t1, t2 = RT(0), RT(1), RT(2), RT(3)
                    P.op("act", L(lambda e, bi: e.activation(out=x1, in_=pss[bi][:, :], func=AF.Copy, scale=sc), banks[0]), reads=[("ps", banks[0])], writes=["x1"])
                    P.op("act", L(lambda e, bi: e.activation(out=x2, in_=pss[bi][:, :], func=AF.Copy, scale=sc), banks[1]), reads=[("ps", banks[1])], writes=["x2"])
                    cs = cos_t[:, hb * 512:(hb + 1) * 512]
                    sn = sin_t[:, hb * 512:(hb + 1) * 512]
                    ck, sk = ("rope", "cos"), ("rope", "sin")
                    P.op("dve", L(lambda e, cs: e.tensor_tensor(out=t1, in0=x1, in1=cs, op=ALU.mult), cs), reads=["x1", ck], writes=["t1"])
                    P.op("dve", L(lambda e, sn: e.tensor_tensor(out=t2, in0=x2, in1=sn, op=ALU.mult), sn), reads=["x2", sk], writes=["t2"])
                    P.op("dve", L(lambda e, hb: e.tensor_tensor(out=dst[:, 0, hb * 512:(hb + 1) * 512], in0=t1, in1=t2, op=ALU.subtract), hb), reads=["t1", "t2"], writes=[dkey])
                    P.op("dve", L(lambda e, cs: e.tensor_tensor(out=t1, in0=x2, in1=cs, op=ALU.mult), cs), reads=["x2", ck], writes=["t1"])
                    P.op("dve", L(lambda e, sn: e.tensor_tensor(out=t2, in0=x1, in1=sn, op=ALU.mult), sn), reads=["x1", sk], writes=["t2"])
                    P.op("dve", L(lambda e, hb: e.tensor_tensor(out=dst[:, 1, hb * 512:(hb + 1) * 512], in0=t1, in1=t2, op=ALU.add), hb), reads=["t1", "t2"], writes=[dkey])

            for hd in range(NH):
                parts = [(256, w_in[:, D + hd * 256:D + (hd + 1) * 256])]
                if mode == "main":
                    parts = [(0, w_in[:, hd * 256:(hd + 1) * 256])] + parts
                wb = load_w(parts)
                if mode == "main":
                    proj_rot(wb, 0, QT, "qt", 1.0)
                proj_rot(wb, 256, KT, "kt", 1.0 / 16.0)
                chk(2)
                wb = load_w([(0, w_in[:, 2 * D + hd * 512:2 * D + (hd + 1) * 512])])
                for n in range(8):
                    bi = nb()
                    for k in range(16):
                        P.op("pe", L(lambda e, wb, k, n, bi: e.matmul(pss[bi][:, :], XM[:, k, n * 128:(n + 1) * 128], W[wb][:, k, :], start=(k == 0), stop=(k == 15)), wb, k, n, bi),
                             reads=[("W", wb), ("xm", k)], writes=[("ps", bi)])
                    P.op("act", L(lambda e, n, bi: e.copy(out=VTOK[:, n, :], in_=pss[bi][:, :]), n, bi), reads=[("ps", bi)], writes=[("vtok", n)])
                chk(3)
                for n in range(8):
                    pb = n % 2
                    for c in range(2):
                        P.op("pe", L(lambda e, n, c, pb: e.transpose(pstb[pb][:, c, :], KT[:, c, n * 128:(n + 1) * 128], identb[:]), n, c, pb),
                             reads=["kt", "ident"], writes=[("pst", pb)])
                    ksc = kdecAsb[:, hd, n:n + 1] if mode == "state" else kdecsb[:, hd:hd + 1]
                    P.op("act", L(lambda e, n, pb, ksc: e.activation(out=KTOK[:, n, :], in_=pstb[pb][:, 0:2, :].rearrange("p a b -> p (a b)"), func=AF.Copy, scale=ksc), n, pb, ksc),
                         reads=[("pst", pb), "kdec", "kdecA"], writes=[("ktok", n)])
                chk(4)
                if mode == "state":
                    for c in range(2):
                        bi = nb()
                        for n in range(8):
                            P.op("pe", L(lambda e, n, c, bi: e.matmul(pss[bi][:, :], KTOK[:, n, c * 128:(c + 1) * 128], VTOK[:, n, :], start=(n == 0), stop=(n == 7)), n, c, bi),
                                 reads=[("ktok", n), ("vtok", n)], writes=[("ps", bi)])
                        P.op("act", L(lambda e, c, bi: e.copy(out=SO[:, c, :], in_=pss[bi][:, :]), c, bi), reads=[("ps", bi)], writes=[("so", c)])
                        P.op("sp", L(lambda e, c, hd: e.dma_start(out=s_out[hd, c * 128:(c + 1) * 128, :], in_=SO[:, c, :]), c, hd), reads=[("so", c)], dma=True, is_output=True)
                    continue
                wb = load_w([(0, w_in[:, 4 * D + hd * 512:4 * D + (hd + 1) * 512])])
                for j in range(4):
                    for hb in range(2):
                        bi = nb()
                        for k in range(16):
                            P.op("pe", L(lambda e, wb, k, j, hb, bi: e.matmul(pss[bi][:, :], W[wb][:, k, j * 128:(j + 1) * 128], XM[:, k, hb * 512:(hb + 1) * 512], start=(k == 0), stop=(k == 15)), wb, k, j, hb, bi),
                                 reads=[("W", wb), ("xm", k)], writes=[("ps", bi)])
                        P.op("act", L(lambda e, j, hb, bi: e.activation(out=GT[:, j, hb * 512:(hb + 1) * 512], in_=pss[bi][:, :], func=AF.Silu), j, hb, bi),
                             reads=[("ps", bi)], writes=["gt"])
                for c in range(2):
                    for n in range(8):
                        P.op("dve", L(lambda e, c, n, hd: e.tensor_tensor(out=QD[:, c, n * 128:(n + 1) * 128], in0=QT[:, c, n * 128:(n + 1) * 128], in1=qdecsb[:, hd, :], op=ALU.mult), c, n, hd),
                             reads=["qt", "qdec"], writes=["qd"])
                P.op("sp", L(lambda e, hd: e.dma_start(out=ST[:], in_=s_in[hd].rearrange("(c p) e -> p c e", p=128)), hd), writes=["st"], dma=True)
                P.op("act", lambda e: e.copy(out=STB[:], in_=ST[:]), reads=["st"], writes=["stb"])
                bo_of = {}

                def stageA(n):
                    tsl = slice(n * 128, (n + 1) * 128)
                    bi = nb()
                    for c in range(2):
                        P.op("pe", L(lambda e, c, bi, tsl: e.matmul(pss[bi][:, 0:128], KT[:, c, tsl], QT[:, c, tsl], start=(c == 0), stop=(c == 1)), c, bi, tsl),
                             reads=["kt", "qt"], writes=[("ps", bi)])
                    iq = n % 2
                    P.op("dve", L(lambda e, bi, iq, hd: e.tensor_tensor(out=IT[:, iq, :], in0=pss[bi][:, 0:128], in1=decTsb[:, hd, :], op=ALU.mult), bi, iq, hd),
                         reads=[("ps", bi), "decT"], writes=[("it", iq)])
                    bo = 4 + n % 2
                    P.op("pe", L(lambda e, bo, iq, n: e.matmul(pss[bo][:, :], IT[:, iq, :], VTOK[:, n, :], start=True, stop=False), bo, iq, n),
                         reads=[("it", iq), ("vtok", n)], writes=[("ps", bo)])
                    for c in range(2):
                        P.op("pe", L(lambda e, bo, c, tsl: e.matmul(pss[bo][:, :], QD[:, c, tsl], STB[:, c, :], start=False, stop=(c == 1)), bo, c, tsl),
                             reads=["qd", "stb"], writes=[("ps", bo)])
                    if n < 7:
                        for c in range(2):
                            bs_ = nb()
                            P.op("pe", L(lambda e, bs_, c, n: e.matmul(pss[bs_][:, :], KTOK[:, n, c * 128:(c + 1) * 128], VTOK[:, n, :], start=True, stop=True), bs_, c, n),
                                 reads=[("ktok", n), ("vtok", n)], writes=[("ps", bs_)])
                            P.op("dve", L(lambda e, bs_, c, hd: e.scalar_tensor_tensor(out=ST[:, c, :], in0=ST[:, c, :], scalar=cd[hd], in1=pss[bs_][:, :], op0=ALU.mult, op1=ALU.add), bs_, c, hd),
                                 reads=["st", ("ps", bs_)], writes=["st"])
                        P.op("act", lambda e: e.copy(out=STB[:], in_=ST[:]), reads=["st"], writes=["stb"])
                    bo_of[n] = (bo, iq)

                def stageB(n):
                    tsl = slice(n * 128, (n + 1) * 128)
                    bo, iq = bo_of[n]
                    P.op("dve", L(lambda e, bo, iq: e.bn_stats(out=bst[:, iq, :], in_=pss[bo][:, :]), bo, iq), reads=[("ps", bo)], writes=[("bst", iq)])
                    P.op("dve", L(lambda e, iq: e.bn_aggr(out=gmv[:, iq, 0:2], in_=bst[:, iq, :]), iq), reads=[("bst", iq)], writes=[("gmv", iq)])
                    P.op("act", L(lambda e, iq: e.activation(out=gmv[:, iq, 2:3], in_=gmv[:, iq, 1:2], func=AF.Sqrt, bias=epsb[:, 1:2]), iq), reads=[("gmv", iq), "eps"], writes=[("gmv2", iq)])
                    P.op("dve", L(lambda e, iq: e.reciprocal(out=gmv[:, iq, 3:4], in_=gmv[:, iq, 2:3]), iq), reads=[("gmv2", iq)], writes=[("gmv3", iq)])
                    P.op("dve", L(lambda e, bo, iq: e.tensor_scalar(out=ON[:, iq, :], in0=pss[bo][:, :], scalar1=gmv[:, iq, 0:1], scalar2=gmv[:, iq, 3:4], op0=ALU.subtract, op1=ALU.mult), bo, iq),
                         reads=[("ps", bo), ("gmv", iq), ("gmv3", iq)], writes=[("on", iq)])
                    for j in range(4):
                        P.op("pe", L(lambda e, iq, j: e.transpose(pstb[iq][:, 4 + j, :], ON[:, iq, j * 128:(j + 1) * 128], identb[:]), iq, j),
                             reads=[("on", iq), "ident"], writes=[("pst", iq)])
                    for j in range(4):
                        ec = hd * 4 + j
                        P.op("act", L(lambda e, j, ec, iq: e.activation(out=tmpz[:, j, :], in_=pstb[iq][:, 4 + j, :], func=AF.Identity, scale=gnsb[:, ec, 0:1], bias=gnsb[:, ec, 1:2]), j, ec, iq),
                             reads=[("pst", iq), "gn"], writes=[("tmpz", j)])
                        P.op("dve", L(lambda e, j, tsl: e.tensor_tensor(out=Zh[:, j, tsl], in0=tmpz[:, j, :], in1=GT[:, j, tsl], op=ALU.mult), j, tsl),
                             reads=[("tmpz", j), "gt"], writes=["zh"])

                stageA(0)
                for n in range(8):
                    if n + 1 < 8:
                        stageA(n + 1)
                    stageB(n)
                P.op("sp", L(lambda e, hd: e.dma_start(out=zs[hd * 512:(hd + 1) * 512, :].rearrange("(j p) t -> p j t", p=128), in_=Zh), hd),
                     reads=["zh"], writes=[("zs", hd)], dma=True)
            if mode == "main":
                P.barrier()
                for half in range(2):
                    P.op("sp", L(lambda e, half: e.dma_start(out=XX[:, half * 16:(half + 1) * 16, :], in_=zs[half * 2048:(half + 1) * 2048, :].rearrange("(j p) t -> p j t", p=128)), half),
                         reads=[("zs", h_) for h_ in range(8)], writes=[("zall", half)], dma=True)
                for ti in range(8):
                    wb = wcount[0] % 2
                    wcount[0] += 1
                    Wv = W[wb][:].rearrange("p k n -> p (k n)").rearrange("p (k n) -> p k n", k=32)
                    P.op("pool", L(lambda e, Wv, ti: e.dma_start(out=Wv, in_=w_out[:, ti * 256:(ti + 1) * 256].rearrange("(k p) n -> p k n", p=128)), Wv, ti),
                         writes=[("W", wb)], dma=True)
                    for mm in range(2):
                        m = ti * 2 + mm
                        for hb in range(2):
                            q = (m * 2 + hb) % 2
                            bi = nb()
                            P.op("sp", L(lambda e, m, hb, q: e.dma_start(out=hres[:, q, :], in_=hT[m * 128:(m + 1) * 128, hb * 512:(hb + 1) * 512]), m, hb, q),
                                 writes=[("hres", q)], dma=True)
                            for k in range(32):
                                P.op("pe", L(lambda e, Wv, k, mm, hb, bi: e.matmul(pss[bi][:, :], Wv[:, k, mm * 128:(mm + 1) * 128], XX[:, k, hb * 512:(hb + 1) * 512], start=(k == 0), stop=(k == 31)), Wv, k, mm, hb, bi),
                                     reads=[("W", wb), ("zall", k // 16)], writes=[("ps", bi)])
                            P.op("dve", L(lambda e, m, q, bi: e.scalar_tensor_tensor(out=hout[:, q, :], in0=pss[bi][:, :], scalar=modsb[:, 2, m:m + 1], in1=hres[:, q, :], op0=ALU.mult, op1=ALU.add), m, q, bi),
                                 reads=[("ps", bi), ("hres", q), "mod"], writes=[("hout", q)])
                            P.op("sp", L(lambda e, m, hb, q: e.dma_start(out=out[m * 128:(m + 1) * 128, hb * 512:(hb + 1) * 512], in_=hout[:, q, :]), m, hb, q),
                                 reads=[("hout", q)], dma=True, is_output=True)

        except _Stop:
            pass
        finish(P)
    return nc

HD = 64
NHC = 16
CH = NHC * HD
C = 128
NLEV = 7


def scan_consts():
    i = np.arange(128)
    tri_incl = (i[:, None] <= i[None, :]).astype(np.float32)
    tri_strict = (i[:, None] < i[None, :]).astype(np.float32)
    low_strict = (i[:, None] > i[None, :]).astype(np.float32)
    mask2 = np.concatenate([tri_strict, tri_incl], 1)
    ones = np.ones((128, 128), np.float32)
    return dict(tri_incl=tri_incl, low_strict=low_strict, mask2=mask2, ones=ones, ident=np.eye(128, dtype=np.float32))


def build_scan(S, o_fm=False, want_o=True):
    NCK = S // C
    nc = mk_nc()
    din = lambda name, shape: dram_in(nc, name, shape, F32)
    lw_t, kk_t, a_t, k_t, v_t = [din(n, [S, CH]) for n in ("lw_t", "kk_t", "a_t", "k_t", "v_t")]
    lw_f, kk_f, a_f, k_f, r_f = [din(n, [CH, S]) for n in ("lw_f", "kk_f", "a_f", "k_f", "r_f")]
    tri_incl = din("tri_incl", [128, 128])
    low_strict = din("low_strict", [128, 128])
    mask2 = din("mask2", [128, 256])
    ones_d = din("ones", [128, 128])
    s0 = din("s0", [NHC, HD, HD])
    o_out = dram_out(nc, "o_out", [CH, S] if o_fm else [S, CH], F32) if want_o else None
    s_out = dram_out(nc, "s_out", [NHC, HD, HD], F32)
    ident_d = din("ident", [128, 128])
    with ExitStack() as es:
        sb = mk_sb(nc, es)
        tri = sb("tri", [128, 128]); lows = sb("lows", [128, 128]); m2 = sb("m2", [128, 256]); ones = sb("ones_sb", [128, 128])
        identf = sb("identf", [128, 128]); identb = sb("identb", [128, 128], BF16)
        TT = {n: [sb(f"{n}{i}", [128, CH]) for i in range(2)] for n in ("lw", "kk", "a", "k", "v")}
        TB = {n: [sb(f"{n}b{i}", [128, CH], BF16) for i in range(2)] for n in ("v", "bh", "kh")}
        FF = {n: [sb(f"{n}f{i}", [128, 8, 128]) for i in range(2)] for n in ("lw", "kk", "a", "k", "r", "at", "bt")}
        FB = {n: [sb(f"{n}fb{i}", [128, 8, 128], BF16) for i in range(2)] for n in ("at", "rt", "bt", "kt")}
        tmpT = [sb(f"tmpT{i}", [128, CH]) for i in range(2)]
        tmpF = [sb(f"tmpF{i}", [128, 8, 128]) for i in range(3)]
        OUT = [sb(f"OUT{i}", [128, 8, 128] if o_fm else [128, CH]) for i in range(2)]
        ST = sb("ST", [128, NHC // 2, HD])
        STb = sb("STb", [128, NHC // 2, HD], BF16)
        pcT = [sb(f"pcT{i}", [128, 8]) for i in range(2)]
        NW = 4
        Nb = [[sb(f"N{w}_{j}", [128, 128], BF16) for j in range(NLEV)] for w in range(NW)]
        Mb = [[sb(f"M{w}_{j}", [128, 128], BF16) for j in range(NLEV)] for w in range(NW)]
        AB = [sb(f"AB{w}", [128, 256], BF16) for w in range(NW)]
        AK = [sb(f"AK{w}", [128, 256], BF16) for w in range(NW)]
        U = [[sb(f"U{w}_{j}", [128, HD], BF16) for j in range(2)] for w in range(NW)]
        pss = mk_pss(nc, es, 8)
        P = mk_prog(nc)
        ld = lambda dst, src, key: P.op("sp", lambda e: e.dma_start(out=dst, in_=src), writes=[key], dma=True)
        ld(tri[:], tri_incl, "tri"); ld(lows[:], low_strict, "lows"); ld(m2[:], mask2, "m2"); ld(ones[:], ones_d, "ones"); ld(identf[:], ident_d, "identf")
        ld(ST[:], s0.rearrange("(g two) k v -> (two k) g v", two=2), "st_all")
        P.op("dve", lambda e: e.tensor_copy(out=identb[:], in_=identf[:]), reads=["identf"], writes=["identb"])
        P.op("act", lambda e: e.copy(out=STb[:], in_=ST[:]), reads=["st_all"], writes=["stb_all"])
        for h in range(NHC):
            P.last_writer[("st", h)] = P.last_writer["st_all"]
            P.last_writer[("stb", h)] = P.last_writer["stb_all"]
        bankc = [0]

        def nb():
            b = bankc[0] % 8
            bankc[0] += 1
            return b

        RECORDING = [None]

        def mark():
            if RECORDING[0] is not None:
                RECORDING[0].append([])

        def chunk_pre(n):
            q = n % 2
            tsl = slice(n * C, (n + 1) * C)
            for nm, src in (("lw", lw_t), ("kk", kk_t), ("a", a_t), ("k", k_t), ("v", v_t)):
                P.op("sp", L(lambda e, nm, src: e.dma_start(out=TT[nm][q][:], in_=src[tsl, :]), nm, src), writes=[("T", nm, q)], dma=True)
            for nm, src in (("lw", lw_f), ("kk", kk_f), ("a", a_f), ("k", k_f), ("r", r_f)):
                P.op("sp", L(lambda e, nm, src: e.dma_start(out=FF[nm][q][:], in_=src[:, tsl].rearrange("(c p) t -> p c t", p=128)), nm, src), writes=[("F", nm, q)], dma=True)
            P.op("act", lambda e: e.copy(out=TB["v"][q][:], in_=TT["v"][q][:]), reads=[("T", "v", q)], writes=[("B", "v", q)])
            mark()
            b0, b1, b2, b3 = nb(), nb(), nb(), nb()
            for half, (bc, bt) in enumerate(((b0, b2), (b1, b3))):
                hs = slice(half * 512, (half + 1) * 512)
                mark()
                P.op("pe", L(lambda e, bc, hs: e.matmul(pss[bc][:, :], tri[:], TT["lw"][q][:, hs], start=True, stop=True), bc, hs), reads=["tri", ("T", "lw", q)], writes=[("ps", bc)])
                P.op("pe", L(lambda e, bt, hs: e.matmul(pss[bt][:, :], ones[:], TT["lw"][q][:, hs], start=True, stop=True), bt, hs), reads=["ones", ("T", "lw", q)], writes=[("ps", bt)])
                P.op("dve", L(lambda e, bc, bt, hs: e.tensor_copy(out=tmpT[0][:, hs], in_=pss[bc][:, :]), bc, bt, hs), reads=[("ps", bc)], writes=[("tmpT", 0, half)])
                P.op("dve", L(lambda e, bt, hs: e.tensor_tensor(out=tmpT[0][:, hs], in0=pss[bt][:, :], in1=tmpT[0][:, hs], op=ALU.subtract), bt, hs), reads=[("ps", bt), ("tmpT", 0, half)], writes=[("tmpT", 0, half)])
                mark()
                P.op("act", L(lambda e, hs: e.activation(out=tmpT[0][:, hs], in_=tmpT[0][:, hs], func=AF.Exp), hs), reads=[("tmpT", 0, half)], writes=[("tmpT", 0, half)])
                P.op("dve", L(lambda e, hs: e.tensor_tensor(out=tmpT[1][:, hs], in0=TT["kk"][q][:, hs], in1=TT["a"][q][:, hs], op=ALU.mult), hs), reads=[("T", "kk", q), ("T", "a", q)], writes=[("tmpT", 1, half)])
                P.op("dve", L(lambda e, hs: e.tensor_tensor(out=TB["bh"][q][:, hs], in0=tmpT[1][:, hs], in1=tmpT[0][:, hs], op=ALU.mult), hs), reads=[("tmpT", 1, half), ("tmpT", 0, half)], writes=[("B", "bh", q)])
                P.op("dve", L(lambda e, hs: e.tensor_tensor(out=TB["kh"][q][:, hs], in0=TT["k"][q][:, hs], in1=tmpT[0][:, hs], op=ALU.mult), hs), reads=[("T", "k", q), ("tmpT", 0, half)], writes=[("B", "kh", q)])
            mark()
            c0, c1 = nb(), nb()
            for g in range(8):
                bc = c0 if g < 4 else c1
                P.op("pe", L(lambda e, bc, g: e.matmul(pss[bc][:, (g % 4) * 128:(g % 4 + 1) * 128], TT["lw"][q][:, g * 128:(g + 1) * 128], tri[:], start=True, stop=True), bc, g),
                     reads=["tri", ("T", "lw", q)], writes=[("ps", bc)])
            csT, e1, e2 = tmpF[0], tmpF[1], tmpF[2]
            for hf, bc in enumerate((c0, c1)):
                gs = slice(hf * 4, (hf + 1) * 4)
                P.op("act", L(lambda e, bc, gs: e.copy(out=csT[:, gs, :], in_=pss[bc][:, :].rearrange("p (g t) -> p g t", g=4)), bc, gs), reads=[("ps", bc)], writes=[("tmpF", 0, hf)])
            mark()
            ck = [("tmpF", 0, 0), ("tmpF", 0, 1)]
            P.op("act", lambda e: e.activation(out=pcT[q][:, :], in_=csT[:, :, 127], func=AF.Exp), reads=ck, writes=[("pc", q)])
            P.op("act", lambda e: e.activation(out=e1[:], in_=csT[:], func=AF.Exp), reads=ck, writes=[("tmpF", 1)])
            P.op("dve", lambda e: e.tensor_tensor(out=FB["rt"][q][:], in0=FF["r"][q][:], in1=e1[:], op=ALU.mult), reads=[("F", "r", q), ("tmpF", 1)], writes=[("FB", "rt", q)])
            P.op("act", lambda e: e.activation(out=e2[:], in_=csT[:], func=AF.Exp, scale=-1.0), reads=ck, writes=[("tmpF", 2)])
            P.op("dve", lambda e: e.tensor_tensor(out=FB["kt"][q][:], in0=FF["k"][q][:], in1=e2[:], op=ALU.mult), reads=[("F", "k", q), ("tmpF", 2)], writes=[("FB", "kt", q)])
            P.op("dve", lambda e: e.tensor_tensor(out=e2[:], in0=FF["kk"][q][:], in1=e2[:], op=ALU.mult), reads=[("F", "kk", q), ("tmpF", 2), ("FB", "kt", q)], writes=[("tmpF", 2)])
            P.op("dve", lambda e: e.tensor_tensor(out=FF["bt"][q][:], in0=FF["a"][q][:], in1=e2[:], op=ALU.mult), reads=[("F", "a", q), ("tmpF", 2)], writes=[("F", "bt", q)])
            P.op("act", lambda e: e.copy(out=FB["bt"][q][:], in_=FF["bt"][q][:]), reads=[("F", "bt", q)], writes=[("FB", "bt", q)])
            P.op("dve", lambda e: e.tensor_tensor(out=e1[:], in0=csT[:], in1=FF["lw"][q][:], op=ALU.subtract), reads=ck + [("F", "lw", q), ("FB", "rt", q)], writes=[("tmpF", 1)])
            P.op("act", lambda e: e.activation(out=e1[:], in_=e1[:], func=AF.Exp), reads=[("tmpF", 1)], writes=[("tmpF", 1)])
            P.op("dve", lambda e: e.scalar_tensor_tensor(out=FF["at"][q][:], in0=FF["kk"][q][:], scalar=-1.0, in1=e1[:], op0=ALU.mult, op1=ALU.mult), reads=[("F", "kk", q), ("tmpF", 1)], writes=[("F", "at", q)])
            P.op("act", lambda e: e.copy(out=FB["at"][q][:], in_=FF["at"][q][:]), reads=[("F", "at", q)], writes=[("FB", "at", q)])

        def head_groups(n, h):
            q = n % 2
            w = h % NW
            pb = (h % 2) * 64
            g = h // 2
            hsl = slice(h * HD, (h + 1) * HD)
            fm = lambda nm: FF[nm][q][pb:pb + 64, g, :]
            fb = lambda nm: FB[nm][q][pb:pb + 64, g, :]
            pre, chain = [], []

            def g_N():
                bi = nb()
                P.op("pe", lambda e: e.matmul(pss[bi][:, 0:128], fm("at"), fm("bt"), start=True, stop=True), reads=[("F", "at", q), ("F", "bt", q)], writes=[("ps", bi)])
                P.op("dve", lambda e: e.tensor_tensor(out=Nb[w][0][:], in0=pss[bi][:, 0:128], in1=lows[:], op=ALU.mult), reads=[("ps", bi), "lows"], writes=[("N", w, 0)])
            pre.append(g_N)

            def g_AB():
                bi = nb()
                P.op("pe", lambda e: e.matmul(pss[bi][:, 0:128], fm("bt"), fm("at"), start=True, stop=True), reads=[("F", "at", q), ("F", "bt", q)], writes=[("ps", bi)])
                if want_o:
                    P.op("pe", lambda e: e.matmul(pss[bi][:, 128:256], fb("bt"), fb("rt"), start=True, stop=True), reads=[("FB", "rt", q), ("FB", "bt", q)], writes=[("ps", bi)])
                    P.op("dve", lambda e: e.tensor_tensor(out=AB[w][:], in0=pss[bi][:, 0:256], in1=m2[:], op=ALU.mult), reads=[("ps", bi), "m2"], writes=[("AB", w)])
                else:
                    P.op("dve", lambda e: e.tensor_tensor(out=AB[w][:, 0:128], in0=pss[bi][:, 0:128], in1=m2[:, 0:128], op=ALU.mult), reads=[("ps", bi), "m2"], writes=[("AB", w)])
            pre.append(g_AB)
            M0 = AB[w][:, 0:128]


            def Mj(j):
                return M0 if j == 0 else Mb[w][j][:]

            def mkey(j):
                return ("AB", w) if j == 0 else ("M", w, j)

            def g_AK():
                bi = nb()
                P.op("pe", lambda e: e.matmul(pss[bi][:, 0:128], fb("kt"), fb("at"), start=True, stop=True), reads=[("FB", "at", q), ("FB", "kt", q)], writes=[("ps", bi)])
                if want_o:
                    P.op("pe", lambda e: e.matmul(pss[bi][:, 128:256], fb("kt"), fb("rt"), start=True, stop=True), reads=[("FB", "rt", q), ("FB", "kt", q)], writes=[("ps", bi)])
                    P.op("dve", lambda e: e.tensor_tensor(out=AK[w][:], in0=pss[bi][:, 0:256], in1=m2[:], op=ALU.mult), reads=[("ps", bi), "m2"], writes=[("AK", w)])
                else:
                    P.op("dve", lambda e: e.tensor_tensor(out=AK[w][:, 0:128], in0=pss[bi][:, 0:128], in1=m2[:, 0:128], op=ALU.mult), reads=[("ps", bi), "m2"], writes=[("AK", w)])
            pre.append(g_AK)

            for j in range(NLEV - 1):
                def g_sq(j=j):
                    b1, b2 = nb(), nb()
                    P.op("pe", lambda e: e.matmul(pss[b1][:, 0:128], Nb[w][j][:], Mj(j), start=True, stop=True), reads=[("N", w, j), mkey(j)], writes=[("ps", b1)])
                    P.op("act", lambda e: e.copy(out=Mb[w][j + 1][:], in_=pss[b1][:, 0:128]), reads=[("ps", b1)], writes=[("M", w, j + 1)])
                    if j < NLEV - 2:
                        P.op("pe", lambda e: e.matmul(pss[b2][:, 0:128], Mj(j), Nb[w][j][:], start=True, stop=True), reads=[("N", w, j), mkey(j)], writes=[("ps", b2)])
                        P.op("dve", lambda e: e.tensor_copy(out=Nb[w][j + 1][:], in_=pss[b2][:, 0:128]), reads=[("ps", b2)], writes=[("N", w, j + 1)])
                pre.append(g_sq)

            Sh = ST[pb:pb + 64, g, :]
            Shb = STb[pb:pb + 64, g, :]
            Vt = TB["v"][q][:, hsl]

            def g_W():
                bi = nb()
                P.op("pe", lambda e: e.matmul(pss[bi][:, 0:HD], AK[w][:, 0:128], Vt, start=True, stop=False), reads=[("AK", w), ("B", "v", q)], writes=[("ps", bi)])
                P.op("pe", lambda e: e.matmul(pss[bi][:, 0:HD], fb("at"), Shb, start=False, stop=True), reads=[("FB", "at", q), ("stb", h)], writes=[("ps", bi)])
                P.op("act", lambda e: e.copy(out=U[w][0][:], in_=pss[bi][:, 0:HD]), reads=[("ps", bi)], writes=[("U", w, 0)])
            chain.append(g_W)
            for j in range(NLEV):
                def g_U(j=j):
                    bi = nb()
                    src, dst = U[w][j % 2], U[w][(j + 1) % 2]
                    P.op("pe", lambda e: e.matmul(pss[bi][:, 0:HD], identb[:], src[:], start=True, stop=False), reads=["identb", ("U", w, j % 2)], writes=[("ps", bi)])
                    P.op("pe", lambda e: e.matmul(pss[bi][:, 0:HD], Mj(j), src[:], start=False, stop=True), reads=[mkey(j), ("U", w, j % 2)], writes=[("ps", bi)])
                    if j % 2 == 0:
                        P.op("dve", lambda e: e.tensor_copy(out=dst[:], in_=pss[bi][:, 0:HD]), reads=[("ps", bi)], writes=[("U", w, (j + 1) % 2)])
                    else:
                        P.op("act", lambda e: e.copy(out=dst[:], in_=pss[bi][:, 0:HD]), reads=[("ps", bi)], writes=[("U", w, (j + 1) % 2)])
                chain.append(g_U)
            Uf = U[w][NLEV % 2]
            ufk = ("U", w, NLEV % 2)

            def g_O():
                bi = nb()
                if o_fm:
                    P.op("pe", lambda e: e.matmul(pss[bi][pb:pb + 64, 0:128], Uf[:], AB[w][:, 128:256], start=True, stop=False), reads=[("AB", w), ufk], writes=[("ps", bi)])
                    P.op("pe", lambda e: e.matmul(pss[bi][pb:pb + 64, 0:128], Vt, AK[w][:, 128:256], start=False, stop=False), reads=[("AK", w), ("B", "v", q)], writes=[("ps", bi)])
                    P.op("pe", lambda e: e.matmul(pss[bi][pb:pb + 64, 0:128], Shb, fb("rt"), start=False, stop=True), reads=[("FB", "rt", q), ("stb", h)], writes=[("ps", bi)])
                    P.op("act", lambda e: e.copy(out=OUT[q][pb:pb + 64, g, :], in_=pss[bi][pb:pb + 64, 0:128]), reads=[("ps", bi)], writes=[("out", q)])
                    return
                P.op("pe", lambda e: e.matmul(pss[bi][:, 0:HD], AB[w][:, 128:256], Uf[:], start=True, stop=False), reads=[("AB", w), ufk], writes=[("ps", bi)])
                P.op("pe", lambda e: e.matmul(pss[bi][:, 0:HD], AK[w][:, 128:256], Vt, start=False, stop=False), reads=[("AK", w), ("B", "v", q)], writes=[("ps", bi)])
                P.op("pe", lambda e: e.matmul(pss[bi][:, 0:HD], fb("rt"), Shb, start=False, stop=True), reads=[("FB", "rt", q), ("stb", h)], writes=[("ps", bi)])
                P.op("act", lambda e: e.copy(out=OUT[q][:, hsl], in_=pss[bi][:, 0:HD]), reads=[("ps", bi)], writes=[("out", q)])
            if want_o:
                chain.append(g_O)

            def g_S():
                bi = nb()
                P.op("pe", lambda e: e.matmul(pss[bi][pb:pb + 64, 0:HD], TB["bh"][q][:, hsl], Uf[:], start=True, stop=False), reads=[("B", "bh", q), ufk], writes=[("ps", bi)])
                P.op("pe", lambda e: e.matmul(pss[bi][pb:pb + 64, 0:HD], TB["kh"][q][:, hsl], Vt, start=False, stop=True), reads=[("B", "kh", q), ("B", "v", q)], writes=[("ps", bi)])
                P.op("dve", lambda e: e.scalar_tensor_tensor(out=Sh, in0=Sh, scalar=pcT[q][pb:pb + 64, g:g + 1], in1=pss[bi][pb:pb + 64, 0:HD], op0=ALU.mult, op1=ALU.add),
                     reads=[("st", h), ("pc", q), ("ps", bi)], writes=[("st", h)])
                P.op("act", lambda e: e.copy(out=Shb, in_=Sh), reads=[("st", h)], writes=[("stb", h)])
            chain.append(g_S)
            return pre, chain

        GH = 4

        def record(fn, *a):
            rec = [[]]
            RECORDING[0] = rec
            P.op = lambda *aa, **kk: rec[-1].append((aa, kk))
            try:
                fn(*a)
            finally:
                del P.op
                RECORDING[0] = None
            units = []
            for grp in rec:
                if any(aa[0] == "pe" for aa, kk in grp):
                    units.append(lambda grp=grp: [P.op(*aa, **kk) for aa, kk in grp])
                else:
                    for aa, kk in grp:
                        units.append(lambda aa=aa, kk=kk: P.op(*aa, **kk))
            return units

        for u_ in record(chunk_pre, 0):
            u_()
        for n in range(NCK):
            nxt = record(chunk_pre, n + 1) if n + 1 < NCK else []
            ngrp = NHC // GH
            per = (len(nxt) + ngrp - 1) // ngrp
            for gi, h0 in enumerate(range(0, NHC, GH)):
                streams = []
                for h in range(h0, h0 + GH):
                    pre, chain = head_groups(n, h)
                    head_pre, sq = pre[:3], pre[3:]
                    merged = list(head_pre) + [chain[0]]
                    us = chain[1:1 + NLEV]
                    tail = chain[1 + NLEV:]
                    for j in range(NLEV):
                        if j < len(sq):
                            merged.append(sq[j])
                        merged.append(us[j])
                    merged += tail
                    streams.append(merged)
                streams.append(nxt[gi * per:(gi + 1) * per])
                while any(streams):
                    for st_ in streams:
                        if st_:
                            st_.pop(0)()
            q = n % 2
            if want_o and o_fm:
                P.op("pool", L(lambda e, n, q: e.dma_start(out=o_out[:, n * C:(n + 1) * C].rearrange("(c p) t -> p c t", p=128), in_=OUT[q][:]), n, q), reads=[("out", q)], writes=["o_dram"], dma=True, is_output=True)
            elif want_o:
                P.op("pool", L(lambda e, n, q: e.dma_start(out=o_out[n * C:(n + 1) * C, :], in_=OUT[q][:]), n, q), reads=[("out", q)], writes=["o_dram"], dma=True, is_output=True)
        P.op("pool", lambda e: e.dma_start(out=s_out.rearrange("(g two) k v -> (two k) g v", two=2), in_=ST[:]), reads=[("st", h) for h in range(NHC)], writes=["s_dram"], dma=True, is_output=True)
        finish(P)
    return nc

T = 1024
LD = 96
LG = 256
GN_EPS_RW = 64 * 1e-5


def blockones():
    b = np.zeros((128, 128), np.float32)
    b[:64, :64] = 1.0
    b[64:, 64:] = 1.0
    return b


def build_rwa(want_tm=False):
    nc = mk_nc()
    din = lambda name, shape: dram_in(nc, name, shape, F32)
    hT = din("hT", [D, T + 1])
    mod = din("mod", [128, 4, 16])
    flag = din("flag", [128, 1])
    mu = din("mu", [128, 6, 16])
    w_rkv = din("w_rkv", [3, D, D])
    w1 = din("w1", [D, LD]); w2 = din("w2", [LD, D]); a1 = din("a1", [D, LD]); a2 = din("a2", [LD, D])
    g1 = din("g1", [D, LG]); g2 = din("g2", [LG, D])
    vecs = din("vecs", [128, 4, 16])
    bo_d = din("bo", [128, 128])
    outs = {n: dram_out(nc, n, [D, T], F32) for n in ("r_o", "k_o", "v_o", "kk_o", "a_o", "lw_o", "g_o")}
    TMN = {"k_o": "k_t", "v_o": "v_t", "kk_o": "kk_t", "a_o": "a_t", "lw_o": "lw_t"}
    if want_tm:
        outs_t = {n: dram_out(nc, n, [T, D], F32) for n in TMN.values()}
        ident_d = din("ident", [128, 128])
    with ExitStack() as es:
        sb = mk_sb(nc, es)
        XM = sb("XM", [128, 16, T + 1], BF16)
        XS = sb("XS", [128, 16, T], BF16)
        W = [sb(f"W{i}", [128, 16, 512], BF16) for i in range(2)]
        stage = sb("stage", [128, 16, 256])
        modsb = sb("modsb", [128, 4, 16]); flagsb = sb("flagsb", [128, 1]); musb = sb("musb", [128, 6, 16]); mu1 = sb("mu1", [128, 6, 16])
        vsb = sb("vsb", [128, 4, 16]); vka = sb("vka", [128, 16])
        ones = sb("ones", [128, 128], BF16); epsb = sb("epsb", [128, 2]); bo = sb("bo_sb", [128, 128])
        rstd = sb("rstd", [128, 512]); sq = sb("sq", [128, 2, 512], BF16); xn = sb("xn", [128, 2, 512])
        w1s = sb("w1s", [128, 16, LD], BF16); a1s = sb("a1s", [128, 16, LD], BF16); g1s = sb("g1s", [128, 16, LG], BF16)
        w2s = sb("w2s", [LD, D], BF16); a2s = sb("a2s", [LD, D], BF16); g2s = sb("g2s", [128, 2, D], BF16)
        T1 = sb("T1", [LD, T], BF16); T2 = sb("T2", [LD, T], BF16); T3 = sb("T3", [128, 2, T], BF16)
        tmul = sb("tmul", [128, 2, T])
        ob = [sb(f"ob{i}", [128, 512]) for i in range(8)]
        if want_tm:
            identF = sb("identF", [128, 128])
            ttile = [sb(f"tt{i}", [128, 4, 128]) for i in range(2)]
        pss = mk_pss(nc, es, 8)
        P = mk_prog(nc)
        ld = lambda dst, src, key, q="sp": P.op(q, lambda e: e.dma_start(out=dst, in_=src), writes=[key], dma=True)
        ld(modsb[:], mod, "mod"); ld(flagsb[:], flag, "flag"); ld(musb[:], mu, "mu"); ld(vsb[:], vecs, "vecs"); ld(bo[:], bo_d, "bo")
        ld(w1s[:], w1.rearrange("(k p) n -> p k n", p=128), "w1", "pool"); ld(a1s[:], a1.rearrange("(k p) n -> p k n", p=128), "a1", "pool")
        ld(g1s[:], g1.rearrange("(k p) n -> p k n", p=128), "g1", "pool")
        ld(w2s[:], w2, "w2", "pool"); ld(a2s[:], a2, "a2", "pool"); ld(g2s[:], g2.rearrange("(k p) n -> p k n", p=128), "g2", "pool")
        if want_tm:
            ld(identF[:], ident_d, "identF")
        P.op("dve", lambda e: e.memset(ones[:], 1.0), writes=["ones"])
        P.op("dve", lambda e: e.memset(epsb[:, 0:1], EPS), writes=["eps"])
        P.op("dve", lambda e: e.memset(epsb[:, 1:2], 1e-12), writes=["eps"])
        P.op("dve", lambda e: e.tensor_scalar(out=modsb[:, 1, :], in0=modsb[:, 1, :], scalar1=1.0, scalar2=None, op0=ALU.add), reads=["mod"], writes=["mod"])
        P.op("dve", lambda e: e.tensor_scalar(out=mu1[:], in0=musb[:], scalar1=-1.0, scalar2=1.0, op0=ALU.mult, op1=ALU.add), reads=["mu"], writes=["mu1"])
        P.op("dve", lambda e: e.tensor_scalar(out=vka[:], in0=vsb[:, 3, :], scalar1=-1.0, scalar2=1.0, op0=ALU.mult, op1=ALU.add), reads=["vecs"], writes=["vka"])
        tk = {"rstd": rstd, "sq": sq, "xn": xn, "eps": epsb}
        emit_norm_mod(P, nc, es, hT, XM, modsb, 0, 1, T + 1, 0, stage, ones, pss, tk, flag=flagsb, halo=1)

        def make_xs(i):
            for c in range(16):
                tq = c % 2
                P.op("act", L(lambda e, c, tq: e.activation(out=tmul[:, tq, :], in_=XM[:, c, 0:T], func=AF.Copy, scale=musb[:, i, c:c + 1]), c, tq),
                     reads=[("xm", c), "mu"], writes=[("tmul", tq)])
                P.op("dve", L(lambda e, c, tq: e.scalar_tensor_tensor(out=XS[:, c, :], in0=XM[:, c, 1:T + 1], scalar=mu1[:, i, c:c + 1], in1=tmul[:, tq, :], op0=ALU.mult, op1=ALU.add), c, tq),
                     reads=[("xm", c), "mu1", ("tmul", tq)], writes=[("xs", c)])

        bankc = [0]; obc = [0]; wcount = [0]

        def nb():
            b = 1 + bankc[0] % 7
            bankc[0] += 1
            return b

        def nob():
            b = obc[0] % 8
            obc[0] += 1
            return b

        def load_w(src):
            wb = wcount[0] % 2
            wcount[0] += 1
            P.op("pool", L(lambda e, wb, src: e.dma_start(out=W[wb][:], in_=src.rearrange("(k p) n -> p k n", p=128)), wb, src), writes=[("W", wb)], dma=True)
            return wb

        def store(name, m, hb, oi):
            P.op("sp", L(lambda e, name, m, hb, oi: e.dma_start(out=outs[name][m * 128:(m + 1) * 128, hb * 512:(hb + 1) * 512], in_=ob[oi][:]), name, m, hb, oi),
                 reads=[("ob", oi)], dma=True, is_output=True)
            if want_tm and name in TMN:
                bi = nb()
                ti = ttc[0] % 2
                ttc[0] += 1
                for j in range(4):
                    P.op("pe", L(lambda e, bi, j, oi: e.matmul(pss[bi][:, j * 128:(j + 1) * 128], ob[oi][:, j * 128:(j + 1) * 128], identF[:], start=True, stop=True), bi, j, oi),
                         reads=[("ob", oi), "identF"], writes=[("ps", bi)])
                P.op("act", L(lambda e, bi, ti: e.copy(out=ttile[ti][:], in_=pss[bi][:, :].rearrange("p (j c) -> p j c", j=4)), bi, ti), reads=[("ps", bi)], writes=[("tt", ti)])
                P.op("sp", L(lambda e, name, m, hb, ti: e.dma_start(out=outs_t[TMN[name]][hb * 512:(hb + 1) * 512, m * 128:(m + 1) * 128].rearrange("(j p) c -> p j c", p=128), in_=ttile[ti][:]), name, m, hb, ti),
                     reads=[("tt", ti)], dma=True, is_output=True)

        ttc = [0]

        def lora_hidden(i, ws, M, nchunk, dst, func, key):
            make_xs(i)
            for mc in range(nchunk):
                mw = min(128, M - mc * 128)
                for hb in range(2):
                    bi = nb()
                    for k in range(16):
                        P.op("pe", L(lambda e, k, mc, mw, hb, bi: e.matmul(pss[bi][0:mw, :], ws[:, k, mc * 128:mc * 128 + mw], XS[:, k, hb * 512:(hb + 1) * 512], start=(k == 0), stop=(k == 15)), k, mc, mw, hb, bi),
                             reads=[key, ("xs", k)], writes=[("ps", bi)])
                    d = dst[0:mw, hb * 512:(hb + 1) * 512] if nchunk == 1 else dst[:, mc, hb * 512:(hb + 1) * 512]
                    P.op("act", L(lambda e, d, bi, mw: e.activation(out=d, in_=pss[bi][0:mw, :], func=func), d, bi, mw), reads=[("ps", bi)], writes=[key + "_h"])
        lora_hidden(3, w1s, LD, 1, T1, AF.Tanh, "w1")
        lora_hidden(4, a1s, LD, 1, T2, AF.Copy, "a1")
        lora_hidden(5, g1s, LG, 2, T3, AF.Sigmoid, "g1")

        def big_proj(i, widx, per_chunk):
            make_xs(i)
            for ti in range(4):
                wb = load_w(w_rkv[widx][:, ti * 512:(ti + 1) * 512])
                for mm_ in range(4):
                    m = ti * 4 + mm_
                    for hb in range(2):
                        bi = nb()
                        for k in range(16):
                            P.op("pe", L(lambda e, wb, k, mm_, hb, bi: e.matmul(pss[bi][:, :], W[wb][:, k, mm_ * 128:(mm_ + 1) * 128], XS[:, k, hb * 512:(hb + 1) * 512], start=(k == 0), stop=(k == 15)), wb, k, mm_, hb, bi),
                                 reads=[("W", wb), ("xs", k)], writes=[("ps", bi)])
                        per_chunk(m, hb, bi)

        def r_chunk(m, hb, bi):
            oi = nob()
            P.op("act", L(lambda e, oi, bi: e.copy(out=ob[oi][:], in_=pss[bi][:, :]), oi, bi), reads=[("ps", bi)], writes=[("ob", oi)])
            store("r_o", m, hb, oi)
        big_proj(0, 0, r_chunk)

        def k_chunk(m, hb, bi):
            hs = slice(hb * 512, (hb + 1) * 512)
            ok, oa, okr, osq, okk = nob(), nob(), nob(), nob(), nob()
            ba = nb()
            P.op("pe", L(lambda e, ba, m, hs: e.matmul(pss[ba][:, :], a2s[:, m * 128:(m + 1) * 128], T2[:, hs], start=True, stop=True), ba, m, hs), reads=["a2", "a1_h"], writes=[("ps", ba)])
            P.op("act", L(lambda e, oa, ba, m: e.activation(out=ob[oa][:], in_=pss[ba][:, :], func=AF.Sigmoid, bias=vsb[:, 1, m:m + 1]), oa, ba, m), reads=[("ps", ba), "vecs"], writes=[("ob", oa)])
            store("a_o", m, hb, oa)
            P.op("act", L(lambda e, ok, bi: e.copy(out=ob[ok][:], in_=pss[bi][:, :]), ok, bi), reads=[("ps", bi)], writes=[("ob", ok)])
            P.op("dve", L(lambda e, okr, ok, m: e.tensor_scalar(out=ob[okr][:], in0=ob[ok][:], scalar1=vsb[:, 2, m:m + 1], scalar2=None, op0=ALU.mult), okr, ok, m), reads=[("ob", ok), "vecs"], writes=[("ob", okr)])
            P.op("act", L(lambda e, osq, okr: e.activation(out=ob[osq][:], in_=ob[okr][:], func=AF.Square), osq, okr), reads=[("ob", okr)], writes=[("ob", osq)])
            bs_ = nb()
            P.op("pe", L(lambda e, bs_, osq: e.matmul(pss[bs_][:, :], bo[:], ob[osq][:], start=True, stop=True), bs_, osq), reads=["bo", ("ob", osq)], writes=[("ps", bs_)])
            P.op("act", L(lambda e, osq, bs_: e.activation(out=ob[osq][:], in_=pss[bs_][:, :], func=AF.Sqrt), osq, bs_), reads=[("ps", bs_)], writes=[("ob", osq)])
            P.op("dve", L(lambda e, osq: e.tensor_scalar(out=ob[osq][:], in0=ob[osq][:], scalar1=1e-12, scalar2=None, op0=ALU.max), osq), reads=[("ob", osq)], writes=[("ob", osq)])
            P.op("dve", L(lambda e, osq: e.reciprocal(out=ob[osq][:], in_=ob[osq][:]), osq), reads=[("ob", osq)], writes=[("ob", osq)])
            P.op("dve", L(lambda e, okk, okr, osq: e.tensor_tensor(out=ob[okk][:], in0=ob[okr][:], in1=ob[osq][:], op=ALU.mult), okk, okr, osq), reads=[("ob", okr), ("ob", osq)], writes=[("ob", okk)])
            store("kk_o", m, hb, okk)
            P.op("dve", L(lambda e, okr, oa, m: e.tensor_scalar(out=ob[okr][:], in0=ob[oa][:], scalar1=vsb[:, 3, m:m + 1], scalar2=vka[:, m:m + 1], op0=ALU.mult, op1=ALU.add), okr, oa, m),
                 reads=[("ob", oa), "vecs", "vka", ("ob", okr)], writes=[("ob", okr)])
            P.op("dve", L(lambda e, ok, okr: e.tensor_tensor(out=ob[ok][:], in0=ob[ok][:], in1=ob[okr][:], op=ALU.mult), ok, okr), reads=[("ob", ok), ("ob", okr)], writes=[("ob", ok)])
            store("k_o", m, hb, ok)
        big_proj(1, 1, k_chunk)

        def v_chunk(m, hb, bi):
            oi = nob()
            P.op("act", L(lambda e, oi, bi: e.copy(out=ob[oi][:], in_=pss[bi][:, :]), oi, bi), reads=[("ps", bi)], writes=[("ob", oi)])
            store("v_o", m, hb, oi)
        big_proj(2, 2, v_chunk)

        c_lw = -math.exp(-0.5)
        for m in range(16):
            for hb in range(2):
                hs = slice(hb * 512, (hb + 1) * 512)
                bi = nb(); oi = nob()
                P.op("pe", L(lambda e, bi, m, hs: e.matmul(pss[bi][:, :], w2s[:, m * 128:(m + 1) * 128], T1[:, hs], start=True, stop=True), bi, m, hs), reads=["w2", "w1_h"], writes=[("ps", bi)])
                P.op("act", L(lambda e, oi, bi, m: e.activation(out=ob[oi][:], in_=pss[bi][:, :], func=AF.Sigmoid, bias=vsb[:, 0, m:m + 1]), oi, bi, m), reads=[("ps", bi), "vecs"], writes=[("ob", oi)])
                P.op("dve", L(lambda e, oi: e.tensor_scalar(out=ob[oi][:], in0=ob[oi][:], scalar1=c_lw, scalar2=None, op0=ALU.mult), oi), reads=[("ob", oi)], writes=[("ob", oi)])
                store("lw_o", m, hb, oi)
                bi = nb(); oi = nob()
                for kc in range(2):
                    P.op("pe", L(lambda e, bi, m, hs, kc: e.matmul(pss[bi][:, :], g2s[:, kc, m * 128:(m + 1) * 128], T3[:, kc, hs], start=(kc == 0), stop=(kc == 1)), bi, m, hs, kc), reads=["g2", "g1_h"], writes=[("ps", bi)])
                P.op("act", L(lambda e, oi, bi: e.copy(out=ob[oi][:], in_=pss[bi][:, :]), oi, bi), reads=[("ps", bi)], writes=[("ob", oi)])
                store("g_o", m, hb, oi)
        finish(P)
    return nc


def build_rwc():
    nc = mk_nc()
    din = lambda name, shape: dram_in(nc, name, shape, F32)
    hT = din("hT", [D, T]); mod = din("mod", [128, 4, 16])
    o_i, r_i, k_i, v_i, g_i = [din(n, [D, T]) for n in ("o_i", "r_i", "k_i", "v_i", "g_i")]
    vecs = din("vecs", [128, 3, 16])
    bo_d = din("bo", [128, 128])
    w_out = din("w_out", [D, D])
    out = dram_out(nc, "out", [D, T], F32)
    with ExitStack() as es:
        sb = mk_sb(nc, es)
        Z = sb("Z", [128, 16, T], BF16)
        W = [sb(f"W{i}", [128, 16, 512], BF16) for i in range(2)]
        modsb = sb("modsb", [128, 4, 16]); vsb = sb("vsb", [128, 3, 16]); bo = sb("bo_sb", [128, 128]); bo64 = sb("bo64", [128, 128]); epsb = sb("epsb", [128, 1])
        IN = {n: [sb(f"in_{n}{i}", [128, 512]) for i in range(2)] for n in ("o", "r", "k", "v", "g")}
        tmpq = [[sb(f"tmp{q}_{i}", [128, 512]) for i in range(3)] for q in range(2)]
        hres = sb("hres", [128, 2, 512]); hout = sb("hout", [128, 2, 512])
        pss = mk_pss(nc, es, 8)
        P = mk_prog(nc)
        ld = lambda dst, src, key, q="sp": P.op(q, lambda e: e.dma_start(out=dst, in_=src), writes=[key], dma=True)
        ld(modsb[:], mod, "mod"); ld(vsb[:], vecs, "vecs"); ld(bo[:], bo_d, "bo")
        P.op("dve", lambda e: e.memset(epsb[:], GN_EPS_RW), writes=["eps"])
        P.op("dve", lambda e: e.tensor_scalar(out=bo64[:], in0=bo[:], scalar1=1.0 / 64, scalar2=None, op0=ALU.mult), reads=["bo"], writes=["bo64"])
        bankc = [0]

        def nb():
            b = bankc[0] % 8
            bankc[0] += 1
            return b
        def rwc_iter(m, hb, q, tmp):
            TK = lambda i: ("tmp", q, i)
            for n, src in (("o", o_i), ("r", r_i), ("k", k_i), ("v", v_i), ("g", g_i)):
                P.op("sp", L(lambda e, n, src, m, hb, q: e.dma_start(out=IN[n][q][:], in_=src[m * 128:(m + 1) * 128, hb * 512:(hb + 1) * 512]), n, src, m, hb, q), writes=[("in", n, q)], dma=True)
            b1 = nb()
            P.op("pe", L(lambda e, b1, q: e.matmul(pss[b1][:, :], bo64[:], IN["o"][q][:], start=True, stop=True), b1, q), reads=["bo64", ("in", "o", q)], writes=[("ps", b1)])
            P.op("dve", L(lambda e, b1, q: e.tensor_tensor(out=tmp[0][:], in0=IN["o"][q][:], in1=pss[b1][:, :], op=ALU.subtract), b1, q), reads=[("ps", b1), ("in", "o", q)], writes=[TK(0)])
            P.op("act", lambda e: e.activation(out=tmp[1][:], in_=tmp[0][:], func=AF.Square), reads=[TK(0)], writes=[TK(1)])
            b2 = nb()
            P.op("pe", L(lambda e, b2: e.matmul(pss[b2][:, :], bo64[:], tmp[1][:], start=True, stop=True), b2), reads=["bo64", TK(1)], writes=[("ps", b2)])
            P.op("act", L(lambda e, b2: e.activation(out=tmp[1][:], in_=pss[b2][:, :], func=AF.Sqrt, bias=epsb[:, 0:1]), b2), reads=[("ps", b2), "eps"], writes=[TK(1)])
            P.op("dve", lambda e: e.reciprocal(out=tmp[1][:], in_=tmp[1][:]), reads=[TK(1)], writes=[TK(1)])
            P.op("dve", lambda e: e.tensor_tensor(out=tmp[0][:], in0=tmp[0][:], in1=tmp[1][:], op=ALU.mult), reads=[TK(0), TK(1)], writes=[TK(0)])
            P.op("act", L(lambda e, m: e.activation(out=tmp[0][:], in_=tmp[0][:], func=AF.Identity, scale=vsb[:, 0, m:m + 1], bias=vsb[:, 1, m:m + 1]), m), reads=[TK(0), "vecs"], writes=[TK(0)])
            P.op("dve", L(lambda e, m, q: e.scalar_tensor_tensor(out=tmp[2][:], in0=IN["r"][q][:], scalar=vsb[:, 2, m:m + 1], in1=IN["k"][q][:], op0=ALU.mult, op1=ALU.mult), m, q),
                 reads=[("in", "r", q), ("in", "k", q), "vecs"], writes=[TK(2)])
            b3 = nb()
            P.op("pe", L(lambda e, b3: e.matmul(pss[b3][:, :], bo[:], tmp[2][:], start=True, stop=True), b3), reads=["bo", TK(2)], writes=[("ps", b3)])
            P.op("dve", L(lambda e, b3, q: e.tensor_tensor(out=tmp[2][:], in0=pss[b3][:, :], in1=IN["v"][q][:], op=ALU.mult), b3, q), reads=[("ps", b3), ("in", "v", q), TK(2)], writes=[TK(2)])
            P.op("dve", lambda e: e.tensor_tensor(out=tmp[0][:], in0=tmp[0][:], in1=tmp[2][:], op=ALU.add), reads=[TK(0), TK(2)], writes=[TK(0)])
            P.op("dve", L(lambda e, m, hb, q: e.tensor_tensor(out=Z[:, m, hb * 512:(hb + 1) * 512], in0=tmp[0][:], in1=IN["g"][q][:], op=ALU.mult), m, hb, q), reads=[TK(0), ("in", "g", q)], writes=[("z", m)])

        it = 0
        for m in range(16):
            for hb in range(2):
                rwc_iter(m, hb, it % 2, tmpq[it % 2])
                it += 1
        wcount = [0]
        for ti in range(4):
            wb = wcount[0] % 2
            wcount[0] += 1
            P.op("pool", L(lambda e, wb, ti: e.dma_start(out=W[wb][:], in_=w_out[:, ti * 512:(ti + 1) * 512].rearrange("(k p) n -> p k n", p=128)), wb, ti), writes=[("W", wb)], dma=True)
            for mm_ in range(4):
                m = ti * 4 + mm_
                for hb in range(2):
                    q = (m * 2 + hb) % 2
                    bi = nb()
                    P.op("sp", L(lambda e, m, hb, q: e.dma_start(out=hres[:, q, :], in_=hT[m * 128:(m + 1) * 128, hb * 512:(hb + 1) * 512]), m, hb, q), writes=[("hres", q)], dma=True)
                    for k in range(16):
                        P.op("pe", L(lambda e, wb, k, mm_, hb, bi: e.matmul(pss[bi][:, :], W[wb][:, k, mm_ * 128:(mm_ + 1) * 128], Z[:, k, hb * 512:(hb + 1) * 512], start=(k == 0), stop=(k == 15)), wb, k, mm_, hb, bi),
                             reads=[("W", wb), ("z", k)], writes=[("ps", bi)])
                    P.op("dve", L(lambda e, m, q, bi: e.scalar_tensor_tensor(out=hout[:, q, :], in0=pss[bi][:, :], scalar=modsb[:, 2, m:m + 1], in1=hres[:, q, :], op0=ALU.mult, op1=ALU.add), m, q, bi),
                         reads=[("ps", bi), ("hres", q), "mod"], writes=[("hout", q)])
                    P.op("sp", L(lambda e, m, hb, q: e.dma_start(out=out[m * 128:(m + 1) * 128, hb * 512:(hb + 1) * 512], in_=hout[:, q, :]), m, hb, q), reads=[("hout", q)], dma=True, is_output=True)
        finish(P)
    return nc

T = 1024
NCOLA = 1536


def build_ada():
    nc = mk_nc()
    din = lambda name, shape: dram_in(nc, name, shape, F32)
    cT = din("cT", [128, 16, 4])
    aw = din("aw", [4, D, NCOLA])
    ab = din("ab", [128, 48])
    out = dram_out(nc, "out", [128, 48, 4], F32)
    with ExitStack() as es:
        sb = mk_sb(nc, es)
        cs = sb("cs", [128, 16, 4]); cond = sb("cond", [128, 16, 4]); absb = sb("absb", [128, 48]); osb = sb("osb", [128, 48, 4])
        W = [sb(f"W{i}", [128, 16, 512]) for i in range(2)]
        pss = mk_pss(nc, es, 4)
        P = mk_prog(nc)
        P.op("sp", lambda e: e.dma_start(out=cs[:], in_=cT), writes=["cs"], dma=True)
        P.op("sp", lambda e: e.dma_start(out=absb[:], in_=ab), writes=["ab"], dma=True)
        P.op("act", lambda e: e.activation(out=cond[:], in_=cs[:], func=AF.Silu), reads=["cs"], writes=["cond"])
        it = 0
        for l in range(4):
            for ti in range(3):
                wb = it % 2
                it += 1
                P.op("sp", L(lambda e, wb, l, ti: e.dma_start(out=W[wb][:], in_=aw[l][:, ti * 512:(ti + 1) * 512].rearrange("(k p) n -> p k n", p=128)), wb, l, ti), writes=[("W", wb)], dma=True)
                for fcl in range(4):
                    fc = l * 12 + ti * 4 + fcl
                    bi = fc % 4
                    for k in range(16):
                        P.op("pe", L(lambda e, wb, k, fcl, bi: e.matmul(pss[bi][:, 0:4], W[wb][:, k, fcl * 128:(fcl + 1) * 128], cond[:, k, :], start=(k == 0), stop=(k == 15)), wb, k, fcl, bi),
                             reads=[("W", wb), "cond"], writes=[("ps", bi)])
                    P.op("act", L(lambda e, fc, bi: e.activation(out=osb[:, fc, :], in_=pss[bi][:, 0:4], func=AF.Identity, bias=absb[:, fc:fc + 1]), fc, bi), reads=[("ps", bi), "ab"], writes=["osb"])
        P.op("sp", lambda e: e.dma_start(out=out, in_=osb[:]), reads=["osb"], dma=True, is_output=True)
        finish(P)
    return nc


def build_final():
    nc = mk_nc()
    din = lambda name, shape: dram_in(nc, name, shape, F32)
    hT = din("hT", [D, T])
    mod = din("mod", [128, 4, 16])
    out = dram_out(nc, "out", [D, T], F32)
    with ExitStack() as es:
        sb = mk_sb(nc, es)
        XO = sb("XO", [128, 16, T])
        stage = sb("stage", [128, 16, 256]); modsb = sb("modsb", [128, 4, 16]); ones = sb("ones", [128, 128], BF16); epsb = sb("epsb", [128, 1])
        rstd = sb("rstd", [128, 512]); sq = sb("sq", [128, 2, 512], BF16); xn = sb("xn", [128, 2, 512])
        pss = mk_pss(nc, es, 2)
        P = mk_prog(nc)
        P.op("sp", lambda e: e.dma_start(out=modsb[:], in_=mod), writes=["mod"], dma=True)
        P.op("dve", lambda e: e.memset(ones[:], 1.0), writes=["ones"])
        P.op("dve", lambda e: e.memset(epsb[:], EPS), writes=["eps"])
        tk = {"rstd": rstd, "sq": sq, "xn": xn, "eps": epsb}
        emit_norm_mod(P, nc, es, hT, XO, modsb, 0, 1, T, 0, stage, ones, pss, tk)
        for c in range(16):
            P.op("sp", L(lambda e, c: e.dma_start(out=out[c * 128:(c + 1) * 128, :], in_=XO[:, c, :]), c), reads=[("xm", c)], dma=True, is_output=True)
        finish(P)
    return nc

NWORDS = 53000
PAIRS = [[0, 1], [2, 3], [4, 5], [6, 7]]
QUADS = [[0, 1, 2, 3], [4, 5, 6, 7]]
P4 = [[0, 4], [1, 5], [2, 6], [3, 7]]


def fused_input_specs():
    sp = {}
    f = lambda n, s, d=F32: sp.__setitem__(n, (s, d))
    f("x_hT", [D, T]); f("flag", [128, 1]); f("selb", [128, 4]); f("sel8", [128, 8])
    f("cT", [128, 16, 4]); f("aw", [4, D, NCOLA]); f("ab", [128, 48])
    for l in range(4):
        f(f"ffn{l}_w_up", [D, 2 * DFF]); f(f"ffn{l}_cw", [128, 86, 4]); f(f"ffn{l}_w_dn", [DFF, D])
    for j in range(2):
        f(f"sg{j}_w_in", [D, 2 * D]); f(f"sg{j}_lnp", [128, 2, D]); f(f"sg{j}_wsT", [128, 16, 128]); f(f"sg{j}_bs", [1, D]); f(f"sg{j}_w_out", [D, D])
    f("tri", [128, 128])
    f("posb", [128, T], I32); f("invf", [128, 1]); f("ret_w_in", [D, 6 * D]); f("kdecA", [128, NH, 8]); f("kdec", [128, NH]); f("ident", [128, 128])
    f("decT", [128, NH, 128]); f("qdec", [128, NH, 128]); f("gnp", [128, 32, 2]); f("ret_w_out", [2 * D, D])
    f("mu", [128, 6, 16]); f("w_rkv", [3, D, D]); f("w1", [D, LD]); f("w2", [LD, D]); f("a1", [D, LD]); f("a2", [LD, D]); f("g1", [D, LG]); f("g2", [LG, D])
    f("vecsA", [128, 4, 16]); f("bo", [128, 128]); f("vecsC", [128, 3, 16]); f("rw_w_out", [D, D])
    f("tri_incl", [128, 128]); f("low_strict", [128, 128]); f("mask2", [128, 256]); f("ones", [128, 128])
    f("fm", [128, 4, 16])
    return sp


def build_fused(debug=False):
    nc = bass.Bass("TRN2", target_bir_lowering=False)
    ext = {n: nc.dram_tensor(n, s, d, kind="ExternalInput").ap() for n, (s, d) in fused_input_specs().items()}
    out = nc.dram_tensor("out", [D, T], F32, kind="ExternalOutput").ap()
    itn = lambda n, s, d=F32: nc.dram_tensor(n, s, d).ap()
    stages = ["0a", "0b", "1a", "1b", "2a", "2b", "3a", "3b"]
    H = {}
    for st in stages:
        H[st] = nc.dram_tensor(f"h{st}", [D, T + 2], F32, kind="ExternalOutput").ap() if debug else itn(f"h{st}", [D, T + 2])
    ada_o = itn("ada_o", [128, 48, 4])
    cc_a = itn("cc_a", [128, 8 * 192]); cc_b = itn("cc_b", [128, 8 * 192]); cc_c = itn("cc_c", [128, 8 * 192])
    s_loc = itn("s_loc", [NH, 256, 512]); s_in = itn("s_in", [NH, 256, 512])
    rw = {n: itn("rw_" + n, [D, T]) for n in ("r_o", "k_o", "v_o", "kk_o", "a_o", "lw_o", "g_o")}
    rwt = {n: itn("rwt_" + n, [T, D]) for n in ("k_t", "v_t", "kk_t", "a_t", "lw_t")}
    oT = itn("oT", [D, T])
    sc_zero = itn("sc_zero", [2, NHC, HD, HD]); sc_loc = itn("sc_loc", [2, NHC, HD, HD]); sc_in = itn("sc_in", [2, NHC, HD, HD]); sc_dump = itn("sc_dump", [2, NHC, HD, HD])
    with ExitStack() as es:
        arena = es.enter_context(nc.sbuf_tensor("arena", [128, NWORDS], F32))
        pss_all = [es.enter_context(nc.psum_tensor(f"ps{i}", [128, 512], F32)) for i in range(8)]
        P = Prog(nc)
        cx = Ctx(nc, P, arena, NWORDS, pss_all)
        CX[0] = cx
        try:
            flagsb = cx.sb("flag_p", [128, 1]); omf = cx.sb("omf_p", [128, 1]); selb = cx.sb("selb_p", [128, 4]); sel8 = cx.sb("sel8_p", [128, 8])
            MODALL = cx.sb("modall", [128, 4, 112])
            cx.base = cx.off
            ldp = lambda dst, src: P.op("sp", lambda e: e.dma_start(out=dst, in_=src), writes=["pers"], dma=True)
            ldp(flagsb[:], ext["flag"]); ldp(selb[:], ext["selb"]); ldp(sel8[:], ext["sel8"])
            P.op("dve", lambda e: e.memset(MODALL[:], 0.0), writes=["modall"])
            P.op("dve", lambda e: e.tensor_scalar(out=omf[:], in0=flagsb[:], scalar1=-1.0, scalar2=1.0, op0=ALU.mult, op1=ALU.add), reads=["pers"], writes=["omf"])
            P.barrier()

            cx.io = {"cT": ext["cT"], "aw": ext["aw"], "ab": ext["ab"], "out": ada_o}
            build_ada()
            cx.reset()
            osb = cx.sb("osb2", [128, 192]); CB = cx.sb("CB", [128, 8, 192]); G = cx.sb("G", [128, 8, 48, 4])
            P.op("sp", lambda e: e.dma_start(out=osb[:], in_=ada_o.rearrange("p a b -> p (a b)")), writes=["osb2"], dma=True)
            for r in range(8):
                P.op("dve", L(lambda e, r: e.tensor_scalar(out=CB[:, r, :], in0=osb[:], scalar1=sel8[:, r:r + 1], scalar2=None, op0=ALU.mult), r), reads=["osb2"], writes=["CB"])
            P.op("sp", lambda e: e.dma_start(out=cc_a, in_=CB[:].rearrange("p a b -> p (a b)")), reads=["CB"], writes=["cc_a"], dma=True)
            P.cc(lambda e: e.collective_compute("AllReduce", ALU.add, replica_groups=QUADS, ins=[cc_a.opt()], outs=[cc_b.opt()]), reads=["cc_a"], writes=["cc_b"])
            P.cc(lambda e: e.collective_compute("AllReduce", ALU.add, replica_groups=P4, ins=[cc_b.opt()], outs=[cc_c.opt()]), reads=["cc_b"], writes=["cc_c"])
            P.op("sp", lambda e: e.dma_start(out=G[:].rearrange("p j q b -> p (j q b)"), in_=cc_c), reads=["cc_c"], writes=["G"], dma=True)
            for l in range(4):
                mv = MODALL[:, l, 0:96].rearrange("p (j f) -> p j f", j=8)
                for b in range(4):
                    if b == 0:
                        P.op("dve", L(lambda e, l, b, mv: e.tensor_scalar(out=mv, in0=G[:, :, l * 12:(l + 1) * 12, b], scalar1=selb[:, b:b + 1], scalar2=None, op0=ALU.mult), l, b, mv), reads=["G"], writes=[("modall", l)])
                    else:
                        P.op("dve", L(lambda e, l, b, mv: e.scalar_tensor_tensor(out=mv, in0=G[:, :, l * 12:(l + 1) * 12, b], scalar=selb[:, b:b + 1], in1=mv, op0=ALU.mult, op1=ALU.add), l, b, mv), reads=["G", ("modall", l)], writes=[("modall", l)])
            P.barrier()

            def modv(l, i0):
                return MODALL[:, l, i0 * 16:(i0 + 4) * 16].rearrange("p (i c) -> p i c", i=4)

            def exchange(src2d, dst2d, n, apply_flag=True):
                cx.reset()
                cx.uid += 1
                xs_src = itn(f"xs_src{cx.uid}", [128, n]); xs_dst = itn(f"xs_dst{cx.uid}", [128, n])
                step = 2048
                t = [cx.sb(f"xt{i}", [128, step]) for i in range(2)]
                for i, c0 in enumerate(range(0, n, step)):
                    w = min(step, n - c0)
                    q = i % 2
                    P.op("sp", L(lambda e, q, c0, w: e.dma_start(out=t[q][:, 0:w], in_=src2d[:, c0:c0 + w]), q, c0, w), writes=[("xt", q)], dma=True)
                    P.op("dve", L(lambda e, q, w: e.tensor_scalar(out=t[q][:, 0:w], in0=t[q][:, 0:w], scalar1=omf[:, 0:1], scalar2=None, op0=ALU.mult), q, w), reads=[("xt", q)], writes=[("xt", q)])
                    P.op("sp", L(lambda e, q, c0, w: e.dma_start(out=xs_src[:, c0:c0 + w], in_=t[q][:, 0:w]), q, c0, w), reads=[("xt", q)], writes=["xs_src"], dma=True)
                P.cc(lambda e: e.collective_compute("AllReduce", ALU.add, replica_groups=PAIRS, ins=[xs_src.opt()], outs=[xs_dst.opt()]), reads=["xs_src"], writes=["xs_dst"])
                for i, c0 in enumerate(range(0, n, step)):
                    w = min(step, n - c0)
                    q = i % 2
                    P.op("sp", L(lambda e, q, c0, w: e.dma_start(out=t[q][:, 0:w], in_=xs_dst[:, c0:c0 + w]), q, c0, w), reads=["xs_dst"], writes=[("xt", q)], dma=True)
                    if apply_flag:
                        P.op("dve", L(lambda e, q, w: e.tensor_scalar(out=t[q][:, 0:w], in0=t[q][:, 0:w], scalar1=flagsb[:, 0:1], scalar2=None, op0=ALU.mult), q, w), reads=[("xt", q)], writes=[("xt", q)])
                    P.op("sp", L(lambda e, q, c0, w: e.dma_start(out=dst2d[:, c0:c0 + w], in_=t[q][:, 0:w]), q, c0, w), reads=[("xt", q)], writes=["xdst"], dma=True)
                P.barrier()

            def halo(hbuf):
                src = hbuf[:, T:T + 2].rearrange("(c p) n -> p c n", p=128)
                dst = hbuf[:, 0:2].rearrange("(c p) n -> p c n", p=128)
                cx.reset()
                cx.uid += 1
                xs_src = itn(f"hx_src{cx.uid}", [128, 32]); xs_dst = itn(f"hx_dst{cx.uid}", [128, 32])
                t = cx.sb("ht", [128, 16, 2])
                P.op("sp", lambda e: e.dma_start(out=t[:], in_=src), writes=["ht"], dma=True)
                P.op("dve", lambda e: e.tensor_scalar(out=t[:], in0=t[:], scalar1=omf[:, 0:1], scalar2=None, op0=ALU.mult), reads=["ht"], writes=["ht"])
                P.op("sp", lambda e: e.dma_start(out=xs_src, in_=t[:].rearrange("p c n -> p (c n)")), reads=["ht"], writes=["xs_src"], dma=True)
                P.cc(lambda e: e.collective_compute("AllReduce", ALU.add, replica_groups=PAIRS, ins=[xs_src.opt()], outs=[xs_dst.opt()]), reads=["xs_src"], writes=["xs_dst"])
                P.op("sp", lambda e: e.dma_start(out=t[:].rearrange("p c n -> p (c n)"), in_=xs_dst), reads=["xs_dst"], writes=["ht"], dma=True)
                P.op("sp", lambda e: e.dma_start(out=dst, in_=t[:]), reads=["ht"], writes=["hdst"], dma=True)
                P.barrier()

            def run_ffn(l, hin, hout):
                halo(hin)
                cx.io = {"hT": hin, "mod": modv(l, 3), "flag": flagsb, "w_up": ext[f"ffn{l}_w_up"], "cw": ext[f"ffn{l}_cw"], "w_dn": ext[f"ffn{l}_w_dn"], "out": hout[:, 2:]}
                build_ffn()

            def run_sg(l, j, hin, hout):
                cx.io = {"hT": hin, "mod": modv(l, 0), "w_in": ext[f"sg{j}_w_in"], "lnp": ext[f"sg{j}_lnp"], "wsT": ext[f"sg{j}_wsT"], "tri": ext["tri"],
                         "bs": ext[f"sg{j}_bs"], "w_out": ext[f"sg{j}_w_out"], "out": hout[:, 2:]}
                build_sg()

            run_sg(0, 0, ext["x_hT"], H["0a"])
            run_ffn(0, H["0a"], H["0b"])
            rio = {"hT": H["0b"][:, 2:], "mod": modv(1, 0), "posb": ext["posb"], "invf": ext["invf"], "w_in": ext["ret_w_in"], "kdecA": ext["kdecA"], "kdec": ext["kdec"],
                   "ident": ext["ident"], "decT": ext["decT"], "qdec": ext["qdec"], "gnp": ext["gnp"], "s_in": s_in, "w_out": ext["ret_w_out"], "out": H["1a"][:, 2:], "s_out": s_loc}
            cx.io = dict(rio)
            build_ret("state")
            v2 = lambda a: a.rearrange("h d e -> (h d e)").rearrange("(p n) -> p n", p=128)
            exchange(v2(s_loc), v2(s_in), 8192)
            cx.io = dict(rio)
            build_ret("main")
            run_ffn(1, H["1a"], H["1b"])
            halo(H["1b"])
            aio = {"hT": H["1b"][:, 1:], "mod": modv(2, 0), "flag": flagsb, "mu": ext["mu"], "w_rkv": ext["w_rkv"], "w1": ext["w1"], "w2": ext["w2"], "a1": ext["a1"], "a2": ext["a2"],
                   "g1": ext["g1"], "g2": ext["g2"], "vecs": ext["vecsA"], "bo": ext["bo"], "ident": ext["ident"]}
            aio.update(rw); aio.update(rwt)
            cx.io = aio
            build_rwa(want_tm=True)
            cx.reset()
            zt = cx.sb("zt", [128, 1024])
            P.op("dve", lambda e: e.memset(zt[:], 0.0), writes=["zt"])
            P.op("sp", lambda e: e.dma_start(out=sc_zero.rearrange("a h k v -> (a h k v)").rearrange("(p n) -> p n", p=128), in_=zt[:]), reads=["zt"], writes=["sc_zero"], dma=True)
            P.barrier()
            for ps_ in (1, 2):
                for hg in range(2):
                    cs_ = slice(hg * CH, (hg + 1) * CH)
                    sio = {"lw_t": rwt["lw_t"][:, cs_], "kk_t": rwt["kk_t"][:, cs_], "a_t": rwt["a_t"][:, cs_], "k_t": rwt["k_t"][:, cs_], "v_t": rwt["v_t"][:, cs_],
                           "lw_f": rw["lw_o"][cs_, :], "kk_f": rw["kk_o"][cs_, :], "a_f": rw["a_o"][cs_, :], "k_f": rw["k_o"][cs_, :], "r_f": rw["r_o"][cs_, :],
                           "ident": ext["ident"], "tri_incl": ext["tri_incl"], "low_strict": ext["low_strict"], "mask2": ext["mask2"], "ones": ext["ones"],
                           "s0": (sc_zero if ps_ == 1 else sc_in)[hg], "s_out": (sc_loc if ps_ == 1 else sc_dump)[hg], "o_out": oT[cs_, :]}
                    cx.io = sio
                    build_scan(T, o_fm=True, want_o=(ps_ == 2))
                if ps_ == 1:
                    v3 = lambda a: a.rearrange("a h k v -> (a h k v)").rearrange("(p n) -> p n", p=128)
                    exchange(v3(sc_loc), v3(sc_in), 1024)
            cx.io = {"hT": H["1b"][:, 2:], "mod": modv(2, 0), "o_i": oT, "r_i": rw["r_o"], "k_i": rw["k_o"], "v_i": rw["v_o"], "g_i": rw["g_o"], "vecs": ext["vecsC"], "bo": ext["bo"],
                     "w_out": ext["rw_w_out"], "out": H["2a"][:, 2:]}
            build_rwc()
            run_ffn(2, H["2a"], H["2b"])
            run_sg(3, 1, H["2b"][:, 2:], H["3a"])
            run_ffn(3, H["3a"], H["3b"])
            cx.io = {"hT": H["3b"][:, 2:], "mod": ext["fm"], "out": out}
            build_final()
        finally:
            CX[0] = None
        P.emit()
    return nc
_FCACHE = {}
_DEBUG = None


def _pp(v):
    return np.ascontiguousarray(np.asarray(v).reshape(16, 128).T)


def fused_maps(inp):
    x = inp["x"]
    rc, cd = ret_consts()
    sc = scan_consts()
    bo = blockones()
    shared = {}
    for l in range(4):
        cw = np.zeros((128, 86, 4), np.float32)
        cw[:, :, 0:3] = inp["ffn_conv_w"][l].T.reshape(86, 128, 3).transpose(1, 0, 2)
        cw[:, :, 3] = inp["ffn_conv_b"][l].reshape(86, 128).T
        shared[f"ffn{l}_w_up"] = inp["ffn_w_up"][l]; shared[f"ffn{l}_cw"] = cw; shared[f"ffn{l}_w_dn"] = inp["ffn_w_down"][l]
    for j in range(2):
        shared[f"sg{j}_w_in"] = inp["sg_w_in"][j]
        shared[f"sg{j}_lnp"] = np.ascontiguousarray(np.broadcast_to(np.stack([inp["sg_ln_g"][j], inp["sg_ln_b"][j]])[None], (128, 2, D))).astype(np.float32)
        shared[f"sg{j}_wsT"] = np.ascontiguousarray(inp["sg_w_s"][j].transpose(2, 0, 1))
        shared[f"sg{j}_bs"] = np.ascontiguousarray(inp["sg_b_s"][j].reshape(1, D))
        shared[f"sg{j}_w_out"] = inp["sg_w_out"][j]
    shared["tri"] = np.triu(np.ones((128, 128), np.float32))
    shared.update({"invf": rc["invf"], "ret_w_in": inp["ret_w_in"][0], "kdecA": rc["kdecA"], "kdec": rc["kdec"], "ident": rc["ident"], "decT": rc["decT"], "qdec": rc["qdec"],
                   "gnp": np.ascontiguousarray(np.stack([inp["ret_gn_g"][0].reshape(32, 128).T, inp["ret_gn_b"][0].reshape(32, 128).T], -1)).astype(np.float32),
                   "ret_w_out": inp["ret_w_out"][0]})
    shared.update({"mu": np.ascontiguousarray(inp["rwkv_mu"][0].reshape(6, 16, 128).transpose(2, 0, 1)), "w_rkv": inp["rwkv_w_rkv"][0], "w1": inp["rwkv_w1"][0], "w2": inp["rwkv_w2"][0],
                   "a1": inp["rwkv_a1"][0], "a2": inp["rwkv_a2"][0], "g1": inp["rwkv_g1"][0], "g2": inp["rwkv_g2"][0],
                   "vecsA": np.ascontiguousarray(np.stack([_pp(inp["rwkv_w0"][0]), _pp(inp["rwkv_a0"][0]), _pp(inp["rwkv_k_k"][0]), _pp(inp["rwkv_k_a"][0])], 1)),
                   "bo": bo,
                   "vecsC": np.ascontiguousarray(np.stack([_pp(inp["rwkv_ln_g"][0]), _pp(inp["rwkv_ln_b"][0]), _pp(inp["rwkv_r_k"][0].reshape(D))], 1)),
                   "rw_w_out": inp["rwkv_w_out"][0]})
    shared.update(sc)
    fm = np.zeros((128, 4, 16), np.float32)
    fm[:, 1, :] = _pp(inp["final_norm_g"])
    shared["fm"] = fm
    shared["cT"] = np.ascontiguousarray(inp["c"].T.reshape(16, 128, 4).transpose(1, 0, 2)).astype(np.float32)
    maps = []
    for i in range(8):
        b, p = i // 2, i % 2
        m = dict(shared)
        m["x_hT"] = np.ascontiguousarray(x[b].T[:, p * T:(p + 1) * T])
        m["flag"] = np.full((128, 1), float(p), np.float32)
        sb_ = np.zeros((128, 4), np.float32); sb_[:, b] = 1.0
        s8 = np.zeros((128, 8), np.float32); s8[:, i] = 1.0
        m["selb"] = sb_; m["sel8"] = s8
        m["aw"] = np.ascontiguousarray(inp["ada_w"][:, :, i * NCOLA:(i + 1) * NCOLA])
        m["ab"] = np.ascontiguousarray(inp["ada_b"][:, i * NCOLA:(i + 1) * NCOLA].reshape(4, 12, 128).transpose(2, 0, 1).reshape(128, 48))
        m["posb"] = np.ascontiguousarray(np.broadcast_to(inp["positions"][b, p * T:(p + 1) * T][None], (128, T))).astype(np.int32)
        maps.append(m)
    return maps


def kernel(**inp):
    inp = {k: np.asarray(v) for k, v in inp.items()}
    dbg = _DEBUG is not None
    if dbg not in _FCACHE:
        _FCACHE[dbg] = build_fused(debug=dbg)
    nc = _FCACHE[dbg]
    maps = fused_maps(inp)
    res = run_bass_kernel_spmd(nc, maps, core_ids=list(range(8))).results
    if dbg:
        for st in ["0a", "0b", "1a", "1b", "2a", "2b", "3a", "3b"]:
            hT = np.zeros((4, D, 2 * T), np.float32)
            for i in range(8):
                hT[i // 2][:, (i % 2) * T:(i % 2 + 1) * T] = res[i]["h" + st][:, 2:]
            _DEBUG((int(st[0]), st[1]), hT)
    oT = np.zeros((4, D, 2 * T), np.float32)
    for i in range(8):
        oT[i // 2][:, (i % 2) * T:(i % 2 + 1) * T] = res[i]["out"]
    return np.ascontiguousarray(oT.transpose(0, 2, 1)).astype(np.float32)
```

```python
import math
import numpy as np
import concourse.bass as bass
import concourse.mybir as mybir
from concourse.bass_utils import run_bass_kernel_spmd
from contextlib import ExitStack

F32 = mybir.dt.float32
BF16 = mybir.dt.bfloat16
I32 = mybir.dt.int32
AF = mybir.ActivationFunctionType
ALU = mybir.AluOpType
AX = mybir.AxisListType

COMPUTE = ("pe", "act", "dve", "pool")
SEM_EPOCH = 20000


class Op:
    __slots__ = ("eng", "fn", "is_dma", "signal", "waits", "sem", "val", "idx", "deps", "is_cc")

    def __init__(self, eng, fn, is_dma):
        self.eng = eng
        self.fn = fn
        self.is_dma = is_dma
        self.signal = False
        self.waits = []
        self.sem = None
        self.val = None
        self.deps = []
        self.is_cc = False


class Prog:
    def __init__(self, nc, n_dma_sems=8):
        self.nc = nc
        self.ops = []
        self.last_writer = {}
        self.readers = {}
        self.n_dma_sems = n_dma_sems
        self.dma_count = {"sp": 0, "act": 0, "pool": 0}
        self.dma_hist = {"sp": [], "act": [], "pool": []}
        self.out_dmas = []
        self.cc_hist = []

    def op(self, eng, fn, reads=(), writes=(), dma=False, is_output=False, cc=False):
        o = Op(eng, fn, dma)
        o.is_cc = cc
        deps = []
        for k in reads:
            lw = self.last_writer.get(k)
            if lw is not None:
                deps.append((lw, "raw"))
        for k in writes:
            lw = self.last_writer.get(k)
            if lw is not None:
                deps.append((lw, "waw"))
            for r in self.readers.get(k, ()):
                deps.append((r, "war"))
        for k in reads:
            self.readers.setdefault(k, []).append(o)
        for k in writes:
            self.last_writer[k] = o
            self.readers[k] = []
        seen = set()
        for d, kind in deps:
            if d is o or id(d) in seen:
                continue
            if (not d.is_dma) and (not o.is_dma) and d.eng == o.eng:
                if d.eng == "pe":
                    continue
                if kind != "raw":
                    if not any((dd is d and kk == "raw") for dd, kk in deps):
                        continue
            seen.add(id(d))
            o.deps.append(d)
            d.signal = True
        if cc:
            o.signal = True
            self.cc_hist.append(o)
        elif dma:
            h = self.dma_hist[eng]
            j = len(h)
            o.idx = j
            if j >= self.n_dma_sems:
                prev = h[j - self.n_dma_sems]
                if id(prev) not in seen:
                    o.deps.append(prev)
                    prev.signal = True
            h.append(o)
            o.signal = True
            if is_output:
                self.out_dmas.append(o)
        self.ops.append(o)
        return o

    def cc(self, fn, reads=(), writes=()):
        return self.op("pool", fn, reads=reads, writes=writes, dma=True, cc=True)

    def barrier(self):
        lasts = {}
        for o in self.ops:
            if not o.is_dma and o.fn is not None:
                lasts[o.eng] = o
        dm = []
        for q in ("sp", "act", "pool"):
            dm += self.dma_hist[q][-self.n_dma_sems:]
        dm += self.cc_hist[-4:]
        for e in ("pe", "act", "dve", "pool", "sp"):
            b = Op(e, None, False)
            for x, lo in lasts.items():
                if x != e:
                    b.deps.append(lo)
                    lo.signal = True
            for d in dm:
                b.deps.append(d)
            self.ops.append(b)
        self.last_writer = {}
        self.readers = {}
        self.out_dmas = []

    def emit(self):
        nc = self.nc
        with ExitStack() as es:
            n_sig = {e: sum(1 for o in self.ops if o.eng == e and not o.is_dma and o.signal) for e in COMPUTE}
            csems = {}
            for e in COMPUTE:
                ne = max(1, (n_sig[e] + SEM_EPOCH - 1) // SEM_EPOCH)
                csems[e] = [es.enter_context(nc.semaphore(f"c_{e}_{i}")) for i in range(ne)]
            dsems = {}
            for q in ("sp", "act", "pool"):
                if self.dma_hist[q]:
                    dsems[q] = [es.enter_context(nc.semaphore(f"d_{q}_{i}")) for i in range(self.n_dma_sems)]
            cnt = {e: 0 for e in COMPUTE}
            streams_cc = []
            for o in self.ops:
                if o.is_cc:
                    o.sem = es.enter_context(nc.semaphore(f"cc_{len(streams_cc)}"))
                    streams_cc.append(o)
                    o.val = 1
                elif o.is_dma:
                    o.sem = dsems[o.eng][o.idx % self.n_dma_sems]
                    o.val = 16 * (o.idx // self.n_dma_sems + 1)
                elif o.signal:
                    c = cnt[o.eng]
                    o.sem = csems[o.eng][c // SEM_EPOCH]
                    o.val = c % SEM_EPOCH + 1
                    cnt[o.eng] = c + 1
            streams = {e: [] for e in ("pe", "act", "dve", "pool", "sp")}
            for o in self.ops:
                streams[o.eng].append(o)
            block = es.enter_context(nc.Block())

            def run(eng_name, eng):
                waited = {}
                for o in streams[eng_name]:
                    for d in o.deps:
                        key = id(d.sem)
                        if waited.get(key, 0) >= d.val:
                            continue
                        eng.wait_ge(d.sem, d.val)
                        waited[key] = d.val
                    if o.fn is None:
                        continue
                    ins = o.fn(eng)
                    if o.signal:
                        ins.then_inc(o.sem, 1 if o.is_cc else (16 if o.is_dma else 1))
                for d in self.out_dmas:
                    if d.eng == eng_name:
                        eng.wait_ge(d.sem, d.val)

            @block.tensor
            def _(e):
                run("pe", e)

            @block.scalar
            def _(e):
                run("act", e)

            @block.vector
            def _(e):
                run("dve", e)

            @block.gpsimd
            def _(e):
                run("pool", e)

            @block.sync
            def _(e):
                run("sp", e)


CX = [None]


class Ctx:
    def __init__(self, nc, P, arena, nwords, pss_all):
        self.nc, self.P, self.arena, self.nwords, self.pss_all = nc, P, arena, nwords, pss_all
        self.base = 0
        self.off = 0
        self.io = {}
        self.uid = 0

    def reset(self):
        self.off = self.base

    def sb(self, name, shape, dt=F32):
        n = 1
        for d in shape[1:]:
            n *= d
        esz = mybir.dt.size(dt)
        words = (n * esz + 3) // 4
        words = (words + 15) // 16 * 16
        assert self.off + words <= self.nwords, f"SBUF arena overflow at {name}: {self.off}+{words} > {self.nwords}"
        v = self.arena[0:shape[0], self.off:self.off + words]
        self.off += words
        if dt != F32:
            v = v.bitcast(dt)
        v = v[:, 0:n]
        if len(shape) == 3:
            v = v.rearrange("p (a b) -> p a b", a=shape[1])
        elif len(shape) == 4:
            v = v.rearrange("p (a b c) -> p a b c", a=shape[1], b=shape[2])
        return v


def mk_nc():
    return CX[0].nc if CX[0] else bass.Bass("TRN2", target_bir_lowering=False)


def dram_in(nc, name, shape, dt):
    if CX[0]:
        return CX[0].io[name]
    return nc.dram_tensor(name, shape, dt, kind="ExternalInput").ap()


def dram_out(nc, name, shape, dt):
    if CX[0]:
        return CX[0].io[name]
    return nc.dram_tensor(name, shape, dt, kind="ExternalOutput").ap()


def dram_tmp(nc, name, shape, dt):
    if CX[0]:
        CX[0].uid += 1
        return nc.dram_tensor(f"{name}_{CX[0].uid}", shape, dt).ap()
    return nc.dram_tensor(name, shape, dt).ap()


def mk_sb(nc, es):
    if CX[0]:
        CX[0].reset()
        return CX[0].sb
    return lambda name, shape, dt=F32: es.enter_context(nc.sbuf_tensor(name, shape, dt))


def mk_pss(nc, es, n):
    if CX[0]:
        return CX[0].pss_all[0:n]
    return [es.enter_context(nc.psum_tensor(f"ps{i}", [128, 512], F32)) for i in range(n)]


def mk_pst(nc, es):
    if CX[0]:
        return [CX[0].pss_all[i][:, :].bitcast(BF16).rearrange("p (a b) -> p a b", a=8) for i in (6, 7)]
    return [es.enter_context(nc.psum_tensor(f"pst{i}", [128, 8, 128], BF16)) for i in range(2)]


def mk_prog(nc):
    return CX[0].P if CX[0] else Prog(nc)


def finish(P):
    if CX[0]:
        P.barrier()
    else:
        P.emit()

D = 2048
DFF = 5504
NP = 43
T = 1024
HALO = 2
EPS = 1e-6


def L(f, *a):
    return lambda e: f(e, *a)


def emit_norm_mod(P, nc, es, hT, XM, modsb, i_sh, i_sc1, ncols, col0_dram, stage, ones, pss, tmp_keys, flag=None, halo=0):
    rstd = tmp_keys["rstd"]
    sq = tmp_keys["sq"]
    xn = tmp_keys["xn"]
    ps = pss[0]
    b0 = 0
    blk = 0
    while b0 < ncols:
        nb = min(256, ncols - b0)
        if ncols - b0 - nb == 1:
            nb -= 1
        P.op("sp", L(lambda e, b0, nb: e.dma_start(out=stage[:, :, 0:nb], in_=hT[:, col0_dram + b0: col0_dram + b0 + nb].rearrange("(c p) t -> p c t", p=128)), b0, nb),
             writes=["stage"], dma=True)
        for c in range(16):
            P.op("act", L(lambda e, c, nb: e.activation(out=sq[:, c % 2, 0:nb], in_=stage[:, c, 0:nb], func=AF.Square), c, nb),
                 reads=["stage"], writes=[("sq", c % 2)])
            P.op("pe", L(lambda e, c, nb: e.matmul(ps[:, 0:nb], ones[:], sq[:, c % 2, 0:nb], start=(c == 0), stop=(c == 15)), c, nb),
                 reads=[("sq", c % 2), "ones"], writes=[("ps", 0)])
        P.op("act", L(lambda e, nb: e.activation(out=rstd[:, 0:nb], in_=ps[:, 0:nb], func=AF.Sqrt, scale=1.0 / D, bias=tmp_keys["eps"][:, 0:1]), nb),
             reads=[("ps", 0), "eps"], writes=["rstd"])
        P.op("dve", L(lambda e, nb: e.reciprocal(out=rstd[:, 0:nb], in_=rstd[:, 0:nb]), nb), reads=["rstd"], writes=["rstd"])
        for c in range(16):
            P.op("dve", L(lambda e, c, nb: e.tensor_tensor(out=xn[:, c % 2, 0:nb], in0=stage[:, c, 0:nb], in1=rstd[:, 0:nb], op=ALU.mult), c, nb),
                 reads=["stage", "rstd"], writes=[("xn", c % 2)])
            P.op("act", L(lambda e, c, nb, b0: e.activation(out=XM[:, c, b0:b0 + nb], in_=xn[:, c % 2, 0:nb], func=AF.Identity,
                                                          scale=modsb[:, i_sc1, c:c + 1], bias=modsb[:, i_sh, c:c + 1]), c, nb, b0),
                 reads=[("xn", c % 2), "mod"], writes=[("xm", c)])
        b0 += nb
        blk += 1
    if flag is not None and halo > 0:
        for c in range(16):
            P.op("dve", L(lambda e, c: e.tensor_scalar(out=XM[:, c, 0:halo], in0=XM[:, c, 0:halo], scalar1=flag[:, 0:1], scalar2=None, op0=ALU.mult), c),
                 reads=[("xm", c), "flag"], writes=[("xm", c)])


def build_ffn():
    nc = mk_nc()
    hT = dram_in(nc, "hT", [D, T + HALO], F32)
    mod = dram_in(nc, "mod", [128, 4, 16], F32)
    flag = dram_in(nc, "flag", [128, 1], F32)
    w_up = dram_in(nc, "w_up", [D, 2 * DFF], F32)
    cw = dram_in(nc, "cw", [128, 86, 4], F32)
    w_dn = dram_in(nc, "w_dn", [DFF, D], F32)
    out = dram_out(nc, "out", [D, T], F32)
    with ExitStack() as es:
        sb = mk_sb(nc, es)
        XM = sb("XM", [128, 16, T + HALO], BF16)
        A = sb("A", [128, NP, T], BF16)
        W = [sb(f"W{i}", [128, 16 * 512], BF16) for i in range(2)]
        stage = sb("stage", [128, 16, 264], F32)
        modsb = sb("modsb", [128, 4, 16], F32)
        flagsb = sb("flagsb", [128, 1], F32)
        cwsb = sb("cwsb", [128, 86, 4], F32)
        ones = sb("ones", [128, 128], BF16)
        epsb = sb("epsb", [128, 1], F32)
        rstd = sb("rstd", [128, 512], F32)
        sq = sb("sq", [128, 2, 512], BF16)
        xn = sb("xn", [128, 2, 512], F32)
        hres = sb("hres", [128, 2, 512], F32)
        hout = sb("hout", [128, 2, 512], F32)
        pss = mk_pss(nc, es, 8)
        P = mk_prog(nc)
        P.op("sp", lambda e: e.dma_start(out=modsb[:], in_=mod), writes=["mod"], dma=True)
        P.op("sp", lambda e: e.dma_start(out=flagsb[:], in_=flag), writes=["flag"], dma=True)
        P.op("sp", lambda e: e.dma_start(out=cwsb[:], in_=cw), writes=["cw"], dma=True)
        P.op("dve", lambda e: e.memset(ones[:], 1.0), writes=["ones"])
        P.op("dve", lambda e: e.memset(epsb[:], EPS), writes=["eps"])
        P.op("dve", lambda e: e.tensor_scalar(out=modsb[:, 1, :], in0=modsb[:, 1, :], scalar1=1.0, scalar2=None, op0=ALU.add), reads=["mod"], writes=["mod"])
        tk = {"rstd": rstd, "sq": sq, "xn": xn, "eps": epsb}
        emit_norm_mod(P, nc, es, hT, XM, modsb, 0, 1, T + HALO, 0, stage, ones, pss, tk, flag=flagsb, halo=HALO)

        stg = stage[:].rearrange("p c t -> p (c t)")
        HUP = lambda vg, b: stg[:, vg * 1032:vg * 1032 + T + HALO]
        ACC = lambda vg, b: stg[:, 2064 + vg * 1024: 2064 + (vg + 1) * 1024]
        tiles = [(i, min(2, NP - i)) for i in range(0, NP, 2)]
        for ti, (p0, npair) in enumerate(tiles):
            wb = ti % 2
            Wt = W[wb][:, 0:16 * 512].rearrange("p (k n) -> p k n", k=16)
            ncol = npair * 128
            P.op("pool", L(lambda e, Wt, p0, ncol: e.dma_start(out=Wt[:, :, 0:ncol], in_=w_up[:, p0 * 128:p0 * 128 + ncol].rearrange("(k p) n -> p k n", p=128)), Wt, p0, ncol),
                 reads=[], writes=[("W", wb)], dma=True)
            P.op("pool", L(lambda e, Wt, p0, ncol: e.dma_start(out=Wt[:, :, 256:256 + ncol], in_=w_up[:, DFF + p0 * 128:DFF + p0 * 128 + ncol].rearrange("(k p) n -> p k n", p=128)), Wt, p0, ncol),
                 reads=[], writes=[("W", wb)], dma=True)
            for j in range(npair):
                pi = p0 + j
                hb = 0
                for vg in range(2):
                    col = vg * 256 + j * 128
                    chunk = pi + vg * NP
                    bank_h = pss[1 + vg]
                    banks = [pss[3 + vg * 2], pss[4 + vg * 2]]
                    for k in range(16):
                        P.op("pe", L(lambda e, Wt, k, col, bank_h: e.matmul(bank_h[:, 0:HALO], Wt[:, k, col:col + 128], XM[:, k, 0:HALO], start=(k == 0), stop=(k == 15)), Wt, k, col, bank_h),
                             reads=[("W", wb), ("xm", k)], writes=[("ps", 1 + vg)])
                    for hb2 in range(2):
                        for k in range(16):
                            P.op("pe", L(lambda e, Wt, k, col, bk, hb2: e.matmul(bk[:, :], Wt[:, k, col:col + 128], XM[:, k, HALO + hb2 * 512: HALO + (hb2 + 1) * 512], start=(k == 0), stop=(k == 15)), Wt, k, col, banks[hb2], hb2),
                                 reads=[("W", wb), ("xm", k)], writes=[("ps", 3 + vg * 2 + hb2)])
                    hup = HUP(vg, hb)
                    acc = ACC(vg, hb)
                    P.op("act", L(lambda e, hup, bank_h: e.copy(out=hup[:, 0:HALO], in_=bank_h[:, 0:HALO]), hup, bank_h),
                         reads=[("ps", 1 + vg)], writes=[("hup", vg, hb)])
                    for hb2 in range(2):
                        P.op("act", L(lambda e, hup, bk, hb2: e.copy(out=hup[:, HALO + hb2 * 512:HALO + (hb2 + 1) * 512], in_=bk[:, :]), hup, banks[hb2], hb2),
                             reads=[("ps", 3 + vg * 2 + hb2)], writes=[("hup", vg, hb)])
                    eng = "dve"
                    P.op(eng, L(lambda e, acc, hup, chunk: e.tensor_scalar(out=acc, in0=hup[:, 2:2 + T], scalar1=cwsb[:, chunk, 2:3], scalar2=cwsb[:, chunk, 3:4], op0=ALU.mult, op1=ALU.add), acc, hup, chunk),
                         reads=[("hup", vg, hb), "cw"], writes=[("acc", vg, hb)])
                    P.op(eng, L(lambda e, acc, hup, chunk: e.scalar_tensor_tensor(out=acc, in0=hup[:, 1:1 + T], scalar=cwsb[:, chunk, 1:2], in1=acc, op0=ALU.mult, op1=ALU.add), acc, hup, chunk),
                         reads=[("hup", vg, hb), "cw", ("acc", vg, hb)], writes=[("acc", vg, hb)])
                    P.op(eng, L(lambda e, acc, hup, chunk: e.scalar_tensor_tensor(out=acc, in0=hup[:, 0:T], scalar=cwsb[:, chunk, 0:1], in1=acc, op0=ALU.mult, op1=ALU.add), acc, hup, chunk),
                         reads=[("hup", vg, hb), "cw", ("acc", vg, hb)], writes=[("acc", vg, hb)])
                accv, accg = ACC(0, hb), ACC(1, hb)
                hupg = HUP(1, hb)
                P.op("act", L(lambda e, hupg, accg: e.activation(out=hupg[:, 0:T], in_=accg, func=AF.Silu), hupg, accg),
                     reads=[("acc", 1, hb)], writes=[("hup", 1, hb)])
                P.op("dve", L(lambda e, pi, hupg, accv: e.tensor_tensor(out=A[:, pi, :], in0=hupg[:, 0:T], in1=accv, op=ALU.mult), pi, hupg, accv),
                     reads=[("hup", 1, hb), ("acc", 0, hb)], writes=[("A", pi)])

        for di in range(16):
            wb = (len(tiles) + di) % 2
            Wt = W[wb][:, 0:NP * 128].rearrange("p (k n) -> p k n", k=NP)
            P.op("pool", L(lambda e, Wt, di: e.dma_start(out=Wt[:, :, :], in_=w_dn[:, di * 128:(di + 1) * 128].rearrange("(k p) n -> p k n", p=128)), Wt, di),
                 writes=[("W", wb)], dma=True)
            m = di
            for hb2 in range(2):
                q = (m * 2 + hb2) % 2
                bi = 1 + (m * 2 + hb2) % 6
                bank = pss[bi]
                P.op("sp", L(lambda e, m, hb2, q: e.dma_start(out=hres[:, q, :], in_=hT[m * 128:(m + 1) * 128, HALO + hb2 * 512:HALO + (hb2 + 1) * 512]), m, hb2, q),
                     writes=[("hres", q)], dma=True)
                for k in range(NP):
                    P.op("pe", L(lambda e, Wt, k, hb2, bank: e.matmul(bank[:, :], Wt[:, k, :], A[:, k, hb2 * 512:(hb2 + 1) * 512], start=(k == 0), stop=(k == NP - 1)), Wt, k, hb2, bank),
                         reads=[("W", wb), ("A", k)], writes=[("ps", bi)])
                P.op("dve", L(lambda e, m, q, bank: e.scalar_tensor_tensor(out=hout[:, q, :], in0=bank[:, :], scalar=modsb[:, 2, m:m + 1], in1=hres[:, q, :], op0=ALU.mult, op1=ALU.add), m, q, bank),
                     reads=[("ps", bi), ("hres", q), "mod"], writes=[("hout", q)])
                P.op("sp", L(lambda e, m, hb2, q: e.dma_start(out=out[m * 128:(m + 1) * 128, hb2 * 512:(hb2 + 1) * 512], in_=hout[:, q, :]), m, hb2, q),
                     reads=[("hout", q)], dma=True, is_output=True)
        finish(P)
    return nc

T = 1024
LN_EPS = 1e-5
GELU = AF.Gelu_apprx_tanh


def build_sg():
    nc = mk_nc()
    hT = dram_in(nc, "hT", [D, T], F32)
    mod = dram_in(nc, "mod", [128, 4, 16], F32)
    w_in = dram_in(nc, "w_in", [D, 2 * D], F32)
    lnp = dram_in(nc, "lnp", [128, 2, D], F32)
    wsT = dram_in(nc, "wsT", [128, 16, 128], F32)
    tri = dram_in(nc, "tri", [128, 128], F32)
    bs = dram_in(nc, "bs", [1, D], F32)
    w_out = dram_in(nc, "w_out", [D, D], F32)
    out = dram_out(nc, "out", [D, T], F32)
    with ExitStack() as es:
        sb = mk_sb(nc, es)
        XM = sb("XM", [128, 16, T], BF16)
        U = sb("U", [128, 16, T], BF16)
        V = sb("V", [128, 8, D], BF16)
        W = [sb(f"W{i}", [128, 16, 512], BF16) for i in range(2)]
        stage = sb("stage", [128, 16, 256], F32)
        modsb = sb("modsb", [128, 4, 16], F32)
        lnsb = sb("lnsb", [128, 2, D], F32)
        wsf = sb("wsf", [128, 16, 128], F32)
        wsb = sb("wsb", [128, 16, 128], BF16)
        trisb = sb("trisb", [128, 128], F32)
        bssb = sb("bssb", [1, D], F32)
        ones = sb("ones", [128, 128], F32)
        epsb = sb("epsb", [128, 2], F32)
        rstd = sb("rstd", [128, 512], F32)
        sq = sb("sq", [128, 2, 512], BF16)
        onesb = sb("onesb", [128, 128], BF16)
        xn = sb("xn", [128, 2, 512], F32)
        hres = sb("hres", [128, 2, 512], F32)
        hout = sb("hout", [128, 2, 512], F32)
        stats = sb("stats", [128, 8, 4, 6], F32)
        mv = sb("mv", [128, 8, 4], F32)
        pss = mk_pss(nc, es, 8)
        P = mk_prog(nc)
        P.op("sp", lambda e: e.dma_start(out=modsb[:], in_=mod), writes=["mod"], dma=True)
        P.op("sp", lambda e: e.dma_start(out=lnsb[:], in_=lnp), writes=["ln"], dma=True)
        P.op("sp", lambda e: e.dma_start(out=wsf[:], in_=wsT), writes=["wsf"], dma=True)
        P.op("sp", lambda e: e.dma_start(out=trisb[:], in_=tri), writes=["tri"], dma=True)
        P.op("sp", lambda e: e.dma_start(out=bssb[:], in_=bs), writes=["bs"], dma=True)
        P.op("dve", lambda e: e.memset(ones[:], 1.0), writes=["ones"])
        P.op("dve", lambda e: e.memset(onesb[:], 1.0), writes=["ones"])
        P.op("dve", lambda e: e.memset(epsb[:, 0:1], EPS), writes=["eps"])
        P.op("dve", lambda e: e.memset(epsb[:, 1:2], LN_EPS), writes=["eps"])
        P.op("dve", lambda e: e.tensor_scalar(out=modsb[:, 1, :], in0=modsb[:, 1, :], scalar1=1.0, scalar2=None, op0=ALU.add), reads=["mod"], writes=["mod"])
        for g in range(16):
            P.op("dve", L(lambda e, g: e.tensor_tensor(out=wsb[:, g, :], in0=wsf[:, g, :], in1=trisb[:], op=ALU.mult), g),
                 reads=["wsf", "tri"], writes=["wsb"])
        tk = {"rstd": rstd, "sq": sq, "xn": xn, "eps": epsb}
        emit_norm_mod(P, nc, es, hT, XM, modsb, 0, 1, T, 0, stage, onesb, pss, tk)

        wcount = [0]

        def load_w(src_ap):
            wb = wcount[0] % 2
            wcount[0] += 1
            P.op("pool", L(lambda e, wb, src_ap: e.dma_start(out=W[wb][:], in_=src_ap.rearrange("(k p) n -> p k n", p=128)), wb, src_ap),
                 writes=[("W", wb)], dma=True)
            return wb

        bankc = [0]

        def next_bank():
            b = 1 + bankc[0] % 7
            bankc[0] += 1
            return b

        for ti in range(4):
            wb = load_w(w_in[:, ti * 512:(ti + 1) * 512])
            for mm in range(4):
                m = ti * 4 + mm
                for hb in range(2):
                    bi = next_bank()
                    for k in range(16):
                        P.op("pe", L(lambda e, wb, k, mm, hb, bi: e.matmul(pss[bi][:, :], W[wb][:, k, mm * 128:(mm + 1) * 128], XM[:, k, hb * 512:(hb + 1) * 512], start=(k == 0), stop=(k == 15)), wb, k, mm, hb, bi),
                             reads=[("W", wb), ("xm", k)], writes=[("ps", bi)])
                    P.op("act", L(lambda e, m, hb, bi: e.activation(out=U[:, m, hb * 512:(hb + 1) * 512], in_=pss[bi][:, :], func=GELU), m, hb, bi),
                         reads=[("ps", bi)], writes=[("u", m)])
        for fb in range(4):
            wb = load_w(w_in[:, D + fb * 512:D + (fb + 1) * 512])
            for n in range(8):
                bi = next_bank()
                for k in range(16):
                    P.op("pe", L(lambda e, wb, k, n, bi: e.matmul(pss[bi][:, :], XM[:, k, n * 128:(n + 1) * 128], W[wb][:, k, :], start=(k == 0), stop=(k == 15)), wb, k, n, bi),
                         reads=[("W", wb), ("xm", k)], writes=[("ps", bi)])
                P.op("act", L(lambda e, n, fb, bi: e.activation(out=V[:, n, fb * 512:(fb + 1) * 512], in_=pss[bi][:, :], func=GELU), n, fb, bi),
                     reads=[("ps", bi)], writes=[("v", n)])
                P.op("dve", L(lambda e, n, fb: e.bn_stats(out=stats[:, n, fb, :], in_=V[:, n, fb * 512:(fb + 1) * 512]), n, fb),
                     reads=[("v", n)], writes=[("stats", n)])
        stg = stage[:].rearrange("p c t -> p (c t)")
        for n in range(8):
            tmp = stg[:, (n % 2) * D:(n % 2 + 1) * D]
            P.op("dve", L(lambda e, n: e.bn_aggr(out=mv[:, n, 0:2], in_=stats[:, n, :, :].rearrange("p a b -> p (a b)")), n),
                 reads=[("stats", n)], writes=[("mv", n)])
            P.op("act", L(lambda e, n: e.activation(out=mv[:, n, 2:3], in_=mv[:, n, 1:2], func=AF.Sqrt, bias=epsb[:, 1:2]), n),
                 reads=[("mv", n), "eps"], writes=[("mv2", n)])
            P.op("dve", L(lambda e, n: e.reciprocal(out=mv[:, n, 3:4], in_=mv[:, n, 2:3]), n), reads=[("mv2", n)], writes=[("mv3", n)])
            P.op("dve", L(lambda e, n, tmp: e.tensor_scalar(out=tmp, in0=V[:, n, :], scalar1=mv[:, n, 0:1], scalar2=mv[:, n, 3:4], op0=ALU.subtract, op1=ALU.mult), n, tmp),
                 reads=[("v", n), ("mv", n), ("mv3", n), "stage"], writes=[("vtmp", n % 2)])
            P.op("dve", L(lambda e, n, tmp: e.tensor_tensor(out=tmp, in0=tmp, in1=lnsb[:, 0, :], op=ALU.mult), n, tmp),
                 reads=[("vtmp", n % 2), "ln"], writes=[("vtmp", n % 2)])
            P.op("dve", L(lambda e, n, tmp: e.tensor_tensor(out=V[:, n, :], in0=tmp, in1=lnsb[:, 1, :], op=ALU.add), n, tmp),
                 reads=[("vtmp", n % 2), "ln"], writes=[("v", n)])
        for n in range(8):
            for gq in range(4):
                bi = next_bank()
                for j in range(4):
                    g = gq * 4 + j
                    P.op("pe", L(lambda e, n, g, j, bi: e.matmul(pss[bi][:, j * 128:(j + 1) * 128], V[:, n, g * 128:(g + 1) * 128], wsb[:, g, :], start=True, stop=False), n, g, j, bi),
                         reads=[("v", n), "wsb"], writes=[("ps", bi)])
                    P.op("pe", L(lambda e, g, j, bi: e.matmul(pss[bi][:, j * 128:(j + 1) * 128], ones[0:1, :], bssb[0:1, g * 128:(g + 1) * 128], start=False, stop=True), g, j, bi),
                         reads=["ones", "bs"], writes=[("ps", bi)])
                P.op("dve", L(lambda e, n, gq, bi: e.tensor_tensor(out=U[:, gq * 4:(gq + 1) * 4, n * 128:(n + 1) * 128], in0=pss[bi][:, :].rearrange("p (j t) -> p j t", j=4),
                                                                  in1=U[:, gq * 4:(gq + 1) * 4, n * 128:(n + 1) * 128], op=ALU.mult), n, gq, bi),
                     reads=[("ps", bi)] + [("u", gq * 4 + j) for j in range(4)], writes=[("u", gq * 4 + j) for j in range(4)])
        for ti in range(4):
            wb = load_w(w_out[:, ti * 512:(ti + 1) * 512])
            for mm in range(4):
                m = ti * 4 + mm
                for hb in range(2):
                    q = (m * 2 + hb) % 2
                    bi = next_bank()
                    P.op("sp", L(lambda e, m, hb, q: e.dma_start(out=hres[:, q, :], in_=hT[m * 128:(m + 1) * 128, hb * 512:(hb + 1) * 512]), m, hb, q),
                         writes=[("hres", q)], dma=True)
                    for k in range(16):
                        P.op("pe", L(lambda e, wb, k, mm, hb, bi: e.matmul(pss[bi][:, :], W[wb][:, k, mm * 128:(mm + 1) * 128], U[:, k, hb * 512:(hb + 1) * 512], start=(k == 0), stop=(k == 15)), wb, k, mm, hb, bi),
                             reads=[("W", wb), ("u", k)], writes=[("ps", bi)])
                    P.op("dve", L(lambda e, m, q, bi: e.scalar_tensor_tensor(out=hout[:, q, :], in0=pss[bi][:, :], scalar=modsb[:, 2, m:m + 1], in1=hres[:, q, :], op0=ALU.mult, op1=ALU.add), m, q, bi),
                         reads=[("ps", bi), ("hres", q), "mod"], writes=[("hout", q)])
                    P.op("sp", L(lambda e, m, hb, q: e.dma_start(out=out[m * 128:(m + 1) * 128, hb * 512:(hb + 1) * 512], in_=hout[:, q, :]), m, hb, q),
                         reads=[("hout", q)], dma=True, is_output=True)
        finish(P)
    return nc

TWO_PI = 2 * math.pi
C1 = 6.28125
C2 = TWO_PI - C1
PI_SAFE = 3.1415925


def emit_rope_tables(P, posi, invf, sin_t, cos_t, tmps, n, key="rope"):
    pf, ang, r, t = tmps[0], tmps[1], tmps[2], tmps[3]
    ki = tmps[4]
    K = lambda s: (key, s)
    P.op("dve", lambda e: e.tensor_copy(out=pf[:, 0:n], in_=posi[:, 0:n]), reads=[K("posi")], writes=[K("pf")])
    P.op("dve", lambda e: e.tensor_scalar(out=ang[:, 0:n], in0=pf[:, 0:n], scalar1=invf[:, 0:1], scalar2=None, op0=ALU.mult), reads=[K("pf"), K("invf")], writes=[K("ang")])
    P.op("dve", lambda e: e.tensor_scalar(out=t[:, 0:n], in0=ang[:, 0:n], scalar1=1.0 / TWO_PI, scalar2=None, op0=ALU.mult), reads=[K("ang")], writes=[K("t")])
    P.op("dve", lambda e: e.tensor_copy(out=ki[:, 0:n], in_=t[:, 0:n]), reads=[K("t")], writes=[K("ki")])
    P.op("dve", lambda e: e.tensor_copy(out=pf[:, 0:n], in_=ki[:, 0:n]), reads=[K("ki"), K("ang")], writes=[K("pf")])
    P.op("dve", lambda e: e.scalar_tensor_tensor(out=r[:, 0:n], in0=pf[:, 0:n], scalar=-C1, in1=ang[:, 0:n], op0=ALU.mult, op1=ALU.add), reads=[K("pf"), K("ang")], writes=[K("r")])
    P.op("dve", lambda e: e.scalar_tensor_tensor(out=r[:, 0:n], in0=pf[:, 0:n], scalar=-C2, in1=r[:, 0:n], op0=ALU.mult, op1=ALU.add), reads=[K("pf"), K("r")], writes=[K("r")])

    def wrap(x, kx):
        P.op("dve", lambda e: e.tensor_scalar(out=t[:, 0:n], in0=x[:, 0:n], scalar1=math.pi, scalar2=-TWO_PI, op0=ALU.is_gt, op1=ALU.mult), reads=[kx], writes=[K("t")])
        P.op("dve", lambda e: e.tensor_tensor(out=x[:, 0:n], in0=x[:, 0:n], in1=t[:, 0:n], op=ALU.add), reads=[kx, K("t")], writes=[kx])
        P.op("dve", lambda e: e.tensor_scalar(out=t[:, 0:n], in0=x[:, 0:n], scalar1=-math.pi, scalar2=TWO_PI, op0=ALU.is_lt, op1=ALU.mult), reads=[kx], writes=[K("t")])
        P.op("dve", lambda e: e.tensor_tensor(out=x[:, 0:n], in0=x[:, 0:n], in1=t[:, 0:n], op=ALU.add), reads=[kx, K("t")], writes=[kx])
        P.op("dve", lambda e: e.tensor_scalar(out=x[:, 0:n], in0=x[:, 0:n], scalar1=PI_SAFE, scalar2=-PI_SAFE, op0=ALU.min, op1=ALU.max), reads=[kx], writes=[kx])

    wrap(r, K("r"))
    P.op("act", lambda e: e.activation(out=sin_t[:, 0:n], in_=r[:, 0:n], func=AF.Sin), reads=[K("r")], writes=[K("sin")])
    P.op("dve", lambda e: e.tensor_scalar(out=ang[:, 0:n], in0=r[:, 0:n], scalar1=math.pi / 2, scalar2=None, op0=ALU.add), reads=[K("r"), K("ang")], writes=[K("ang")])
    wrap(ang, K("ang"))
    P.op("act", lambda e: e.activation(out=cos_t[:, 0:n], in_=ang[:, 0:n], func=AF.Sin), reads=[K("ang")], writes=[K("cos")])

T = 1024
NH = 8
GN_EPS = 1e-6


def ret_consts():
    lg = np.log1p(-np.exp2(-5.0 - np.arange(NH, dtype=np.float64)))
    idx = np.arange(128, dtype=np.float64)
    rel = idx[None, :] - idx[:, None]
    decT = np.where(rel >= 0, np.exp(lg[:, None, None] * np.maximum(rel, 0)), 0.0)
    decT = np.ascontiguousarray(decT.transpose(1, 0, 2)).astype(np.float32)
    qdec = np.exp(lg[:, None] * (idx + 1.0))
    qdec = np.ascontiguousarray(np.broadcast_to(qdec[None], (128, NH, 128))).astype(np.float32)
    kdec = np.exp(lg[None, :] * (127.0 - idx[:, None])).astype(np.float32)
    n = np.arange(8, dtype=np.float64)
    kdecA = np.exp(lg[None, :, None] * (1023.0 - (n[None, None, :] * 128 + idx[:, None, None]))).astype(np.float32)
    cd = [float(np.exp(lg[h] * 128.0)) for h in range(NH)]
    ident = np.eye(128, dtype=np.float32)
    invf = (10000.0 ** (-np.arange(128, dtype=np.float64) / 128)).astype(np.float32)[:, None]
    return dict(decT=decT, qdec=qdec, kdec=kdec, kdecA=kdecA, ident=ident, invf=invf), cd


class _Stop(Exception):
    pass


STOP = [0]


def chk(n):
    if STOP[0] == n:
        raise _Stop()


def build_ret(mode):
    _, cd = ret_consts()
    nc = mk_nc()
    dt_in = lambda name, shape, dt=F32: dram_in(nc, name, shape, dt)
    hT = dt_in("hT", [D, T])
    mod = dt_in("mod", [128, 4, 16])
    posb = dt_in("posb", [128, T], I32)
    invf = dt_in("invf", [128, 1])
    w_in = dt_in("w_in", [D, 6 * D])
    kdecA = dt_in("kdecA", [128, NH, 8])
    kdec = dt_in("kdec", [128, NH])
    ident = dt_in("ident", [128, 128])
    if mode == "main":
        decT = dt_in("decT", [128, NH, 128])
        qdec = dt_in("qdec", [128, NH, 128])
        gnp = dt_in("gnp", [128, 32, 2])
        s_in = dt_in("s_in", [NH, 256, 512])
        w_out = dt_in("w_out", [2 * D, D])
        out = dram_out(nc, "out", [D, T], F32)
        zs = dram_tmp(nc, "zs", [32 * 128, T], BF16)
    else:
        s_out = dram_out(nc, "s_out", [NH, 256, 512], F32)
    with ExitStack() as es:
        sb = mk_sb(nc, es)
        XX = sb("XX", [128, 32, T], BF16)
        XM = XX[:, 0:16, :]
        W = [sb(f"W{i}", [128, 16, 512], BF16) for i in range(2)]
        scratch = sb("scratch", [128, 16, 320], F32)
        scr = scratch[:].rearrange("p c t -> p (c t)")
        modsb = sb("modsb", [128, 4, 16], F32)
        ones = sb("ones", [128, 128], BF16)
        epsb = sb("epsb", [128, 2], F32)
        rstd = sb("rstd", [128, 512], F32)
        sq = sb("sq", [128, 2, 512], BF16)
        xn = sb("xn", [128, 2, 512], F32)
        posi = sb("posi", [128, T], I32)
        invsb = sb("invsb", [128, 1], F32)
        sin_t = sb("sin_t", [128, T], F32)
        cos_t = sb("cos_t", [128, T], F32)
        identf = sb("identf", [128, 128], F32)
        identb = sb("identb", [128, 128], BF16)
        kdecsb = sb("kdecsb", [128, NH], F32)
        kdecAsb = sb("kdecAsb", [128, NH, 8], F32)
        KT = sb("KT", [128, 2, T], BF16)
        KTOK = sb("KTOK", [128, 8, 256], BF16)
        VTOK = sb("VTOK", [128, 8, 512], BF16)
        pss = mk_pss(nc, es, 6)
        pstb = mk_pst(nc, es)
        if mode == "main":
            QT = sb("QT", [128, 2, T], BF16)
            QD = sb("QD", [128, 2, T], BF16)
            GT = sb("GT", [128, 4, T], BF16)
            ST = sb("ST", [128, 2, 512], F32)
            STB = sb("STB", [128, 2, 512], BF16)
            ON = sb("ON", [128, 2, 512], BF16)
            IT = sb("IT", [128, 2, 128], BF16)
            tmpz = sb("tmpz", [128, 4, 128], F32)
            decTsb = sb("decTsb", [128, NH, 128], F32)
            qdecsb = sb("qdecsb", [128, NH, 128], F32)
            gnsb = sb("gnsb", [128, 32, 2], F32)
            hres = sb("hres", [128, 2, 512], F32)
            hout = sb("hout", [128, 2, 512], F32)
            bst = sb("bst", [128, 2, 6], F32)
            gmv = sb("gmv", [128, 2, 4], F32)
        else:
            SO = sb("SO", [128, 2, 512], F32)
        P = mk_prog(nc)
        try:
            ld = lambda dst, src, key: P.op("sp", lambda e: e.dma_start(out=dst, in_=src), writes=[key], dma=True)
            ld(modsb[:], mod, "mod")
            ld(posi[:], posb, ("rope", "posi"))
            ld(invsb[:], invf, ("rope", "invf"))
            ld(identf[:], ident, "identf")
            ld(kdecsb[:], kdec, "kdec")
            ld(kdecAsb[:], kdecA, "kdecA")
            if mode == "main":
                ld(decTsb[:], decT, "decT")
                ld(qdecsb[:], qdec, "qdec")
                ld(gnsb[:], gnp, "gn")
            P.op("dve", lambda e: e.memset(ones[:], 1.0), writes=["ones"])
            P.op("dve", lambda e: e.memset(epsb[:, 0:1], EPS), writes=["eps"])
            P.op("dve", lambda e: e.memset(epsb[:, 1:2], GN_EPS), writes=["eps"])
            P.op("dve", lambda e: e.tensor_copy(out=identb[:], in_=identf[:]), reads=["identf"], writes=["ident"])
            P.op("dve", lambda e: e.tensor_scalar(out=modsb[:, 1, :], in0=modsb[:, 1, :], scalar1=1.0, scalar2=None, op0=ALU.add), reads=["mod"], writes=["mod"])
            tm = [scr[:, i * 1024:(i + 1) * 1024] for i in range(4)]
            ki = scr[:, 4096:5120].bitcast(I32)
            emit_rope_tables(P, posi, invsb, sin_t, cos_t, tm + [ki], T)
            stage = scratch[:, :, 0:256]
            tk = {"rstd": rstd, "sq": sq, "xn": xn, "eps": epsb}
            P.barrier()
            emit_norm_mod(P, nc, es, hT, XM, modsb, 0, 1, T, 0, stage, ones, pss, tk)
            P.barrier()
            chk(1)
            RT = lambda i: scr[:, i * 512:(i + 1) * 512]
            Zh = scr[:, 3072:3072 + 2048].bitcast(BF16).rearrange("p (j t) -> p j t", j=4)

            wcount = [0]

            def load_w(parts):
                wb = wcount[0] % 2
                wcount[0] += 1
                for off, src in parts:
                    ncol = src.shape[1]
                    P.op("pool", L(lambda e, wb, off, src, ncol: e.dma_start(out=W[wb][:, :, off:off + ncol], in_=src.rearrange("(k p) n -> p k n", p=128)), wb, off, src, ncol),
                         writes=[("W", wb)], dma=True)
                return wb

            bankc = [0]

            def nb():
                b = 1 + bankc[0] % (3 if mode == "main" else 5)
                bankc[0] += 1
                return b

            def proj_rot(wb, col0, dst, dkey, sc):
                for hb in range(2):
                    banks = [nb(), nb()]
                    for c in range(2):
                        for k in range(16):
                            P.op("pe", L(lambda e, wb, k, c, hb, bi: e.matmul(pss[bi][:, :], W[wb][:, k, col0 + c * 128:col0 + (c + 1) * 128], XM[:, k, hb * 512:(hb + 1) * 512], start=(k == 0), stop=(k == 15)), wb, k, c, hb, banks[c]),
                                 reads=[("W", wb), ("xm", k)], writes=[("ps", banks[c])])
                    x1, x2, t1, t2 = RT(0), RT(1), RT(2), RT(3)
                    P.op("act", L(lambda e, bi: e.activation(out=x1, in_=pss[bi][:, :], func=AF.Copy, scale=sc), banks[0]), reads=[("ps", banks[0])], writes=["x1"])
                    P.op("act", L(lambda e, bi: e.activation(out=x2, in_=pss[bi][:, :], func=AF.Copy, scale=sc), banks[1]), reads=[("ps", banks[1])], writes=["x2"])
                    cs = cos_t[:, hb * 512:(hb + 1) * 512]
                    sn = sin_t[:, hb * 512:(hb + 1) * 512]
                    ck, sk = ("rope", "cos"), ("rope", "sin")
                    P.op("dve", L(lambda e, cs: e.tensor_tensor(out=t1, in0=x1, in1=cs, op=ALU.mult), cs), reads=["x1", ck], writes=["t1"])
                    P.op("dve", L(lambda e, sn: e.tensor_tensor(out=t2, in0=x2, in1=sn, op=ALU.mult), sn), reads=["x2", sk], writes=["t2"])
                    P.op("dve", L(lambda e, hb: e.tensor_tensor(out=dst[:, 0, hb * 512:(hb + 1) * 512], in0=t1, in1=t2, op=ALU.subtract), hb), reads=["t1", "t2"], writes=[dkey])
                    P.op("dve", L(lambda e, cs: e.tensor_tensor(out=t1, in0=x2, in1=cs, op=ALU.mult), cs), reads=["x2", ck], writes=["t1"])
                    P.op("dve", L(lambda e, sn: e.tensor_tensor(out=t2, in0=x1, in1=sn, op=ALU.mult), sn), reads=["x1", sk], writes=["t2"])
                    P.op("dve", L(lambda e, hb: e.tensor_tensor(out=dst[:, 1, hb * 512:(hb + 1) * 512], in0=t1, in1=t2, op=ALU.add), hb), reads=["t1", "t2"], writes=[dkey])

            for hd in range(NH):
                parts = [(256, w_in[:, D + hd * 256:D + (hd + 1) * 256])]
                if mode == "main":
                    parts = [(0, w_in[:, hd * 256:(hd + 1) * 256])] + parts
                wb = load_w(parts)
                if mode == "main":
                    proj_rot(wb, 0, QT, "qt", 1.0)
                proj_rot(wb, 256, KT, "kt", 1.0 / 16.0)
                chk(2)
                wb = load_w([(0, w_in[:, 2 * D + hd * 512:2 * D + (hd + 1) * 512])])
                for n in range(8):
                    bi = nb()
                    for k in range(16):
                        P.op("pe", L(lambda e, wb, k, n, bi: e.matmul(pss[bi][:, :], XM[:, k, n * 128:(n + 1) * 128], W[wb][:, k, :], start=(k == 0), stop=(k == 15)), wb, k, n, bi),
                             reads=[("W", wb), ("xm", k)], writes=[("ps", bi)])
                    P.op("act", L(lambda e, n, bi: e.copy(out=VTOK[:, n, :], in_=pss[bi][:, :]), n, bi), reads=[("ps", bi)], writes=[("vtok", n)])
                chk(3)
                for n in range(8):
                    pb = n % 2
                    for c in range(2):
                        P.op("pe", L(lambda e, n, c, pb: e.transpose(pstb[pb][:, c, :], KT[:, c, n * 128:(n + 1) * 128], identb[:]), n, c, pb),
                             reads=["kt", "ident"], writes=[("pst", pb)])
                    ksc = kdecAsb[:, hd, n:n + 1] if mode == "state" else kdecsb[:, hd:hd + 1]
                    P.op("act", L(lambda e, n, pb, ksc: e.activation(out=KTOK[:, n, :], in_=pstb[pb][:, 0:2, :].rearrange("p a b -> p (a b)"), func=AF.Copy, scale=ksc), n, pb, ksc),
                         reads=[("pst", pb), "kdec", "kdecA"], writes=[("ktok", n)])
                chk(4)
                if mode == "state":
                    for c in range(2):
                        bi = nb()
                        for n in range(8):
                            P.op("pe", L(lambda e, n, c, bi: e.matmul(pss[bi][:, :], KTOK[:, n, c * 128:(c + 1) * 128], VTOK[:, n, :], start=(n == 0), stop=(n == 7)), n, c, bi),
                                 reads=[("ktok", n), ("vtok", n)], writes=[("ps", bi)])
                        P.op("act", L(lambda e, c, bi: e.copy(out=SO[:, c, :], in_=pss[bi][:, :]), c, bi), reads=[("ps", bi)], writes=[("so", c)])
                        P.op("sp", L(lambda e, c, hd: e.dma_start(out=s_out[hd, c * 128:(c + 1) * 128, :], in_=SO[:, c, :]), c, hd), reads=[("so", c)], dma=True, is_output=True)
                    continue
                wb = load_w([(0, w_in[:, 4 * D + hd * 512:4 * D + (hd + 1) * 512])])
                for j in range(4):
                    for hb in range(2):
                        bi = nb()
                        for k in range(16):
                            P.op("pe", L(lambda e, wb, k, j, hb, bi: e.matmul(pss[bi][:, :], W[wb][:, k, j * 128:(j + 1) * 128], XM[:, k, hb * 512:(hb + 1) * 512], start=(k == 0), stop=(k == 15)), wb, k, j, hb, bi),
                                 reads=[("W", wb), ("xm", k)], writes=[("ps", bi)])
                        P.op("act", L(lambda e, j, hb, bi: e.activation(out=GT[:, j, hb * 512:(hb + 1) * 512], in_=pss[bi][:, :], func=AF.Silu), j, hb, bi),
                             reads=[("ps", bi)], writes=["gt"])
                for c in range(2):
                    for n in range(8):
                        P.op("dve", L(lambda e, c, n, hd: e.tensor_tensor(out=QD[:, c, n * 128:(n + 1) * 128], in0=QT[:, c, n * 128:(n + 1) * 128], in1=qdecsb[:, hd, :], op=ALU.mult), c, n, hd),
                             reads=["qt", "qdec"], writes=["qd"])
                P.op("sp", L(lambda e, hd: e.dma_start(out=ST[:], in_=s_in[hd].rearrange("(c p) e -> p c e", p=128)), hd), writes=["st"], dma=True)
                P.op("act", lambda e: e.copy(out=STB[:], in_=ST[:]), reads=["st"], writes=["stb"])
                bo_of = {}

                def stageA(n):
                    tsl = slice(n * 128, (n + 1) * 128)
                    bi = nb()
                    for c in range(2):
                        P.op("pe", L(lambda e, c, bi, tsl: e.matmul(pss[bi][:, 0:128], KT[:, c, tsl], QT[:, c, tsl], start=(c == 0), stop=(c == 1)), c, bi, tsl),
                             reads=["kt", "qt"], writes=[("ps", bi)])
                    iq = n % 2
                    P.op("dve", L(lambda e, bi, iq, hd: e.tensor_tensor(out=IT[:, iq, :], in0=pss[bi][:, 0:128], in1=decTsb[:, hd, :], op=ALU.mult), bi, iq, hd),
                         reads=[("ps", bi), "decT"], writes=[("it", iq)])
                    bo = 4 + n % 2
                    P.op("pe", L(lambda e, bo, iq, n: e.matmul(pss[bo][:, :], IT[:, iq, :], VTOK[:, n, :], start=True, stop=False), bo, iq, n),
                         reads=[("it", iq), ("vtok", n)], writes=[("ps", bo)])
                    for c in range(2):
                        P.op("pe", L(lambda e, bo, c, tsl: e.matmul(pss[bo][:, :], QD[:, c, tsl], STB[:, c, :], start=False, stop=(c == 1)), bo, c, tsl),
                             reads=["qd", "stb"], writes=[("ps", bo)])
                    if n < 7:
                        for c in range(2):
                            bs_ = nb()
                            P.op("pe", L(lambda e, bs_, c, n: e.matmul(pss[bs_][:, :], KTOK[:, n, c * 128:(c + 1) * 128], VTOK[:, n, :], start=True, stop=True), bs_, c, n),
                                 reads=[("ktok", n), ("vtok", n)], writes=[("ps", bs_)])
                            P.op("dve", L(lambda e, bs_, c, hd: e.scalar_tensor_tensor(out=ST[:, c, :], in0=ST[:, c, :], scalar=cd[hd], in1=pss[bs_][:, :], op0=ALU.mult, op1=ALU.add), bs_, c, hd),
                                 reads=["st", ("ps", bs_)], writes=["st"])
                        P.op("act", lambda e: e.copy(out=STB[:], in_=ST[:]), reads=["st"], writes=["stb"])
                    bo_of[n] = (bo, iq)

                def stageB(n):
                    tsl = slice(n * 128, (n + 1) * 128)
                    bo, iq = bo_of[n]
                    P.op("dve", L(lambda e, bo, iq: e.bn_stats(out=bst[:, iq, :], in_=pss[bo][:, :]), bo, iq), reads=[("ps", bo)], writes=[("bst", iq)])
                    P.op("dve", L(lambda e, iq: e.bn_aggr(out=gmv[:, iq, 0:2], in_=bst[:, iq, :]), iq), reads=[("bst", iq)], writes=[("gmv", iq)])
                    P.op("act", L(lambda e, iq: e.activation(out=gmv[:, iq, 2:3], in_=gmv[:, iq, 1:2], func=AF.Sqrt, bias=epsb[:, 1:2]), iq), reads=[("gmv", iq), "eps"], writes=[("gmv2", iq)])
                    P.op("dve", L(lambda e, iq: e.reciprocal(out=gmv[:, iq, 3:4], in_=gmv[:, iq, 2:3]), iq), reads=[("gmv2", iq)], writes=[("gmv3", iq)])
                    P.op("dve", L(lambda e, bo, iq: e.tensor_scalar(out=ON[:, iq, :], in0=pss[bo][:, :], scalar1=gmv[:, iq, 0:1], scalar2=gmv[:, iq, 3:4], op0=ALU.subtract, op1=ALU.mult), bo, iq),
                         reads=[("ps", bo), ("gmv", iq), ("gmv3", iq)], writes=[("on", iq)])
                    for j in range(4):
                        P.op("pe", L(lambda e, iq, j: e.transpose(pstb[iq][:, 4 + j, :], ON[:, iq, j * 128:(j + 1) * 128], identb[:]), iq, j),
                             reads=[("on", iq), "ident"], writes=[("pst", iq)])
                    for j in range(4):
                        ec = hd * 4 + j
                        P.op("act", L(lambda e, j, ec, iq: e.activation(out=tmpz[:, j, :], in_=pstb[iq][:, 4 + j, :], func=AF.Identity, scale=gnsb[:, ec, 0:1], bias=gnsb[:, ec, 1:2]), j, ec, iq),
                             reads=[("pst", iq), "gn"], writes=[("tmpz", j)])
                        P.op("dve", L(lambda e, j, tsl: e.tensor_tensor(out=Zh[:, j, tsl], in0=tmpz[:, j, :], in1=GT[:, j, tsl], op=ALU.mult), j, tsl),
                             reads=[("tmpz", j), "gt"], writes=["zh"])

                stageA(0)
                for n in range(8):
                    if n + 1 < 8:
                        stageA(n + 1)
                    stageB(n)
                P.op("sp", L(lambda e, hd: e.dma_start(out=zs[hd * 512:(hd + 1) * 512, :].rearrange("(j p) t -> p j t", p=128), in_=Zh), hd),
                     reads=["zh"], writes=[("zs", hd)], dma=True)
            if mode == "main":
                P.barrier()
                for half in range(2):
                    P.op("sp", L(lambda e, half: e.dma_start(out=XX[:, half * 16:(half + 1) * 16, :], in_=zs[half * 2048:(half + 1) * 2048, :].rearrange("(j p) t -> p j t", p=128)), half),
                         reads=[("zs", h_) for h_ in range(8)], writes=[("zall", half)], dma=True)
                for ti in range(8):
                    wb = wcount[0] % 2
                    wcount[0] += 1
                    Wv = W[wb][:].rearrange("p k n -> p (k n)").rearrange("p (k n) -> p k n", k=32)
                    P.op("pool", L(lambda e, Wv, ti: e.dma_start(out=Wv, in_=w_out[:, ti * 256:(ti + 1) * 256].rearrange("(k p) n -> p k n", p=128)), Wv, ti),
                         writes=[("W", wb)], dma=True)
                    for mm in range(2):
                        m = ti * 2 + mm
                        for hb in range(2):
                            q = (m * 2 + hb) % 2
                            bi = nb()
                            P.op("sp", L(lambda e, m, hb, q: e.dma_start(out=hres[:, q, :], in_=hT[m * 128:(m + 1) * 128, hb * 512:(hb + 1) * 512]), m, hb, q),
                                 writes=[("hres", q)], dma=True)
                            for k in range(32):
                                P.op("pe", L(lambda e, Wv, k, mm, hb, bi: e.matmul(pss[bi][:, :], Wv[:, k, mm * 128:(mm + 1) * 128], XX[:, k, hb * 512:(hb + 1) * 512], start=(k == 0), stop=(k == 31)), Wv, k, mm, hb, bi),
                                     reads=[("W", wb), ("zall", k // 16)], writes=[("ps", bi)])
                            P.op("dve", L(lambda e, m, q, bi: e.scalar_tensor_tensor(out=hout[:, q, :], in0=pss[bi][:, :], scalar=modsb[:, 2, m:m + 1], in1=hres[:, q, :], op0=ALU.mult, op1=ALU.add), m, q, bi),
                                 reads=[("ps", bi), ("hres", q), "mod"], writes=[("hout", q)])
                            P.op("sp", L(lambda e, m, hb, q: e.dma_start(out=out[m * 128:(m + 1) * 128, hb * 512:(hb + 1) * 512], in_=hout[:, q, :]), m, hb, q),
                                 reads=[("hout", q)], dma=True, is_output=True)

        except _Stop:
            pass
        finish(P)
    return nc

HD = 64
NHC = 16
CH = NHC * HD
C = 128
NLEV = 7


def scan_consts():
    i = np.arange(128)
    tri_incl = (i[:, None] <= i[None, :]).astype(np.float32)
    tri_strict = (i[:, None] < i[None, :]).astype(np.float32)
    low_strict = (i[:, None] > i[None, :]).astype(np.float32)
    mask2 = np.concatenate([tri_strict, tri_incl], 1)
    ones = np.ones((128, 128), np.float32)
    return dict(tri_incl=tri_incl, low_strict=low_strict, mask2=mask2, ones=ones, ident=np.eye(128, dtype=np.float32))


def build_scan(S, o_fm=False, want_o=True):
    NCK = S // C
    nc = mk_nc()
    din = lambda name, shape: dram_in(nc, name, shape, F32)
    lw_t, kk_t, a_t, k_t, v_t = [din(n, [S, CH]) for n in ("lw_t", "kk_t", "a_t", "k_t", "v_t")]
    lw_f, kk_f, a_f, k_f, r_f = [din(n, [CH, S]) for n in ("lw_f", "kk_f", "a_f", "k_f", "r_f")]
    tri_incl = din("tri_incl", [128, 128])
    low_strict = din("low_strict", [128, 128])
    mask2 = din("mask2", [128, 256])
    ones_d = din("ones", [128, 128])
    s0 = din("s0", [NHC, HD, HD])
    o_out = dram_out(nc, "o_out", [CH, S] if o_fm else [S, CH], F32) if want_o else None
    s_out = dram_out(nc, "s_out", [NHC, HD, HD], F32)
    ident_d = din("ident", [128, 128])
    with ExitStack() as es:
        sb = mk_sb(nc, es)
        tri = sb("tri", [128, 128]); lows = sb("lows", [128, 128]); m2 = sb("m2", [128, 256]); ones = sb("ones_sb", [128, 128])
        identf = sb("identf", [128, 128]); identb = sb("identb", [128, 128], BF16)
        TT = {n: [sb(f"{n}{i}", [128, CH]) for i in range(2)] for n in ("lw", "kk", "a", "k", "v")}
        TB = {n: [sb(f"{n}b{i}", [128, CH], BF16) for i in range(2)] for n in ("v", "bh", "kh")}
        FF = {n: [sb(f"{n}f{i}", [128, 8, 128]) for i in range(2)] for n in ("lw", "kk", "a", "k", "r", "at", "bt")}
        FB = {n: [sb(f"{n}fb{i}", [128, 8, 128], BF16) for i in range(2)] for n in ("at", "rt", "bt", "kt")}
        tmpT = [sb(f"tmpT{i}", [128, CH]) for i in range(2)]
        tmpF = [sb(f"tmpF{i}", [128, 8, 128]) for i in range(3)]
        OUT = [sb(f"OUT{i}", [128, 8, 128] if o_fm else [128, CH]) for i in range(2)]
        ST = sb("ST", [128, NHC // 2, HD])
        STb = sb("STb", [128, NHC // 2, HD], BF16)
        pcT = [sb(f"pcT{i}", [128, 8]) for i in range(2)]
        NW = 4
        Nb = [[sb(f"N{w}_{j}", [128, 128], BF16) for j in range(NLEV)] for w in range(NW)]
        Mb = [[sb(f"M{w}_{j}", [128, 128], BF16) for j in range(NLEV)] for w in range(NW)]
        AB = [sb(f"AB{w}", [128, 256], BF16) for w in range(NW)]
        AK = [sb(f"AK{w}", [128, 256], BF16) for w in range(NW)]
        U = [[sb(f"U{w}_{j}", [128, HD], BF16) for j in range(2)] for w in range(NW)]
        pss = mk_pss(nc, es, 8)
        P = mk_prog(nc)
        ld = lambda dst, src, key: P.op("sp", lambda e: e.dma_start(out=dst, in_=src), writes=[key], dma=True)
        ld(tri[:], tri_incl, "tri"); ld(lows[:], low_strict, "lows"); ld(m2[:], mask2, "m2"); ld(ones[:], ones_d, "ones"); ld(identf[:], ident_d, "identf")
        ld(ST[:], s0.rearrange("(g two) k v -> (two k) g v", two=2), "st_all")
        P.op("dve", lambda e: e.tensor_copy(out=identb[:], in_=identf[:]), reads=["identf"], writes=["identb"])
        P.op("act", lambda e: e.copy(out=STb[:], in_=ST[:]), reads=["st_all"], writes=["stb_all"])
        for h in range(NHC):
            P.last_writer[("st", h)] = P.last_writer["st_all"]
            P.last_writer[("stb", h)] = P.last_writer["stb_all"]
        bankc = [0]

        def nb():
            b = bankc[0] % 8
            bankc[0] += 1
            return b

        RECORDING = [None]

        def mark():
            if RECORDING[0] is not None:
                RECORDING[0].append([])

        def chunk_pre(n):
            q = n % 2
            tsl = slice(n * C, (n + 1) * C)
            for nm, src in (("lw", lw_t), ("kk", kk_t), ("a", a_t), ("k", k_t), ("v", v_t)):
                P.op("sp", L(lambda e, nm, src: e.dma_start(out=TT[nm][q][:], in_=src[tsl, :]), nm, src), writes=[("T", nm, q)], dma=True)
            for nm, src in ((("lw", lw_f), ("kk", kk_f), ("a", a_f), ("k", k_f), ("r", r_f)) if want_o else (("lw", lw_f), ("kk", kk_f), ("a", a_f), ("k", k_f))):
                P.op("sp", L(lambda e, nm, src: e.dma_start(out=FF[nm][q][:], in_=src[:, tsl].rearrange("(c p) t -> p c t", p=128)), nm, src), writes=[("F", nm, q)], dma=True)
            P.op("act", lambda e: e.copy(out=TB["v"][q][:], in_=TT["v"][q][:]), reads=[("T", "v", q)], writes=[("B", "v", q)])
            mark()
            b0, b1, b2, b3 = nb(), nb(), nb(), nb()
            for half, (bc, bt) in enumerate(((b0, b2), (b1, b3))):
                hs = slice(half * 512, (half + 1) * 512)
                mark()
                P.op("pe", L(lambda e, bc, hs: e.matmul(pss[bc][:, :], tri[:], TT["lw"][q][:, hs], start=True, stop=True), bc, hs), reads=["tri", ("T", "lw", q)], writes=[("ps", bc)])
                P.op("pe", L(lambda e, bt, hs: e.matmul(pss[bt][:, :], ones[:], TT["lw"][q][:, hs], start=True, stop=True), bt, hs), reads=["ones", ("T", "lw", q)], writes=[("ps", bt)])
                P.op("dve", L(lambda e, bc, bt, hs: e.tensor_copy(out=tmpT[0][:, hs], in_=pss[bc][:, :]), bc, bt, hs), reads=[("ps", bc)], writes=[("tmpT", 0, half)])
                P.op("dve", L(lambda e, bt, hs: e.tensor_tensor(out=tmpT[0][:, hs], in0=pss[bt][:, :], in1=tmpT[0][:, hs], op=ALU.subtract), bt, hs), reads=[("ps", bt), ("tmpT", 0, half)], writes=[("tmpT", 0, half)])
                mark()
                P.op("act", L(lambda e, hs: e.activation(out=tmpT[0][:, hs], in_=tmpT[0][:, hs], func=AF.Exp), hs), reads=[("tmpT", 0, half)], writes=[("tmpT", 0, half)])
                P.op("dve", L(lambda e, hs: e.tensor_tensor(out=tmpT[1][:, hs], in0=TT["kk"][q][:, hs], in1=TT["a"][q][:, hs], op=ALU.mult), hs), reads=[("T", "kk", q), ("T", "a", q)], writes=[("tmpT", 1, half)])
                P.op("dve", L(lambda e, hs: e.tensor_tensor(out=TB["bh"][q][:, hs], in0=tmpT[1][:, hs], in1=tmpT[0][:, hs], op=ALU.mult), hs), reads=[("tmpT", 1, half), ("tmpT", 0, half)], writes=[("B", "bh", q)])
                P.op("dve", L(lambda e, hs: e.tensor_tensor(out=TB["kh"][q][:, hs], in0=TT["k"][q][:, hs], in1=tmpT[0][:, hs], op=ALU.mult), hs), reads=[("T", "k", q), ("tmpT", 0, half)], writes=[("B", "kh", q)])
            mark()
            c0, c1 = nb(), nb()
            for g in range(8):
                bc = c0 if g < 4 else c1
                P.op("pe", L(lambda e, bc, g: e.matmul(pss[bc][:, (g % 4) * 128:(g % 4 + 1) * 128], TT["lw"][q][:, g * 128:(g + 1) * 128], tri[:], start=True, stop=True), bc, g),
                     reads=["tri", ("T", "lw", q)], writes=[("ps", bc)])
            csT, e1, e2 = tmpF[0], tmpF[1], tmpF[2]
            for hf, bc in enumerate((c0, c1)):
                gs = slice(hf * 4, (hf + 1) * 4)
                P.op("act", L(lambda e, bc, gs: e.copy(out=csT[:, gs, :], in_=pss[bc][:, :].rearrange("p (g t) -> p g t", g=4)), bc, gs), reads=[("ps", bc)], writes=[("tmpF", 0, hf)])
            mark()
            ck = [("tmpF", 0, 0), ("tmpF", 0, 1)]
            P.op("act", lambda e: e.activation(out=pcT[q][:, :], in_=csT[:, :, 127], func=AF.Exp), reads=ck, writes=[("pc", q)])
            if want_o: P.op("act", lambda e: e.activation(out=e1[:], in_=csT[:], func=AF.Exp), reads=ck, writes=[("tmpF", 1)])
            if want_o: P.op("dve", lambda e: e.tensor_tensor(out=FB["rt"][q][:], in0=FF["r"][q][:], in1=e1[:], op=ALU.mult), reads=[("F", "r", q), ("tmpF", 1)], writes=[("FB", "rt", q)])
            P.op("act", lambda e: e.activation(out=e2[:], in_=csT[:], func=AF.Exp, scale=-1.0), reads=ck, writes=[("tmpF", 2)])
            P.op("dve", lambda e: e.tensor_tensor(out=FB["kt"][q][:], in0=FF["k"][q][:], in1=e2[:], op=ALU.mult), reads=[("F", "k", q), ("tmpF", 2)], writes=[("FB", "kt", q)])
            P.op("dve", lambda e: e.tensor_tensor(out=e2[:], in0=FF["kk"][q][:], in1=e2[:], op=ALU.mult), reads=[("F", "kk", q), ("tmpF", 2), ("FB", "kt", q)], writes=[("tmpF", 2)])
            P.op("dve", lambda e: e.tensor_tensor(out=FF["bt"][q][:], in0=FF["a"][q][:], in1=e2[:], op=ALU.mult), reads=[("F", "a", q), ("tmpF", 2)], writes=[("F", "bt", q)])
            P.op("act", lambda e: e.copy(out=FB["bt"][q][:], in_=FF["bt"][q][:]), reads=[("F", "bt", q)], writes=[("FB", "bt", q)])
            P.op("dve", lambda e: e.tensor_tensor(out=e1[:], in0=csT[:], in1=FF["lw"][q][:], op=ALU.subtract), reads=ck + [("F", "lw", q), ("FB", "rt", q)], writes=[("tmpF", 1)])
            P.op("act", lambda e: e.activation(out=e1[:], in_=e1[:], func=AF.Exp), reads=[("tmpF", 1)], writes=[("tmpF", 1)])
            P.op("dve", lambda e: e.scalar_tensor_tensor(out=FF["at"][q][:], in0=FF["kk"][q][:], scalar=-1.0, in1=e1[:], op0=ALU.mult, op1=ALU.mult), reads=[("F", "kk", q), ("tmpF", 1)], writes=[("F", "at", q)])
            P.op("act", lambda e: e.copy(out=FB["at"][q][:], in_=FF["at"][q][:]), reads=[("F", "at", q)], writes=[("FB", "at", q)])

        def head_groups(n, h):
            q = n % 2
            w = h % NW
            pb = (h % 2) * 64
            g = h // 2
            hsl = slice(h * HD, (h + 1) * HD)
            fm = lambda nm: FF[nm][q][pb:pb + 64, g, :]
            fb = lambda nm: FB[nm][q][pb:pb + 64, g, :]
            pre, chain = [], []

            def g_N():
                bi = nb()
                P.op("pe", lambda e: e.matmul(pss[bi][:, 0:128], fm("at"), fm("bt"), start=True, stop=True), reads=[("F", "at", q), ("F", "bt", q)], writes=[("ps", bi)])
                P.op("dve", lambda e: e.tensor_tensor(out=Nb[w][0][:], in0=pss[bi][:, 0:128], in1=lows[:], op=ALU.mult), reads=[("ps", bi), "lows"], writes=[("N", w, 0)])
            pre.append(g_N)

            def g_AB():
                bi = nb()
                P.op("pe", lambda e: e.matmul(pss[bi][:, 0:128], fm("bt"), fm("at"), start=True, stop=True), reads=[("F", "at", q), ("F", "bt", q)], writes=[("ps", bi)])
                if want_o:
                    P.op("pe", lambda e: e.matmul(pss[bi][:, 128:256], fb("bt"), fb("rt"), start=True, stop=True), reads=[("FB", "rt", q), ("FB", "bt", q)], writes=[("ps", bi)])
                    P.op("dve", lambda e: e.tensor_tensor(out=AB[w][:], in0=pss[bi][:, 0:256], in1=m2[:], op=ALU.mult), reads=[("ps", bi), "m2"], writes=[("AB", w)])
                else:
                    P.op("dve", lambda e: e.tensor_tensor(out=AB[w][:, 0:128], in0=pss[bi][:, 0:128], in1=m2[:, 0:128], op=ALU.mult), reads=[("ps", bi), "m2"], writes=[("AB", w)])
            pre.append(g_AB)
            M0 = AB[w][:, 0:128]


            def Mj(j):
                return M0 if j == 0 else Mb[w][j][:]

            def mkey(j):
                return ("AB", w) if j == 0 else ("M", w, j)

            def g_AK():
                bi = nb()
                P.op("pe", lambda e: e.matmul(pss[bi][:, 0:128], fb("kt"), fb("at"), start=True, stop=True), reads=[("FB", "at", q), ("FB", "kt", q)], writes=[("ps", bi)])
                if want_o:
                    P.op("pe", lambda e: e.matmul(pss[bi][:, 128:256], fb("kt"), fb("rt"), start=True, stop=True), reads=[("FB", "rt", q), ("FB", "kt", q)], writes=[("ps", bi)])
                    P.op("dve", lambda e: e.tensor_tensor(out=AK[w][:], in0=pss[bi][:, 0:256], in1=m2[:], op=ALU.mult), reads=[("ps", bi), "m2"], writes=[("AK", w)])
                else:
                    P.op("dve", lambda e: e.tensor_tensor(out=AK[w][:, 0:128], in0=pss[bi][:, 0:128], in1=m2[:, 0:128], op=ALU.mult), reads=[("ps", bi), "m2"], writes=[("AK", w)])
            pre.append(g_AK)

            for j in range(NLEV - 1):
                def g_sq(j=j):
                    b1, b2 = nb(), nb()
                    P.op("pe", lambda e: e.matmul(pss[b1][:, 0:128], Nb[w][j][:], Mj(j), start=True, stop=True), reads=[("N", w, j), mkey(j)], writes=[("ps", b1)])
                    P.op("act", lambda e: e.copy(out=Mb[w][j + 1][:], in_=pss[b1][:, 0:128]), reads=[("ps", b1)], writes=[("M", w, j + 1)])
                    if j < NLEV - 2:
                        P.op("pe", lambda e: e.matmul(pss[b2][:, 0:128], Mj(j), Nb[w][j][:], start=True, stop=True), reads=[("N", w, j), mkey(j)], writes=[("ps", b2)])
                        P.op("dve", lambda e: e.tensor_copy(out=Nb[w][j + 1][:], in_=pss[b2][:, 0:128]), reads=[("ps", b2)], writes=[("N", w, j + 1)])
                pre.append(g_sq)

            Sh = ST[pb:pb + 64, g, :]
            Shb = STb[pb:pb + 64, g, :]
            Vt = TB["v"][q][:, hsl]

            def g_W():
                bi = nb()
                P.op("pe", lambda e: e.matmul(pss[bi][:, 0:HD], AK[w][:, 0:128], Vt, start=True, stop=False), reads=[("AK", w), ("B", "v", q)], writes=[("ps", bi)])
                P.op("pe", lambda e: e.matmul(pss[bi][:, 0:HD], fb("at"), Shb, start=False, stop=True), reads=[("FB", "at", q), ("stb", h)], writes=[("ps", bi)])
                P.op("act", lambda e: e.copy(out=U[w][0][:], in_=pss[bi][:, 0:HD]), reads=[("ps", bi)], writes=[("U", w, 0)])
            chain.append(g_W)
            for j in range(NLEV):
                def g_U(j=j):
                    bi = nb()
                    src, dst = U[w][j % 2], U[w][(j + 1) % 2]
                    P.op("pe", lambda e: e.matmul(pss[bi][:, 0:HD], identb[:], src[:], start=True, stop=False), reads=["identb", ("U", w, j % 2)], writes=[("ps", bi)])
                    P.op("pe", lambda e: e.matmul(pss[bi][:, 0:HD], Mj(j), src[:], start=False, stop=True), reads=[mkey(j), ("U", w, j % 2)], writes=[("ps", bi)])
                    if j % 2 == 0:
                        P.op("dve", lambda e: e.tensor_copy(out=dst[:], in_=pss[bi][:, 0:HD]), reads=[("ps", bi)], writes=[("U", w, (j + 1) % 2)])
                    else:
                        P.op("act", lambda e: e.copy(out=dst[:], in_=pss[bi][:, 0:HD]), reads=[("ps", bi)], writes=[("U", w, (j + 1) % 2)])
                chain.append(g_U)
            Uf = U[w][NLEV % 2]
            ufk = ("U", w, NLEV % 2)

            def g_O():
                bi = nb()
                if o_fm:
                    P.op("pe", lambda e: e.matmul(pss[bi][pb:pb + 64, 0:128], Uf[:], AB[w][:, 128:256], start=True, stop=False), reads=[("AB", w), ufk], writes=[("ps", bi)])
                    P.op("pe", lambda e: e.matmul(pss[bi][pb:pb + 64, 0:128], Vt, AK[w][:, 128:256], start=False, stop=False), reads=[("AK", w), ("B", "v", q)], writes=[("ps", bi)])
                    P.op("pe", lambda e: e.matmul(pss[bi][pb:pb + 64, 0:128], Shb, fb("rt"), start=False, stop=True), reads=[("FB", "rt", q), ("stb", h)], writes=[("ps", bi)])
                    P.op("act", lambda e: e.copy(out=OUT[q][pb:pb + 64, g, :], in_=pss[bi][pb:pb + 64, 0:128]), reads=[("ps", bi)], writes=[("out", q)])
                    return
                P.op("pe", lambda e: e.matmul(pss[bi][:, 0:HD], AB[w][:, 128:256], Uf[:], start=True, stop=False), reads=[("AB", w), ufk], writes=[("ps", bi)])
                P.op("pe", lambda e: e.matmul(pss[bi][:, 0:HD], AK[w][:, 128:256], Vt, start=False, stop=False), reads=[("AK", w), ("B", "v", q)], writes=[("ps", bi)])
                P.op("pe", lambda e: e.matmul(pss[bi][:, 0:HD], fb("rt"), Shb, start=False, stop=True), reads=[("FB", "rt", q), ("stb", h)], writes=[("ps", bi)])
                P.op("act", lambda e: e.copy(out=OUT[q][:, hsl], in_=pss[bi][:, 0:HD]), reads=[("ps", bi)], writes=[("out", q)])
            if want_o:
                chain.append(g_O)

            def g_S():
                bi = nb()
                P.op("pe", lambda e: e.matmul(pss[bi][pb:pb + 64, 0:HD], TB["bh"][q][:, hsl], Uf[:], start=True, stop=False), reads=[("B", "bh", q), ufk], writes=[("ps", bi)])
                P.op("pe", lambda e: e.matmul(pss[bi][pb:pb + 64, 0:HD], TB["kh"][q][:, hsl], Vt, start=False, stop=True), reads=[("B", "kh", q), ("B", "v", q)], writes=[("ps", bi)])
                P.op("dve", lambda e: e.scalar_tensor_tensor(out=Sh, in0=Sh, scalar=pcT[q][pb:pb + 64, g:g + 1], in1=pss[bi][pb:pb + 64, 0:HD], op0=ALU.mult, op1=ALU.add),
                     reads=[("st", h), ("pc", q), ("ps", bi)], writes=[("st", h)])
                P.op("act", lambda e: e.copy(out=Shb, in_=Sh), reads=[("st", h)], writes=[("stb", h)])
            chain.append(g_S)
            return pre, chain

        GH = 4

        def record(fn, *a):
            rec = [[]]
            RECORDING[0] = rec
            P.op = lambda *aa, **kk: rec[-1].append((aa, kk))
            try:
                fn(*a)
            finally:
                del P.op
                RECORDING[0] = None
            units = []
            for grp in rec:
                if any(aa[0] == "pe" for aa, kk in grp):
                    units.append(lambda grp=grp: [P.op(*aa, **kk) for aa, kk in grp])
                else:
                    for aa, kk in grp:
                        units.append(lambda aa=aa, kk=kk: P.op(*aa, **kk))
            return units

        for u_ in record(chunk_pre, 0):
            u_()
        for n in range(NCK):
            nxt = record(chunk_pre, n + 1) if n + 1 < NCK else []
            ngrp = NHC // GH
            per = (len(nxt) + ngrp - 1) // ngrp
            for gi, h0 in enumerate(range(0, NHC, GH)):
                streams = []
                for h in range(h0, h0 + GH):
                    pre, chain = head_groups(n, h)
                    head_pre, sq = pre[:3], pre[3:]
                    merged = list(head_pre) + [chain[0]]
                    us = chain[1:1 + NLEV]
                    tail = chain[1 + NLEV:]
                    for j in range(NLEV):
                        if j < len(sq):
                            merged.append(sq[j])
                        merged.append(us[j])
                    merged += tail
                    streams.append(merged)
                streams.append(nxt[gi * per:(gi + 1) * per])
                while any(streams):
                    for st_ in streams:
                        if st_:
                            st_.pop(0)()
            q = n % 2
            if want_o and o_fm:
                P.op("pool", L(lambda e, n, q: e.dma_start(out=o_out[:, n * C:(n + 1) * C].rearrange("(c p) t -> p c t", p=128), in_=OUT[q][:]), n, q), reads=[("out", q)], writes=["o_dram"], dma=True, is_output=True)
            elif want_o:
                P.op("pool", L(lambda e, n, q: e.dma_start(out=o_out[n * C:(n + 1) * C, :], in_=OUT[q][:]), n, q), reads=[("out", q)], writes=["o_dram"], dma=True, is_output=True)
        P.op("pool", lambda e: e.dma_start(out=s_out.rearrange("(g two) k v -> (two k) g v", two=2), in_=ST[:]), reads=[("st", h) for h in range(NHC)], writes=["s_dram"], dma=True, is_output=True)
        finish(P)
    return nc

T = 1024
LD = 96
LG = 256
GN_EPS_RW = 64 * 1e-5


def blockones():
    b = np.zeros((128, 128), np.float32)
    b[:64, :64] = 1.0
    b[64:, 64:] = 1.0
    return b


def build_rwa(want_tm=False):
    nc = mk_nc()
    din = lambda name, shape: dram_in(nc, name, shape, F32)
    hT = din("hT", [D, T + 1])
    mod = din("mod", [128, 4, 16])
    flag = din("flag", [128, 1])
    mu = din("mu", [128, 6, 16])
    w_rkv = din("w_rkv", [3, D, D])
    w1 = din("w1", [D, LD]); w2 = din("w2", [LD, D]); a1 = din("a1", [D, LD]); a2 = din("a2", [LD, D])
    g1 = din("g1", [D, LG]); g2 = din("g2", [LG, D])
    vecs = din("vecs", [128, 4, 16])
    bo_d = din("bo", [128, 128])
    outs = {n: dram_out(nc, n, [D, T], F32) for n in ("r_o", "k_o", "v_o", "kk_o", "a_o", "lw_o", "g_o")}
    TMN = {"k_o": "k_t", "v_o": "v_t", "kk_o": "kk_t", "a_o": "a_t", "lw_o": "lw_t"}
    if want_tm:
        outs_t = {n: dram_out(nc, n, [T, D], F32) for n in TMN.values()}
        ident_d = din("ident", [128, 128])
    with ExitStack() as es:
        sb = mk_sb(nc, es)
        XM = sb("XM", [128, 16, T + 1], BF16)
        XS = sb("XS", [128, 16, T], BF16)
        W = [sb(f"W{i}", [128, 16, 512], BF16) for i in range(2)]
        stage = sb("stage", [128, 16, 256])
        modsb = sb("modsb", [128, 4, 16]); flagsb = sb("flagsb", [128, 1]); musb = sb("musb", [128, 6, 16]); mu1 = sb("mu1", [128, 6, 16])
        vsb = sb("vsb", [128, 4, 16]); vka = sb("vka", [128, 16])
        ones = sb("ones", [128, 128], BF16); epsb = sb("epsb", [128, 2]); bo = sb("bo_sb", [128, 128])
        rstd = sb("rstd", [128, 512]); sq = sb("sq", [128, 2, 512], BF16); xn = sb("xn", [128, 2, 512])
        w1s = sb("w1s", [128, 16, LD], BF16); a1s = sb("a1s", [128, 16, LD], BF16); g1s = sb("g1s", [128, 16, LG], BF16)
        w2s = sb("w2s", [LD, D], BF16); a2s = sb("a2s", [LD, D], BF16); g2s = sb("g2s", [128, 2, D], BF16)
        T1 = sb("T1", [LD, T], BF16); T2 = sb("T2", [LD, T], BF16); T3 = sb("T3", [128, 2, T], BF16)
        tmul = sb("tmul", [128, 2, T])
        ob = [sb(f"ob{i}", [128, 512]) for i in range(8)]
        if want_tm:
            identF = sb("identF", [128, 128])
            ttile = [sb(f"tt{i}", [128, 4, 128]) for i in range(2)]
        pss = mk_pss(nc, es, 8)
        P = mk_prog(nc)
        ld = lambda dst, src, key, q="sp": P.op(q, lambda e: e.dma_start(out=dst, in_=src), writes=[key], dma=True)
        ld(modsb[:], mod, "mod"); ld(flagsb[:], flag, "flag"); ld(musb[:], mu, "mu"); ld(vsb[:], vecs, "vecs"); ld(bo[:], bo_d, "bo")
        ld(w1s[:], w1.rearrange("(k p) n -> p k n", p=128), "w1", "pool"); ld(a1s[:], a1.rearrange("(k p) n -> p k n", p=128), "a1", "pool")
        ld(g1s[:], g1.rearrange("(k p) n -> p k n", p=128), "g1", "pool")
        ld(w2s[:], w2, "w2", "pool"); ld(a2s[:], a2, "a2", "pool"); ld(g2s[:], g2.rearrange("(k p) n -> p k n", p=128), "g2", "pool")
        if want_tm:
            ld(identF[:], ident_d, "identF")
        P.op("dve", lambda e: e.memset(ones[:], 1.0), writes=["ones"])
        P.op("dve", lambda e: e.memset(epsb[:, 0:1], EPS), writes=["eps"])
        P.op("dve", lambda e: e.memset(epsb[:, 1:2], 1e-12), writes=["eps"])
        P.op("dve", lambda e: e.tensor_scalar(out=modsb[:, 1, :], in0=modsb[:, 1, :], scalar1=1.0, scalar2=None, op0=ALU.add), reads=["mod"], writes=["mod"])
        P.op("dve", lambda e: e.tensor_scalar(out=mu1[:], in0=musb[:], scalar1=-1.0, scalar2=1.0, op0=ALU.mult, op1=ALU.add), reads=["mu"], writes=["mu1"])
        P.op("dve", lambda e: e.tensor_scalar(out=vka[:], in0=vsb[:, 3, :], scalar1=-1.0, scalar2=1.0, op0=ALU.mult, op1=ALU.add), reads=["vecs"], writes=["vka"])
        tk = {"rstd": rstd, "sq": sq, "xn": xn, "eps": epsb}
        emit_norm_mod(P, nc, es, hT, XM, modsb, 0, 1, T + 1, 0, stage, ones, pss, tk, flag=flagsb, halo=1)

        def make_xs(i):
            for c in range(16):
                tq = c % 2
                P.op("act", L(lambda e, c, tq: e.activation(out=tmul[:, tq, :], in_=XM[:, c, 0:T], func=AF.Copy, scale=musb[:, i, c:c + 1]), c, tq),
                     reads=[("xm", c), "mu"], writes=[("tmul", tq)])
                P.op("dve", L(lambda e, c, tq: e.scalar_tensor_tensor(out=XS[:, c, :], in0=XM[:, c, 1:T + 1], scalar=mu1[:, i, c:c + 1], in1=tmul[:, tq, :], op0=ALU.mult, op1=ALU.add), c, tq),
                     reads=[("xm", c), "mu1", ("tmul", tq)], writes=[("xs", c)])

        bankc = [0]; obc = [0]; wcount = [0]

        def nb():
            b = 1 + bankc[0] % 7
            bankc[0] += 1
            return b

        def nob():
            b = obc[0] % 8
            obc[0] += 1
            return b

        def load_w(src):
            wb = wcount[0] % 2
            wcount[0] += 1
            P.op("pool", L(lambda e, wb, src: e.dma_start(out=W[wb][:], in_=src.rearrange("(k p) n -> p k n", p=128)), wb, src), writes=[("W", wb)], dma=True)
            return wb

        def store(name, m, hb, oi):
            P.op("sp", L(lambda e, name, m, hb, oi: e.dma_start(out=outs[name][m * 128:(m + 1) * 128, hb * 512:(hb + 1) * 512], in_=ob[oi][:]), name, m, hb, oi),
                 reads=[("ob", oi)], dma=True, is_output=True)
            if want_tm and name in TMN:
                bi = nb()
                ti = ttc[0] % 2
                ttc[0] += 1
                for j in range(4):
                    P.op("pe", L(lambda e, bi, j, oi: e.matmul(pss[bi][:, j * 128:(j + 1) * 128], ob[oi][:, j * 128:(j + 1) * 128], identF[:], start=True, stop=True), bi, j, oi),
                         reads=[("ob", oi), "identF"], writes=[("ps", bi)])
                P.op("act", L(lambda e, bi, ti: e.copy(out=ttile[ti][:], in_=pss[bi][:, :].rearrange("p (j c) -> p j c", j=4)), bi, ti), reads=[("ps", bi)], writes=[("tt", ti)])
                P.op("sp", L(lambda e, name, m, hb, ti: e.dma_start(out=outs_t[TMN[name]][hb * 512:(hb + 1) * 512, m * 128:(m + 1) * 128].rearrange("(j p) c -> p j c", p=128), in_=ttile[ti][:]), name, m, hb, ti),
                     reads=[("tt", ti)], dma=True, is_output=True)

        ttc = [0]

        def lora_hidden(i, ws, M, nchunk, dst, func, key):
            make_xs(i)
            for mc in range(nchunk):
                mw = min(128, M - mc * 128)
                for hb in range(2):
                    bi = nb()
                    for k in range(16):
                        P.op("pe", L(lambda e, k, mc, mw, hb, bi: e.matmul(pss[bi][0:mw, :], ws[:, k, mc * 128:mc * 128 + mw], XS[:, k, hb * 512:(hb + 1) * 512], start=(k == 0), stop=(k == 15)), k, mc, mw, hb, bi),
                             reads=[key, ("xs", k)], writes=[("ps", bi)])
                    d = dst[0:mw, hb * 512:(hb + 1) * 512] if nchunk == 1 else dst[:, mc, hb * 512:(hb + 1) * 512]
                    P.op("act", L(lambda e, d, bi, mw: e.activation(out=d, in_=pss[bi][0:mw, :], func=func), d, bi, mw), reads=[("ps", bi)], writes=[key + "_h"])
        lora_hidden(3, w1s, LD, 1, T1, AF.Tanh, "w1")
        lora_hidden(4, a1s, LD, 1, T2, AF.Copy, "a1")
        lora_hidden(5, g1s, LG, 2, T3, AF.Sigmoid, "g1")

        def big_proj(i, widx, per_chunk):
            make_xs(i)
            for ti in range(4):
                wb = load_w(w_rkv[widx][:, ti * 512:(ti + 1) * 512])
                for mm_ in range(4):
                    m = ti * 4 + mm_
                    for hb in range(2):
                        bi = nb()
                        for k in range(16):
                            P.op("pe", L(lambda e, wb, k, mm_, hb, bi: e.matmul(pss[bi][:, :], W[wb][:, k, mm_ * 128:(mm_ + 1) * 128], XS[:, k, hb * 512:(hb + 1) * 512], start=(k == 0), stop=(k == 15)), wb, k, mm_, hb, bi),
                                 reads=[("W", wb), ("xs", k)], writes=[("ps", bi)])
                        per_chunk(m, hb, bi)

        def r_chunk(m, hb, bi):
            oi = nob()
            P.op("act", L(lambda e, oi, bi: e.copy(out=ob[oi][:], in_=pss[bi][:, :]), oi, bi), reads=[("ps", bi)], writes=[("ob", oi)])
            store("r_o", m, hb, oi)
        big_proj(0, 0, r_chunk)

        def k_chunk(m, hb, bi):
            hs = slice(hb * 512, (hb + 1) * 512)
            ok, oa, okr, osq, okk = nob(), nob(), nob(), nob(), nob()
            ba = nb()
            P.op("pe", L(lambda e, ba, m, hs: e.matmul(pss[ba][:, :], a2s[:, m * 128:(m + 1) * 128], T2[:, hs], start=True, stop=True), ba, m, hs), reads=["a2", "a1_h"], writes=[("ps", ba)])
            P.op("act", L(lambda e, oa, ba, m: e.activation(out=ob[oa][:], in_=pss[ba][:, :], func=AF.Sigmoid, bias=vsb[:, 1, m:m + 1]), oa, ba, m), reads=[("ps", ba), "vecs"], writes=[("ob", oa)])
            store("a_o", m, hb, oa)
            P.op("act", L(lambda e, ok, bi: e.copy(out=ob[ok][:], in_=pss[bi][:, :]), ok, bi), reads=[("ps", bi)], writes=[("ob", ok)])
            P.op("dve", L(lambda e, okr, ok, m: e.tensor_scalar(out=ob[okr][:], in0=ob[ok][:], scalar1=vsb[:, 2, m:m + 1], scalar2=None, op0=ALU.mult), okr, ok, m), reads=[("ob", ok), "vecs"], writes=[("ob", okr)])
            P.op("act", L(lambda e, osq, okr: e.activation(out=ob[osq][:], in_=ob[okr][:], func=AF.Square), osq, okr), reads=[("ob", okr)], writes=[("ob", osq)])
            bs_ = nb()
            P.op("pe", L(lambda e, bs_, osq: e.matmul(pss[bs_][:, :], bo[:], ob[osq][:], start=True, stop=True), bs_, osq), reads=["bo", ("ob", osq)], writes=[("ps", bs_)])
            P.op("act", L(lambda e, osq, bs_: e.activation(out=ob[osq][:], in_=pss[bs_][:, :], func=AF.Sqrt), osq, bs_), reads=[("ps", bs_)], writes=[("ob", osq)])
            P.op("dve", L(lambda e, osq: e.tensor_scalar(out=ob[osq][:], in0=ob[osq][:], scalar1=1e-12, scalar2=None, op0=ALU.max), osq), reads=[("ob", osq)], writes=[("ob", osq)])
            P.op("dve", L(lambda e, osq: e.reciprocal(out=ob[osq][:], in_=ob[osq][:]), osq), reads=[("ob", osq)], writes=[("ob", osq)])
            P.op("dve", L(lambda e, okk, okr, osq: e.tensor_tensor(out=ob[okk][:], in0=ob[okr][:], in1=ob[osq][:], op=ALU.mult), okk, okr, osq), reads=[("ob", okr), ("ob", osq)], writes=[("ob", okk)])
            store("kk_o", m, hb, okk)
            P.op("dve", L(lambda e, okr, oa, m: e.tensor_scalar(out=ob[okr][:], in0=ob[oa][:], scalar1=vsb[:, 3, m:m + 1], scalar2=vka[:, m:m + 1], op0=ALU.mult, op1=ALU.add), okr, oa, m),
                 reads=[("ob", oa), "vecs", "vka", ("ob", okr)], writes=[("ob", okr)])
            P.op("dve", L(lambda e, ok, okr: e.tensor_tensor(out=ob[ok][:], in0=ob[ok][:], in1=ob[okr][:], op=ALU.mult), ok, okr), reads=[("ob", ok), ("ob", okr)], writes=[("ob", ok)])
            store("k_o", m, hb, ok)
        big_proj(1, 1, k_chunk)

        def v_chunk(m, hb, bi):
            oi = nob()
            P.op("act", L(lambda e, oi, bi: e.copy(out=ob[oi][:], in_=pss[bi][:, :]), oi, bi), reads=[("ps", bi)], writes=[("ob", oi)])
            store("v_o", m, hb, oi)
        big_proj(2, 2, v_chunk)

        c_lw = -math.exp(-0.5)
        for m in range(16):
            for hb in range(2):
                hs = slice(hb * 512, (hb + 1) * 512)
                bi = nb(); oi = nob()
                P.op("pe", L(lambda e, bi, m, hs: e.matmul(pss[bi][:, :], w2s[:, m * 128:(m + 1) * 128], T1[:, hs], start=True, stop=True), bi, m, hs), reads=["w2", "w1_h"], writes=[("ps", bi)])
                P.op("act", L(lambda e, oi, bi, m: e.activation(out=ob[oi][:], in_=pss[bi][:, :], func=AF.Sigmoid, bias=vsb[:, 0, m:m + 1]), oi, bi, m), reads=[("ps", bi), "vecs"], writes=[("ob", oi)])
                P.op("dve", L(lambda e, oi: e.tensor_scalar(out=ob[oi][:], in0=ob[oi][:], scalar1=c_lw, scalar2=None, op0=ALU.mult), oi), reads=[("ob", oi)], writes=[("ob", oi)])
                store("lw_o", m, hb, oi)
                bi = nb(); oi = nob()
                for kc in range(2):
                    P.op("pe", L(lambda e, bi, m, hs, kc: e.matmul(pss[bi][:, :], g2s[:, kc, m * 128:(m + 1) * 128], T3[:, kc, hs], start=(kc == 0), stop=(kc == 1)), bi, m, hs, kc), reads=["g2", "g1_h"], writes=[("ps", bi)])
                P.op("act", L(lambda e, oi, bi: e.copy(out=ob[oi][:], in_=pss[bi][:, :]), oi, bi), reads=[("ps", bi)], writes=[("ob", oi)])
                store("g_o", m, hb, oi)
        finish(P)
    return nc


def build_rwc():
    nc = mk_nc()
    din = lambda name, shape: dram_in(nc, name, shape, F32)
    hT = din("hT", [D, T]); mod = din("mod", [128, 4, 16])
    o_i, r_i, k_i, v_i, g_i = [din(n, [D, T]) for n in ("o_i", "r_i", "k_i", "v_i", "g_i")]
    vecs = din("vecs", [128, 3, 16])
    bo_d = din("bo", [128, 128])
    w_out = din("w_out", [D, D])
    out = dram_out(nc, "out", [D, T], F32)
    with ExitStack() as es:
        sb = mk_sb(nc, es)
        Z = sb("Z", [128, 16, T], BF16)
        W = [sb(f"W{i}", [128, 16, 512], BF16) for i in range(2)]
        modsb = sb("modsb", [128, 4, 16]); vsb = sb("vsb", [128, 3, 16]); bo = sb("bo_sb", [128, 128]); bo64 = sb("bo64", [128, 128]); epsb = sb("epsb", [128, 1])
        IN = {n: [sb(f"in_{n}{i}", [128, 512]) for i in range(2)] for n in ("o", "r", "k", "v", "g")}
        tmpq = [[sb(f"tmp{q}_{i}", [128, 512]) for i in range(3)] for q in range(2)]
        hres = sb("hres", [128, 2, 512]); hout = sb("hout", [128, 2, 512])
        pss = mk_pss(nc, es, 8)
        P = mk_prog(nc)
        ld = lambda dst, src, key, q="sp": P.op(q, lambda e: e.dma_start(out=dst, in_=src), writes=[key], dma=True)
        ld(modsb[:], mod, "mod"); ld(vsb[:], vecs, "vecs"); ld(bo[:], bo_d, "bo")
        P.op("dve", lambda e: e.memset(epsb[:], GN_EPS_RW), writes=["eps"])
        P.op("dve", lambda e: e.tensor_scalar(out=bo64[:], in0=bo[:], scalar1=1.0 / 64, scalar2=None, op0=ALU.mult), reads=["bo"], writes=["bo64"])
        bankc = [0]

        def nb():
            b = bankc[0] % 8
            bankc[0] += 1
            return b
        def rwc_iter(m, hb, q, tmp):
            TK = lambda i: ("tmp", q, i)
            for n, src in (("o", o_i), ("r", r_i), ("k", k_i), ("v", v_i), ("g", g_i)):
                P.op("sp", L(lambda e, n, src, m, hb, q: e.dma_start(out=IN[n][q][:], in_=src[m * 128:(m + 1) * 128, hb * 512:(hb + 1) * 512]), n, src, m, hb, q), writes=[("in", n, q)], dma=True)
            b1 = nb()
            P.op("pe", L(lambda e, b1, q: e.matmul(pss[b1][:, :], bo64[:], IN["o"][q][:], start=True, stop=True), b1, q), reads=["bo64", ("in", "o", q)], writes=[("ps", b1)])
            P.op("dve", L(lambda e, b1, q: e.tensor_tensor(out=tmp[0][:], in0=IN["o"][q][:], in1=pss[b1][:, :], op=ALU.subtract), b1, q), reads=[("ps", b1), ("in", "o", q)], writes=[TK(0)])
            P.op("act", lambda e: e.activation(out=tmp[1][:], in_=tmp[0][:], func=AF.Square), reads=[TK(0)], writes=[TK(1)])
            b2 = nb()
            P.op("pe", L(lambda e, b2: e.matmul(pss[b2][:, :], bo64[:], tmp[1][:], start=True, stop=True), b2), reads=["bo64", TK(1)], writes=[("ps", b2)])
            P.op("act", L(lambda e, b2: e.activation(out=tmp[1][:], in_=pss[b2][:, :], func=AF.Sqrt, bias=epsb[:, 0:1]), b2), reads=[("ps", b2), "eps"], writes=[TK(1)])
            P.op("dve", lambda e: e.reciprocal(out=tmp[1][:], in_=tmp[1][:]), reads=[TK(1)], writes=[TK(1)])
            P.op("dve", lambda e: e.tensor_tensor(out=tmp[0][:], in0=tmp[0][:], in1=tmp[1][:], op=ALU.mult), reads=[TK(0), TK(1)], writes=[TK(0)])
            P.op("act", L(lambda e, m: e.activation(out=tmp[0][:], in_=tmp[0][:], func=AF.Identity, scale=vsb[:, 0, m:m + 1], bias=vsb[:, 1, m:m + 1]), m), reads=[TK(0), "vecs"], writes=[TK(0)])
            P.op("dve", L(lambda e, m, q: e.scalar_tensor_tensor(out=tmp[2][:], in0=IN["r"][q][:], scalar=vsb[:, 2, m:m + 1], in1=IN["k"][q][:], op0=ALU.mult, op1=ALU.mult), m, q),
                 reads=[("in", "r", q), ("in", "k", q), "vecs"], writes=[TK(2)])
            b3 = nb()
            P.op("pe", L(lambda e, b3: e.matmul(pss[b3][:, :], bo[:], tmp[2][:], start=True, stop=True), b3), reads=["bo", TK(2)], writes=[("ps", b3)])
            P.op("dve", L(lambda e, b3, q: e.tensor_tensor(out=tmp[2][:], in0=pss[b3][:, :], in1=IN["v"][q][:], op=ALU.mult), b3, q), reads=[("ps", b3), ("in", "v", q), TK(2)], writes=[TK(2)])
            P.op("dve", lambda e: e.tensor_tensor(out=tmp[0][:], in0=tmp[0][:], in1=tmp[2][:], op=ALU.add), reads=[TK(0), TK(2)], writes=[TK(0)])
            P.op("dve", L(lambda e, m, hb, q: e.tensor_tensor(out=Z[:, m, hb * 512:(hb + 1) * 512], in0=tmp[0][:], in1=IN["g"][q][:], op=ALU.mult), m, hb, q), reads=[TK(0), ("in", "g", q)], writes=[("z", m)])

        it = 0
        for m in range(16):
            for hb in range(2):
                rwc_iter(m, hb, it % 2, tmpq[it % 2])
                it += 1
        wcount = [0]
        for ti in range(4):
            wb = wcount[0] % 2
            wcount[0] += 1
            P.op("pool", L(lambda e, wb, ti: e.dma_start(out=W[wb][:], in_=w_out[:, ti * 512:(ti + 1) * 512].rearrange("(k p) n -> p k n", p=128)), wb, ti), writes=[("W", wb)], dma=True)
            for mm_ in range(4):
                m = ti * 4 + mm_
                for hb in range(2):
                    q = (m * 2 + hb) % 2
                    bi = nb()
                    P.op("sp", L(lambda e, m, hb, q: e.dma_start(out=hres[:, q, :], in_=hT[m * 128:(m + 1) * 128, hb * 512:(hb + 1) * 512]), m, hb, q), writes=[("hres", q)], dma=True)
                    for k in range(16):
                        P.op("pe", L(lambda e, wb, k, mm_, hb, bi: e.matmul(pss[bi][:, :], W[wb][:, k, mm_ * 128:(mm_ + 1) * 128], Z[:, k, hb * 512:(hb + 1) * 512], start=(k == 0), stop=(k == 15)), wb, k, mm_, hb, bi),
                             reads=[("W", wb), ("z", k)], writes=[("ps", bi)])
                    P.op("dve", L(lambda e, m, q, bi: e.scalar_tensor_tensor(out=hout[:, q, :], in0=pss[bi][:, :], scalar=modsb[:, 2, m:m + 1], in1=hres[:, q, :], op0=ALU.mult, op1=ALU.add), m, q, bi),
                         reads=[("ps", bi), ("hres", q), "mod"], writes=[("hout", q)])
                    P.op("sp", L(lambda e, m, hb, q: e.dma_start(out=out[m * 128:(m + 1) * 128, hb * 512:(hb + 1) * 512], in_=hout[:, q, :]), m, hb, q), reads=[("hout", q)], dma=True, is_output=True)
        finish(P)
    return nc

T = 1024
NCOLA = 1536


def build_ada():
    nc = mk_nc()
    din = lambda name, shape: dram_in(nc, name, shape, F32)
    cT = din("cT", [128, 16, 4])
    aw = din("aw", [4, D, NCOLA])
    ab = din("ab", [128, 48])
    out = dram_out(nc, "out", [128, 48, 4], F32)
    with ExitStack() as es:
        sb = mk_sb(nc, es)
        cs = sb("cs", [128, 16, 4]); cond = sb("cond", [128, 16, 4]); absb = sb("absb", [128, 48]); osb = sb("osb", [128, 48, 4])
        W = [sb(f"W{i}", [128, 16, 512], BF16) for i in range(3)]
        condb = sb("condb", [128, 16, 4], BF16)
        pss = mk_pss(nc, es, 4)
        P = mk_prog(nc)
        P.op("sp", lambda e: e.dma_start(out=cs[:], in_=cT), writes=["cs"], dma=True)
        P.op("sp", lambda e: e.dma_start(out=absb[:], in_=ab), writes=["ab"], dma=True)
        P.op("act", lambda e: e.activation(out=cond[:], in_=cs[:], func=AF.Silu), reads=["cs"], writes=["cond0"])
        P.op("dve", lambda e: e.tensor_copy(out=condb[:], in_=cond[:]), reads=["cond0"], writes=["cond"])
        it = 0
        for l in range(4):
            for ti in range(3):
                wb = it % 3
                it += 1
                P.op("pool", L(lambda e, wb, l, ti: e.dma_start(out=W[wb][:], in_=aw[l][:, ti * 512:(ti + 1) * 512].rearrange("(k p) n -> p k n", p=128)), wb, l, ti), writes=[("W", wb)], dma=True)
                for fcl in range(4):
                    fc = l * 12 + ti * 4 + fcl
                    bi = fc % 4
                    for k in range(16):
                        P.op("pe", L(lambda e, wb, k, fcl, bi: e.matmul(pss[bi][:, 0:4], W[wb][:, k, fcl * 128:(fcl + 1) * 128], condb[:, k, :], start=(k == 0), stop=(k == 15)), wb, k, fcl, bi),
                             reads=[("W", wb), "cond"], writes=[("ps", bi)])
                    P.op("act", L(lambda e, fc, bi: e.activation(out=osb[:, fc, :], in_=pss[bi][:, 0:4], func=AF.Identity, bias=absb[:, fc:fc + 1]), fc, bi), reads=[("ps", bi), "ab"], writes=["osb"])
        P.op("sp", lambda e: e.dma_start(out=out, in_=osb[:]), reads=["osb"], dma=True, is_output=True)
        finish(P)
    return nc


def build_final():
    nc = mk_nc()
    din = lambda name, shape: dram_in(nc, name, shape, F32)
    hT = din("hT", [D, T])
    mod = din("mod", [128, 4, 16])
    out = dram_out(nc, "out", [D, T], F32)
    with ExitStack() as es:
        sb = mk_sb(nc, es)
        XO = sb("XO", [128, 16, T])
        stage = sb("stage", [128, 16, 256]); modsb = sb("modsb", [128, 4, 16]); ones = sb("ones", [128, 128], BF16); epsb = sb("epsb", [128, 1])
        rstd = sb("rstd", [128, 512]); sq = sb("sq", [128, 2, 512], BF16); xn = sb("xn", [128, 2, 512])
        pss = mk_pss(nc, es, 2)
        P = mk_prog(nc)
        P.op("sp", lambda e: e.dma_start(out=modsb[:], in_=mod), writes=["mod"], dma=True)
        P.op("dve", lambda e: e.memset(ones[:], 1.0), writes=["ones"])
        P.op("dve", lambda e: e.memset(epsb[:], EPS), writes=["eps"])
        tk = {"rstd": rstd, "sq": sq, "xn": xn, "eps": epsb}
        emit_norm_mod(P, nc, es, hT, XO, modsb, 0, 1, T, 0, stage, ones, pss, tk)
        for c in range(16):
            P.op("sp", L(lambda e, c: e.dma_start(out=out[c * 128:(c + 1) * 128, :], in_=XO[:, c, :]), c), reads=[("xm", c)], dma=True, is_output=True)
        finish(P)
    return nc

NWORDS = 53000
PAIRS = [[0, 1], [2, 3], [4, 5], [6, 7]]
QUADS = [[0, 1, 2, 3], [4, 5, 6, 7]]
P4 = [[0, 4], [1, 5], [2, 6], [3, 7]]


def fused_input_specs():
    sp = {}
    f = lambda n, s, d=F32: sp.__setitem__(n, (s, d))
    f("x_hT", [D, T]); f("flag", [128, 1]); f("selb", [128, 4]); f("sel8", [128, 8])
    f("cT", [128, 16, 4]); f("aw", [4, D, NCOLA]); f("ab", [128, 48])
    for l in range(4):
        f(f"ffn{l}_w_up", [D, 2 * DFF]); f(f"ffn{l}_cw", [128, 86, 4]); f(f"ffn{l}_w_dn", [DFF, D])
    for j in range(2):
        f(f"sg{j}_w_in", [D, 2 * D]); f(f"sg{j}_lnp", [128, 2, D]); f(f"sg{j}_wsT", [128, 16, 128]); f(f"sg{j}_bs", [1, D]); f(f"sg{j}_w_out", [D, D])
    f("tri", [128, 128])
    f("posb", [128, T], I32); f("invf", [128, 1]); f("ret_w_in", [D, 6 * D]); f("kdecA", [128, NH, 8]); f("kdec", [128, NH]); f("ident", [128, 128])
    f("decT", [128, NH, 128]); f("qdec", [128, NH, 128]); f("gnp", [128, 32, 2]); f("ret_w_out", [2 * D, D])
    f("mu", [128, 6, 16]); f("w_rkv", [3, D, D]); f("w1", [D, LD]); f("w2", [LD, D]); f("a1", [D, LD]); f("a2", [LD, D]); f("g1", [D, LG]); f("g2", [LG, D])
    f("vecsA", [128, 4, 16]); f("bo", [128, 128]); f("vecsC", [128, 3, 16]); f("rw_w_out", [D, D])
    f("tri_incl", [128, 128]); f("low_strict", [128, 128]); f("mask2", [128, 256]); f("ones", [128, 128])
    f("fm", [128, 4, 16])
    return sp


def build_fused(debug=False):
    nc = bass.Bass("TRN2", target_bir_lowering=False)
    ext = {n: nc.dram_tensor(n, s, d, kind="ExternalInput").ap() for n, (s, d) in fused_input_specs().items()}
    out = nc.dram_tensor("out", [D, T], F32, kind="ExternalOutput").ap()
    itn = lambda n, s, d=F32: nc.dram_tensor(n, s, d).ap()
    stages = ["0a", "0b", "1a", "1b", "2a", "2b", "3a", "3b"]
    H = {}
    for st in stages:
        H[st] = nc.dram_tensor(f"h{st}", [D, T + 2], F32, kind="ExternalOutput").ap() if debug else itn(f"h{st}", [D, T + 2])
    ada_o = itn("ada_o", [128, 48, 4])
    cc_a = itn("cc_a", [128, 8 * 192]); cc_b = itn("cc_b", [128, 8 * 192]); cc_c = itn("cc_c", [128, 8 * 192])
    s_loc = itn("s_loc", [NH, 256, 512]); s_in = itn("s_in", [NH, 256, 512])
    rw = {n: itn("rw_" + n, [D, T]) for n in ("r_o", "k_o", "v_o", "kk_o", "a_o", "lw_o", "g_o")}
    rwt = {n: itn("rwt_" + n, [T, D]) for n in ("k_t", "v_t", "kk_t", "a_t", "lw_t")}
    oT = itn("oT", [D, T])
    sc_zero = itn("sc_zero", [2, NHC, HD, HD]); sc_loc = itn("sc_loc", [2, NHC, HD, HD]); sc_in = itn("sc_in", [2, NHC, HD, HD]); sc_dump = itn("sc_dump", [2, NHC, HD, HD])
    with ExitStack() as es:
        arena = es.enter_context(nc.sbuf_tensor("arena", [128, NWORDS], F32))
        pss_all = [es.enter_context(nc.psum_tensor(f"ps{i}", [128, 512], F32)) for i in range(8)]
        P = Prog(nc)
        cx = Ctx(nc, P, arena, NWORDS, pss_all)
        CX[0] = cx
        try:
            flagsb = cx.sb("flag_p", [128, 1]); omf = cx.sb("omf_p", [128, 1]); selb = cx.sb("selb_p", [128, 4]); sel8 = cx.sb("sel8_p", [128, 8])
            MODALL = cx.sb("modall", [128, 4, 112])
            cx.base = cx.off
            ldp = lambda dst, src: P.op("sp", lambda e: e.dma_start(out=dst, in_=src), writes=["pers"], dma=True)
            ldp(flagsb[:], ext["flag"]); ldp(selb[:], ext["selb"]); ldp(sel8[:], ext["sel8"])
            P.op("dve", lambda e: e.memset(MODALL[:], 0.0), writes=["modall"])
            P.op("dve", lambda e: e.tensor_scalar(out=omf[:], in0=flagsb[:], scalar1=-1.0, scalar2=1.0, op0=ALU.mult, op1=ALU.add), reads=["pers"], writes=["omf"])
            P.barrier()

            cx.io = {"cT": ext["cT"], "aw": ext["aw"], "ab": ext["ab"], "out": ada_o}
            build_ada()
            cx.reset()
            osb = cx.sb("osb2", [128, 192]); CB = cx.sb("CB", [128, 8, 192]); G = cx.sb("G", [128, 8, 48, 4])
            P.op("sp", lambda e: e.dma_start(out=osb[:], in_=ada_o.rearrange("p a b -> p (a b)")), writes=["osb2"], dma=True)
            for r in range(8):
                P.op("dve", L(lambda e, r: e.tensor_scalar(out=CB[:, r, :], in0=osb[:], scalar1=sel8[:, r:r + 1], scalar2=None, op0=ALU.mult), r), reads=["osb2"], writes=["CB"])
            P.op("sp", lambda e: e.dma_start(out=cc_a, in_=CB[:].rearrange("p a b -> p (a b)")), reads=["CB"], writes=["cc_a"], dma=True)
            P.cc(lambda e: e.collective_compute("AllReduce", ALU.add, replica_groups=QUADS, ins=[cc_a.opt()], outs=[cc_b.opt()]), reads=["cc_a"], writes=["cc_b"])
            P.cc(lambda e: e.collective_compute("AllReduce", ALU.add, replica_groups=P4, ins=[cc_b.opt()], outs=[cc_c.opt()]), reads=["cc_b"], writes=["cc_c"])
            P.op("sp", lambda e: e.dma_start(out=G[:].rearrange("p j q b -> p (j q b)"), in_=cc_c), reads=["cc_c"], writes=["G"], dma=True)
            for l in range(4):
                mv = MODALL[:, l, 0:96].rearrange("p (j f) -> p j f", j=8)
                for b in range(4):
                    if b == 0:
                        P.op("dve", L(lambda e, l, b, mv: e.tensor_scalar(out=mv, in0=G[:, :, l * 12:(l + 1) * 12, b], scalar1=selb[:, b:b + 1], scalar2=None, op0=ALU.mult), l, b, mv), reads=["G"], writes=[("modall", l)])
                    else:
                        P.op("dve", L(lambda e, l, b, mv: e.scalar_tensor_tensor(out=mv, in0=G[:, :, l * 12:(l + 1) * 12, b], scalar=selb[:, b:b + 1], in1=mv, op0=ALU.mult, op1=ALU.add), l, b, mv), reads=["G", ("modall", l)], writes=[("modall", l)])
            P.barrier()

            def modv(l, i0):
                return MODALL[:, l, i0 * 16:(i0 + 4) * 16].rearrange("p (i c) -> p i c", i=4)

            def exchange(src2d, dst2d, n, apply_flag=True):
                cx.reset()
                cx.uid += 1
                xs_src = itn(f"xs_src{cx.uid}", [128, n]); xs_dst = itn(f"xs_dst{cx.uid}", [128, n])
                step = 2048
                t = [cx.sb(f"xt{i}", [128, step]) for i in range(2)]
                for i, c0 in enumerate(range(0, n, step)):
                    w = min(step, n - c0)
                    q = i % 2
                    P.op("sp", L(lambda e, q, c0, w: e.dma_start(out=t[q][:, 0:w], in_=src2d[:, c0:c0 + w]), q, c0, w), writes=[("xt", q)], dma=True)
                    P.op("dve", L(lambda e, q, w: e.tensor_scalar(out=t[q][:, 0:w], in0=t[q][:, 0:w], scalar1=omf[:, 0:1], scalar2=None, op0=ALU.mult), q, w), reads=[("xt", q)], writes=[("xt", q)])
                    P.op("sp", L(lambda e, q, c0, w: e.dma_start(out=xs_src[:, c0:c0 + w], in_=t[q][:, 0:w]), q, c0, w), reads=[("xt", q)], writes=["xs_src"], dma=True)
                P.cc(lambda e: e.collective_compute("AllReduce", ALU.add, replica_groups=PAIRS, ins=[xs_src.opt()], outs=[xs_dst.opt()]), reads=["xs_src"], writes=["xs_dst"])
                for i, c0 in enumerate(range(0, n, step)):
                    w = min(step, n - c0)
                    q = i % 2
                    P.op("sp", L(lambda e, q, c0, w: e.dma_start(out=t[q][:, 0:w], in_=xs_dst[:, c0:c0 + w]), q, c0, w), reads=["xs_dst"], writes=[("xt", q)], dma=True)
                    if apply_flag:
                        P.op("dve", L(lambda e, q, w: e.tensor_scalar(out=t[q][:, 0:w], in0=t[q][:, 0:w], scalar1=flagsb[:, 0:1], scalar2=None, op0=ALU.mult), q, w), reads=[("xt", q)], writes=[("xt", q)])
                    P.op("sp", L(lambda e, q, c0, w: e.dma_start(out=dst2d[:, c0:c0 + w], in_=t[q][:, 0:w]), q, c0, w), reads=[("xt", q)], writes=["xdst"], dma=True)
                P.barrier()

            def halo(hbuf):
                src = hbuf[:, T:T + 2].rearrange("(c p) n -> p c n", p=128)
                dst = hbuf[:, 0:2].rearrange("(c p) n -> p c n", p=128)
                cx.reset()
                cx.uid += 1
                xs_src = itn(f"hx_src{cx.uid}", [128, 32]); xs_dst = itn(f"hx_dst{cx.uid}", [128, 32])
                t = cx.sb("ht", [128, 16, 2])
                P.op("sp", lambda e: e.dma_start(out=t[:], in_=src), writes=["ht"], dma=True)
                P.op("dve", lambda e: e.tensor_scalar(out=t[:], in0=t[:], scalar1=omf[:, 0:1], scalar2=None, op0=ALU.mult), reads=["ht"], writes=["ht"])
                P.op("sp", lambda e: e.dma_start(out=xs_src, in_=t[:].rearrange("p c n -> p (c n)")), reads=["ht"], writes=["xs_src"], dma=True)
                P.cc(lambda e: e.collective_compute("AllReduce", ALU.add, replica_groups=PAIRS, ins=[xs_src.opt()], outs=[xs_dst.opt()]), reads=["xs_src"], writes=["xs_dst"])
                P.op("sp", lambda e: e.dma_start(out=t[:].rearrange("p c n -> p (c n)"), in_=xs_dst), reads=["xs_dst"], writes=["ht"], dma=True)
                P.op("sp", lambda e: e.dma_start(out=dst, in_=t[:]), reads=["ht"], writes=["hdst"], dma=True)
                P.barrier()

            def run_ffn(l, hin, hout):
                halo(hin)
                cx.io = {"hT": hin, "mod": modv(l, 3), "flag": flagsb, "w_up": ext[f"ffn{l}_w_up"], "cw": ext[f"ffn{l}_cw"], "w_dn": ext[f"ffn{l}_w_dn"], "out": hout[:, 2:]}
                build_ffn()

            def run_sg(l, j, hin, hout):
                cx.io = {"hT": hin, "mod": modv(l, 0), "w_in": ext[f"sg{j}_w_in"], "lnp": ext[f"sg{j}_lnp"], "wsT": ext[f"sg{j}_wsT"], "tri": ext["tri"],
                         "bs": ext[f"sg{j}_bs"], "w_out": ext[f"sg{j}_w_out"], "out": hout[:, 2:]}
                build_sg()

            run_sg(0, 0, ext["x_hT"], H["0a"])
            run_ffn(0, H["0a"], H["0b"])
            rio = {"hT": H["0b"][:, 2:], "mod": modv(1, 0), "posb": ext["posb"], "invf": ext["invf"], "w_in": ext["ret_w_in"], "kdecA": ext["kdecA"], "kdec": ext["kdec"],
                   "ident": ext["ident"], "decT": ext["decT"], "qdec": ext["qdec"], "gnp": ext["gnp"], "s_in": s_in, "w_out": ext["ret_w_out"], "out": H["1a"][:, 2:], "s_out": s_loc}
            cx.io = dict(rio)
            build_ret("state")
            v2 = lambda a: a.rearrange("h d e -> (h d e)").rearrange("(p n) -> p n", p=128)
            exchange(v2(s_loc), v2(s_in), 8192)
            cx.io = dict(rio)
            build_ret("main")
            run_ffn(1, H["1a"], H["1b"])
            halo(H["1b"])
            aio = {"hT": H["1b"][:, 1:], "mod": modv(2, 0), "flag": flagsb, "mu": ext["mu"], "w_rkv": ext["w_rkv"], "w1": ext["w1"], "w2": ext["w2"], "a1": ext["a1"], "a2": ext["a2"],
                   "g1": ext["g1"], "g2": ext["g2"], "vecs": ext["vecsA"], "bo": ext["bo"], "ident": ext["ident"]}
            aio.update(rw); aio.update(rwt)
            cx.io = aio
            build_rwa(want_tm=True)
            cx.reset()
            zt = cx.sb("zt", [128, 1024])
            P.op("dve", lambda e: e.memset(zt[:], 0.0), writes=["zt"])
            P.op("sp", lambda e: e.dma_start(out=sc_zero.rearrange("a h k v -> (a h k v)").rearrange("(p n) -> p n", p=128), in_=zt[:]), reads=["zt"], writes=["sc_zero"], dma=True)
            P.barrier()
            for ps_ in (1, 2):
                for hg in range(2):
                    cs_ = slice(hg * CH, (hg + 1) * CH)
                    sio = {"lw_t": rwt["lw_t"][:, cs_], "kk_t": rwt["kk_t"][:, cs_], "a_t": rwt["a_t"][:, cs_], "k_t": rwt["k_t"][:, cs_], "v_t": rwt["v_t"][:, cs_],
                           "lw_f": rw["lw_o"][cs_, :], "kk_f": rw["kk_o"][cs_, :], "a_f": rw["a_o"][cs_, :], "k_f": rw["k_o"][cs_, :], "r_f": rw["r_o"][cs_, :],
                           "ident": ext["ident"], "tri_incl": ext["tri_incl"], "low_strict": ext["low_strict"], "mask2": ext["mask2"], "ones": ext["ones"],
                           "s0": (sc_zero if ps_ == 1 else sc_in)[hg], "s_out": (sc_loc if ps_ == 1 else sc_dump)[hg], "o_out": oT[cs_, :]}
                    cx.io = sio
                    build_scan(T, o_fm=True, want_o=(ps_ == 2))
                if ps_ == 1:
                    v3 = lambda a: a.rearrange("a h k v -> (a h k v)").rearrange("(p n) -> p n", p=128)
                    exchange(v3(sc_loc), v3(sc_in), 1024)
            cx.io = {"hT": H["1b"][:, 2:], "mod": modv(2, 0), "o_i": oT, "r_i": rw["r_o"], "k_i": rw["k_o"], "v_i": rw["v_o"], "g_i": rw["g_o"], "vecs": ext["vecsC"], "bo": ext["bo"],
                     "w_out": ext["rw_w_out"], "out": H["2a"][:, 2:]}
            build_rwc()
            run_ffn(2, H["2a"], H["2b"])
            run_sg(3, 1, H["2b"][:, 2:], H["3a"])
            run_ffn(3, H["3a"], H["3b"])
            cx.io = {"hT": H["3b"][:, 2:], "mod": ext["fm"], "out": out}
            build_final()
        finally:
            CX[0] = None
        P.emit()
    return nc
_FCACHE = {}
_DEBUG = None


def _pp(v):
    return np.ascontiguousarray(np.asarray(v).reshape(16, 128).T)


def fused_maps(inp):
    x = inp["x"]
    rc, cd = ret_consts()
    sc = scan_consts()
    bo = blockones()
    shared = {}
    for l in range(4):
        cw = np.zeros((128, 86, 4), np.float32)
        cw[:, :, 0:3] = inp["ffn_conv_w"][l].T.reshape(86, 128, 3).transpose(1, 0, 2)
        cw[:, :, 3] = inp["ffn_conv_b"][l].reshape(86, 128).T
        shared[f"ffn{l}_w_up"] = inp["ffn_w_up"][l]; shared[f"ffn{l}_cw"] = cw; shared[f"ffn{l}_w_dn"] = inp["ffn_w_down"][l]
    for j in range(2):
        shared[f"sg{j}_w_in"] = inp["sg_w_in"][j]
        shared[f"sg{j}_lnp"] = np.ascontiguousarray(np.broadcast_to(np.stack([inp["sg_ln_g"][j], inp["sg_ln_b"][j]])[None], (128, 2, D))).astype(np.float32)
        shared[f"sg{j}_wsT"] = np.ascontiguousarray(inp["sg_w_s"][j].transpose(2, 0, 1))
        shared[f"sg{j}_bs"] = np.ascontiguousarray(inp["sg_b_s"][j].reshape(1, D))
        shared[f"sg{j}_w_out"] = inp["sg_w_out"][j]
    shared["tri"] = np.triu(np.ones((128, 128), np.float32))
    shared.update({"invf": rc["invf"], "ret_w_in": inp["ret_w_in"][0], "kdecA": rc["kdecA"], "kdec": rc["kdec"], "ident": rc["ident"], "decT": rc["decT"], "qdec": rc["qdec"],
                   "gnp": np.ascontiguousarray(np.stack([inp["ret_gn_g"][0].reshape(32, 128).T, inp["ret_gn_b"][0].reshape(32, 128).T], -1)).astype(np.float32),
                   "ret_w_out": inp["ret_w_out"][0]})
    shared.update({"mu": np.ascontiguousarray(inp["rwkv_mu"][0].reshape(6, 16, 128).transpose(2, 0, 1)), "w_rkv": inp["rwkv_w_rkv"][0], "w1": inp["rwkv_w1"][0], "w2": inp["rwkv_w2"][0],
                   "a1": inp["rwkv_a1"][0], "a2": inp["rwkv_a2"][0], "g1": inp["rwkv_g1"][0], "g2": inp["rwkv_g2"][0],
                   "vecsA": np.ascontiguousarray(np.stack([_pp(inp["rwkv_w0"][0]), _pp(inp["rwkv_a0"][0]), _pp(inp["rwkv_k_k"][0]), _pp(inp["rwkv_k_a"][0])], 1)),
                   "bo": bo,
                   "vecsC": np.ascontiguousarray(np.stack([_pp(inp["rwkv_ln_g"][0]), _pp(inp["rwkv_ln_b"][0]), _pp(inp["rwkv_r_k"][0].reshape(D))], 1)),
                   "rw_w_out": inp["rwkv_w_out"][0]})
    shared.update(sc)
    fm = np.zeros((128, 4, 16), np.float32)
    fm[:, 1, :] = _pp(inp["final_norm_g"])
    shared["fm"] = fm
    shared["cT"] = np.ascontiguousarray(inp["c"].T.reshape(16, 128, 4).transpose(1, 0, 2)).astype(np.float32)
    maps = []
    for i in range(8):
        b, p = i // 2, i % 2
        m = dict(shared)
        m["x_hT"] = np.ascontiguousarray(x[b].T[:, p * T:(p + 1) * T])
        m["flag"] = np.full((128, 1), float(p), np.float32)
        sb_ = np.zeros((128, 4), np.float32); sb_[:, b] = 1.0
        s8 = np.zeros((128, 8), np.float32); s8[:, i] = 1.0
        m["selb"] = sb_; m["sel8"] = s8
        m["aw"] = np.ascontiguousarray(inp["ada_w"][:, :, i * NCOLA:(i + 1) * NCOLA])
        m["ab"] = np.ascontiguousarray(inp["ada_b"][:, i * NCOLA:(i + 1) * NCOLA].reshape(4, 12, 128).transpose(2, 0, 1).reshape(128, 48))
        m["posb"] = np.ascontiguousarray(np.broadcast_to(inp["positions"][b, p * T:(p + 1) * T][None], (128, T))).astype(np.int32)
        maps.append(m)
    return maps


def kernel(**inp):
    inp = {k: np.asarray(v) for k, v in inp.items()}
    dbg = _DEBUG is not None
    if dbg not in _FCACHE:
        _FCACHE[dbg] = build_fused(debug=dbg)
    nc = _FCACHE[dbg]
    maps = fused_maps(inp)
    res = run_bass_kernel_spmd(nc, maps, core_ids=list(range(8))).results
    if dbg:
        for st in ["0a", "0b", "1a", "1b", "2a", "2b", "3a", "3b"]:
            hT = np.zeros((4, D, 2 * T), np.float32)
            for i in range(8):
                hT[i // 2][:, (i % 2) * T:(i % 2 + 1) * T] = res[i]["h" + st][:, 2:]
            _DEBUG((int(st[0]), st[1]), hT)
    oT = np.zeros((4, D, 2 * T), np.float32)
    for i in range(8):
        oT[i // 2][:, (i % 2) * T:(i % 2 + 1) * T] = res[i]["out"]
    return np.ascontiguousarray(oT.transpose(0, 2, 1)).astype(np.float32)
```
